# Optimizing a Trainium2 kernel written in Bass

```python
import math
import jax
import jax.numpy as jnp
from jax import lax
import numpy as np

D_MODEL = 1024
BATCH = 8
SEQ = 4096
DEPTH = 2

MEM_LEN = 256
EPS = 1e-6
NEG_INF = -1e30
FORCE_SCORE = 1e4

CONV_CH = 512
CONV_WIDTH = 31

NSA_HEADS = 8
NSA_KV_GROUPS = 2
NSA_HEAD_DIM = 64
CMP_BLOCK = 32
CMP_STRIDE = 16
CMP_HIDDEN = 256
SLC_BLOCK = 64
N_SELECT = 16
WINDOW = 512
NSA_QBLOCK = 64

MLA_HEADS = 4
Q_RANK = 384
KV_RANK = 256
QK_NOPE = 128
QK_ROPE = 64
V_DIM = 128
ROPE_THETA = 10000.0
ATTN_QBLOCK = 128

REL_BUCKETS = 32
REL_MAX_DIST = 128

XATTN_HEADS = 4
XATTN_HEAD_DIM = 128

FFN_HIDDEN = -(-8 * D_MODEL // (3 * 256)) * 256

IN_SIZES = (2 * CONV_CH, NSA_HEADS * NSA_HEAD_DIM, 6 * NSA_KV_GROUPS * NSA_HEAD_DIM, 3 * NSA_HEADS, Q_RANK, KV_RANK, QK_ROPE, 3 * D_MODEL)
IN_COLS = 2 * CONV_CH + NSA_HEADS * NSA_HEAD_DIM + 6 * NSA_KV_GROUPS * NSA_HEAD_DIM + 3 * NSA_HEADS + Q_RANK + KV_RANK + QK_ROPE + 3 * D_MODEL

kernel_name = 'hybrid_conformer_nsa_mla_block'


def rmsnorm(x, g):
    xf = x.astype(jnp.float32)
    y = xf * lax.rsqrt(jnp.mean(xf * xf, axis=-1, keepdims=True) + EPS)
    return (y * g.astype(jnp.float32)).astype(x.dtype)


def layernorm(x, g, b):
    xf = x.astype(jnp.float32)
    mu = jnp.mean(xf, axis=-1, keepdims=True)
    var = jnp.mean(jnp.square(xf - mu), axis=-1, keepdims=True)
    y = (xf - mu) * lax.rsqrt(var + EPS) * g.astype(jnp.float32) + b.astype(jnp.float32)
    return y.astype(x.dtype)


def t5_bucket(dist):
    exact = REL_BUCKETS // 2
    d = jnp.maximum(dist, 0)
    log_ratio = jnp.log(jnp.maximum(d, 1).astype(jnp.float32) / exact) / math.log(REL_MAX_DIST / exact)
    large = jnp.minimum(exact + (log_ratio * (REL_BUCKETS - exact)).astype(jnp.int32), REL_BUCKETS - 1)
    return jnp.where(d < exact, d, large)


def apply_rope(x, cos, sin):
    xf = x.astype(jnp.float32)
    x1, x2 = jnp.split(xf, 2, axis=-1)
    return jnp.concatenate([x1 * cos - x2 * sin, x2 * cos + x1 * sin], axis=-1).astype(x.dtype)


def conformer_conv(u_glu, conv_w, conv_b, ln_g, ln_b, w_proj):
    a, b = jnp.split(u_glu, 2, axis=-1)
    u = a * jax.nn.sigmoid(b)
    u = lax.conv_general_dilated(u, conv_w[:, None, :], window_strides=(1,), padding=[(CONV_WIDTH - 1, 0)],
                                 dimension_numbers=('NWC', 'WIO', 'NWC'), feature_group_count=CONV_CH) + conv_b
    u = jax.nn.silu(layernorm(u, ln_g, ln_b))
    return u @ w_proj


def compress_blocks(tok, pos, w1, b1, w2, n_cmp):
    bsz = tok.shape[0]
    idx = CMP_STRIDE * jnp.arange(n_cmp)[:, None] + jnp.arange(CMP_BLOCK)[None, :]
    blk = tok[:, idx] + pos[:, None, :]
    blk = blk.transpose(0, 1, 3, 2, 4).reshape(bsz, n_cmp, NSA_KV_GROUPS, CMP_BLOCK * NSA_HEAD_DIM)
    return jax.nn.gelu(blk @ w1 + b1) @ w2


def nsa_attention(q, kv, gate_logits, rel_bias, cmp_pos_k, cmp_w1_k, cmp_b1_k, cmp_w2_k,
                  cmp_pos_v, cmp_w1_v, cmp_b1_v, cmp_w2_v):
    bsz, seq = q.shape[0], q.shape[1]
    G, HG, dh = NSA_KV_GROUPS, NSA_HEADS // NSA_KV_GROUPS, NSA_HEAD_DIM
    n_cmp = (seq - CMP_BLOCK) // CMP_STRIDE + 1
    n_sb = seq // SLC_BLOCK
    n_sel = min(N_SELECT, n_sb)
    n_keys = n_sel * SLC_BLOCK
    kw_len = NSA_QBLOCK + WINDOW
    scale = dh ** -0.5
    q = q.reshape(bsz, seq, G, HG, dh)
    kv = kv.reshape(bsz, seq, 6, G, dh)
    k_cmp = compress_blocks(kv[:, :, 0], cmp_pos_k, cmp_w1_k, cmp_b1_k, cmp_w2_k, n_cmp)
    v_cmp = compress_blocks(kv[:, :, 1], cmp_pos_v, cmp_w1_v, cmp_b1_v, cmp_w2_v, n_cmp)
    k_slc = kv[:, :, 2].reshape(bsz, n_sb, SLC_BLOCK, G, dh).transpose(0, 3, 1, 2, 4)
    v_slc = kv[:, :, 3].reshape(bsz, n_sb, SLC_BLOCK, G, dh).transpose(0, 3, 1, 2, 4)
    k_win = jnp.pad(kv[:, :, 4], ((0, 0), (WINDOW, 0), (0, 0), (0, 0)))
    v_win = jnp.pad(kv[:, :, 5], ((0, 0), (WINDOW, 0), (0, 0), (0, 0)))
    gates = jax.nn.sigmoid(gate_logits.astype(jnp.float32)).astype(q.dtype).reshape(bsz, seq, G, HG, 3)
    rb_group = rel_bias.reshape(REL_BUCKETS, G, HG).transpose(1, 0, 2)
    c_start = CMP_STRIDE * jnp.arange(n_cmp)
    c_end = c_start + CMP_BLOCK - 1
    s_start = SLC_BLOCK * jnp.arange(n_sb)
    cover = ((c_start[:, None] < s_start[None, :] + SLC_BLOCK) & (c_start[:, None] + CMP_BLOCK > s_start[None, :])).astype(jnp.float32)
    b_idx = jnp.arange(bsz)[:, None, None, None]
    g_idx = jnp.arange(G)[None, :, None, None]
    j_idx = jnp.arange(n_sb)

    def block(i):
        q0 = i * NSA_QBLOCK
        tq = q0 + jnp.arange(NSA_QBLOCK)
        qb = lax.dynamic_slice_in_dim(q, q0, NSA_QBLOCK, axis=1)
        gb = lax.dynamic_slice_in_dim(gates, q0, NSA_QBLOCK, axis=1)
        dist_c = tq[:, None] - c_end[None, :]
        valid_c = dist_c >= 0
        bias_c = rel_bias[t5_bucket(dist_c)].reshape(NSA_QBLOCK, n_cmp, G, HG).transpose(2, 3, 0, 1)
        lc = jnp.einsum('bqghd,bcgd->bghqc', qb, k_cmp, preferred_element_type=jnp.float32) * scale + bias_c
        pc = jax.nn.softmax(jnp.where(valid_c, lc, NEG_INF), axis=-1) * jnp.any(valid_c, axis=-1)[:, None]
        o_cmp = jnp.einsum('bghqc,bcgd->bqghd', pc.astype(q.dtype), v_cmp)
        imp = jnp.einsum('bghqc,cn->bgqn', pc, cover)
        cur = tq // SLC_BLOCK
        forced = (j_idx[None] == 0) | (j_idx[None] == cur[:, None]) | (j_idx[None] == cur[:, None] - 1)
        causal_s = s_start[None, :] <= tq[:, None]
        score = jnp.where(forced, FORCE_SCORE, jnp.where(causal_s, imp, -1.0))
        top_val, top_idx = lax.top_k(score, n_sel)
        ks = k_slc[b_idx, g_idx, top_idx]
        vs = v_slc[b_idx, g_idx, top_idx]
        k_pos = top_idx[..., None] * SLC_BLOCK + jnp.arange(SLC_BLOCK)
        dist_s = tq[None, None, :, None, None] - k_pos
        valid_s = ((top_val >= 0)[..., None] & (dist_s >= 0)).reshape(bsz, G, 1, NSA_QBLOCK, n_keys)
        bias_s = rb_group[g_idx[..., None], t5_bucket(dist_s)]
        bias_s = bias_s.transpose(0, 1, 5, 2, 3, 4).reshape(bsz, G, HG, NSA_QBLOCK, n_keys)
        ls = jnp.einsum('bqghd,bgqnkd->bghqnk', qb, ks, preferred_element_type=jnp.float32)
        ls = ls.reshape(bsz, G, HG, NSA_QBLOCK, n_keys) * scale + bias_s
        ps = jax.nn.softmax(jnp.where(valid_s, ls, NEG_INF), axis=-1)
        o_slc = jnp.einsum('bghqk,bgqkd->bqghd', ps.astype(q.dtype), vs.reshape(bsz, G, NSA_QBLOCK, n_keys, dh))
        kw = lax.dynamic_slice_in_dim(k_win, q0, kw_len, axis=1)
        vw = lax.dynamic_slice_in_dim(v_win, q0, kw_len, axis=1)
        k_pos_w = q0 - WINDOW + jnp.arange(kw_len)
        dist_w = tq[:, None] - k_pos_w[None, :]
        valid_w = (dist_w >= 0) & (dist_w < WINDOW) & (k_pos_w[None, :] >= 0)
        bias_w = rel_bias[t5_bucket(dist_w)].reshape(NSA_QBLOCK, kw_len, G, HG).transpose(2, 3, 0, 1)
        lw = jnp.einsum('bqghd,bkgd->bghqk', qb, kw, preferred_element_type=jnp.float32) * scale + bias_w
        pw = jax.nn.softmax(jnp.where(valid_w, lw, NEG_INF), axis=-1)
        o_win = jnp.einsum('bghqk,bkgd->bqghd', pw.astype(q.dtype), vw)
        out = gb[..., 0:1] * o_cmp + gb[..., 1:2] * o_slc + gb[..., 2:3] * o_win
        return out.reshape(bsz, NSA_QBLOCK, NSA_HEADS * dh)

    out = lax.map(block, jnp.arange(seq // NSA_QBLOCK))
    return out.transpose(1, 0, 2, 3).reshape(bsz, seq, NSA_HEADS * dh)


def mla_attention(c_q, c_kv, k_rope, positions, norm_q, norm_kv, w_uq, w_ukv):
    bsz, seq = c_q.shape[0], c_q.shape[1]
    q = (rmsnorm(c_q, norm_q) @ w_uq).reshape(bsz, seq, MLA_HEADS, QK_NOPE + QK_ROPE)
    kv = (rmsnorm(c_kv, norm_kv) @ w_ukv).reshape(bsz, seq, MLA_HEADS, QK_NOPE + V_DIM)
    q_nope, q_pe = q[..., :QK_NOPE], q[..., QK_NOPE:]
    k_nope, v = kv[..., :QK_NOPE], kv[..., QK_NOPE:]
    half = QK_ROPE // 2
    inv_freq = ROPE_THETA ** (-jnp.arange(half, dtype=jnp.float32) / half)
    ang = positions.astype(jnp.float32)[..., None] * inv_freq
    cos, sin = jnp.cos(ang), jnp.sin(ang)
    q_pe = apply_rope(q_pe, cos[:, :, None, :], sin[:, :, None, :])
    k_pe = apply_rope(k_rope, cos, sin)
    scale = (QK_NOPE + QK_ROPE) ** -0.5
    k_idx = jnp.arange(seq)

    def block(i):
        q0 = i * ATTN_QBLOCK
        qn = lax.dynamic_slice_in_dim(q_nope, q0, ATTN_QBLOCK, axis=1)
        qp = lax.dynamic_slice_in_dim(q_pe, q0, ATTN_QBLOCK, axis=1)
        logits = (jnp.einsum('bqhd,bkhd->bhqk', qn, k_nope, preferred_element_type=jnp.float32)
                  + jnp.einsum('bqhr,bkr->bhqk', qp, k_pe, preferred_element_type=jnp.float32)) * scale
        causal = k_idx[None, :] <= (q0 + jnp.arange(ATTN_QBLOCK))[:, None]
        p = jax.nn.softmax(jnp.where(causal, logits, NEG_INF), axis=-1)
        return jnp.einsum('bhqk,bkhd->bqhd', p.astype(v.dtype), v)

    out = lax.map(block, jnp.arange(seq // ATTN_QBLOCK))
    return out.transpose(1, 0, 2, 3, 4).reshape(bsz, seq, MLA_HEADS * V_DIM)


def memory_cross_attention(h, mem_n, w_q, w_kv, w_o):
    bsz, seq = h.shape[0], h.shape[1]
    m_len = mem_n.shape[1]
    q = (h @ w_q).reshape(bsz, seq, XATTN_HEADS, XATTN_HEAD_DIM)
    kv = (mem_n @ w_kv).reshape(bsz, m_len, 2, XATTN_HEADS, XATTN_HEAD_DIM)
    logits = jnp.einsum('bqhd,bmhd->bhqm', q, kv[:, :, 0], preferred_element_type=jnp.float32) * XATTN_HEAD_DIM ** -0.5
    p = jax.nn.softmax(logits, axis=-1)
    o = jnp.einsum('bhqm,bmhd->bqhd', p.astype(h.dtype), kv[:, :, 1]).reshape(bsz, seq, XATTN_HEADS * XATTN_HEAD_DIM)
    return o @ w_o


def setup_inputs(seed: int = 0) -> dict:
    key = jax.random.key(seed)
    keys = jax.random.split(key, 48)
    counter = [0]
    L = DEPTH

    def nxt():
        k = keys[counter[0]]
        counter[0] += 1
        return k

    def dense(shape, fan_in):
        return jax.random.normal(nxt(), shape, jnp.float32) * fan_in ** -0.5

    def gain(shape):
        return 1.0 + 0.02 * jax.random.normal(nxt(), shape, jnp.float32)

    def small(shape):
        return 0.02 * jax.random.normal(nxt(), shape, jnp.float32)

    dh = NSA_HEAD_DIM
    x = jax.random.normal(nxt(), (BATCH, SEQ, D_MODEL), jnp.float32)
    mem = jax.random.normal(nxt(), (BATCH, MEM_LEN, D_MODEL), jnp.float32)
    offsets = jax.random.randint(nxt(), (BATCH, 1), 0, 2048, dtype=jnp.int32)
    positions = offsets + jnp.arange(SEQ, dtype=jnp.int32)[None, :]
    return {
        'x': x,
        'mem': mem,
        'positions': positions,
        'rel_bias': 0.1 * jax.random.normal(nxt(), (REL_BUCKETS, NSA_HEADS), jnp.float32),
        'norm_mix': gain((L, D_MODEL)),
        'norm_xattn': gain((L, D_MODEL)),
        'norm_mem': gain((L, D_MODEL)),
        'norm_ffn': gain((L, D_MODEL)),
        'norm_final': gain((D_MODEL,)),
        'w_in': dense((L, D_MODEL, IN_COLS), D_MODEL),
        'conv_w': dense((L, CONV_WIDTH, CONV_CH), CONV_WIDTH),
        'conv_b': small((L, CONV_CH)),
        'conv_ln_g': gain((L, CONV_CH)),
        'conv_ln_b': small((L, CONV_CH)),
        'w_branch_conv': dense((L, CONV_CH, D_MODEL), CONV_CH),
        'cmp_pos_k': dense((L, CMP_BLOCK, dh), 4),
        'cmp_w1_k': dense((L, CMP_BLOCK * dh, CMP_HIDDEN), CMP_BLOCK * dh),
        'cmp_b1_k': small((L, CMP_HIDDEN)),
        'cmp_w2_k': dense((L, CMP_HIDDEN, dh), CMP_HIDDEN),
        'cmp_pos_v': dense((L, CMP_BLOCK, dh), 4),
        'cmp_w1_v': dense((L, CMP_BLOCK * dh, CMP_HIDDEN), CMP_BLOCK * dh),
        'cmp_b1_v': small((L, CMP_HIDDEN)),
        'cmp_w2_v': dense((L, CMP_HIDDEN, dh), CMP_HIDDEN),
        'w_branch_nsa': dense((L, NSA_HEADS * dh, D_MODEL), NSA_HEADS * dh),
        'mla_norm_q': gain((L, Q_RANK)),
        'mla_norm_kv': gain((L, KV_RANK)),
        'w_uq': dense((L, Q_RANK, MLA_HEADS * (QK_NOPE + QK_ROPE)), Q_RANK),
        'w_ukv': dense((L, KV_RANK, MLA_HEADS * (QK_NOPE + V_DIM)), KV_RANK),
        'w_branch_mla': dense((L, MLA_HEADS * V_DIM, D_MODEL), MLA_HEADS * V_DIM),
        'w_out': dense((L, D_MODEL, D_MODEL), D_MODEL),
        'w_xq': dense((L, D_MODEL, XATTN_HEADS * XATTN_HEAD_DIM), D_MODEL),
        'w_xkv': dense((L, D_MODEL, 2 * XATTN_HEADS * XATTN_HEAD_DIM), D_MODEL),
        'w_xo': dense((L, XATTN_HEADS * XATTN_HEAD_DIM, D_MODEL), XATTN_HEADS * XATTN_HEAD_DIM),
        'w_gate_up': dense((L, D_MODEL, 2 * FFN_HIDDEN), D_MODEL),
        'w_down': dense((L, FFN_HIDDEN, D_MODEL), FFN_HIDDEN),
    }


def reference(x, mem, positions, rel_bias, norm_mix, norm_xattn, norm_mem, norm_ffn, norm_final,
              w_in, conv_w, conv_b, conv_ln_g, conv_ln_b, w_branch_conv,
              cmp_pos_k, cmp_w1_k, cmp_b1_k, cmp_w2_k, cmp_pos_v, cmp_w1_v, cmp_b1_v, cmp_w2_v, w_branch_nsa,
              mla_norm_q, mla_norm_kv, w_uq, w_ukv, w_branch_mla, w_out,
              w_xq, w_xkv, w_xo, w_gate_up, w_down):
    split_at = [int(v) for v in np.cumsum(IN_SIZES)[:-1]]
    for layer in range(DEPTH):
        h = rmsnorm(x, norm_mix[layer])
        z = h @ w_in[layer]
        u_glu, q_nsa, kv_nsa, g_nsa, c_q, c_kv, k_rope, g_merge = jnp.split(z, split_at, axis=-1)
        y_conv = conformer_conv(u_glu, conv_w[layer], conv_b[layer], conv_ln_g[layer], conv_ln_b[layer], w_branch_conv[layer])
        y_nsa = nsa_attention(q_nsa, kv_nsa, g_nsa, rel_bias,
                              cmp_pos_k[layer], cmp_w1_k[layer], cmp_b1_k[layer], cmp_w2_k[layer],
                              cmp_pos_v[layer], cmp_w1_v[layer], cmp_b1_v[layer], cmp_w2_v[layer]) @ w_branch_nsa[layer]
        y_mla = mla_attention(c_q, c_kv, k_rope, positions, mla_norm_q[layer], mla_norm_kv[layer],
                              w_uq[layer], w_ukv[layer]) @ w_branch_mla[layer]
        g = jax.nn.sigmoid(g_merge.astype(jnp.float32)).astype(x.dtype)
        g_conv, g_attn, g_mla = jnp.split(g, 3, axis=-1)
        x = x + (g_conv * y_conv + g_attn * y_nsa + g_mla * y_mla) @ w_out[layer]
        x = x + memory_cross_attention(rmsnorm(x, norm_xattn[layer]), rmsnorm(mem, norm_mem[layer]),
                                       w_xq[layer], w_xkv[layer], w_xo[layer])
        gate, up = jnp.split(rmsnorm(x, norm_ffn[layer]) @ w_gate_up[layer], 2, axis=-1)
        x = x + (jax.nn.silu(gate) * up) @ w_down[layer]
    return rmsnorm(x, norm_final)
```

```python
import contextlib
import math
import numpy as np
import concourse.bass as bass
import concourse.mybir as mybir
from concourse.ap import AP
from concourse.bass_utils import run_bass_kernel_spmd

F32 = mybir.dt.float32
BF16 = mybir.dt.bfloat16
I32 = mybir.dt.int32
U8 = mybir.dt.uint8
ALU = mybir.AluOpType
AF = mybir.ActivationFunctionType
AX = mybir.AxisListType

ENGS = ["pe", "act", "dve", "pool", "sp"]
N_DSEM = 8
DTSIZE = {F32: 4, BF16: 2, I32: 4, U8: 1}

S = 4096
D = 1024
NT = S // 128
NQC = S // 512
IN_COLS = 6104
FFN = 2816
NEG = -30000.0


class Buf:
    __slots__ = ("w", "r")

    def __init__(self):
        self.w = None
        self.r = {}


class Tl:
    __slots__ = ("ap", "b")

    def __init__(self, ap, b=None):
        self.ap = ap
        self.b = b if b is not None else Buf()


class Prog:
    def __init__(self, nc, es):
        self.nc = nc
        self.es = es
        self.q = {e: [] for e in ENGS}
        self.cnt = {}
        self.sems = {}
        self.known = {e: {} for e in ENGS}
        self.ep = {}
        self.ekey = {}
        for e in ENGS:
            self.ep[e] = 0
            self._new_epoch(e)
        self.dpool = {}
        self.dnext = {}
        for e in ("sp", "pool", "act"):
            self.dpool[e] = []
            for i in range(N_DSEM):
                k = f"d_{e}_{i}"
                self.sems[k] = es.enter_context(nc.semaphore(k))
                self.cnt[k] = 0
                self.dpool[e].append(k)
            self.dnext[e] = 0
        self.nins = 0

    SEM_LIMIT = 12000

    def _new_epoch(self, e):
        key = f"{e}#{self.ep[e]}"
        self.ep[e] += 1
        self.sems[key] = self.es.enter_context(self.nc.semaphore(f"sem_{e}_{self.ep[e]}"))
        self.cnt[key] = 0
        self.ekey[e] = key

    def _need(self, eng, R, W):
        need = {}

        def add(ev):
            if ev is None:
                return
            k, v = ev
            if need.get(k, 0) < v:
                need[k] = v
        for b in R:
            add(b.w)
        for b in W:
            add(b.w)
            for k, v in b.r.items():
                add((k, v))
        kn = self.known[eng]
        for k, v in need.items():
            if eng == "pe" and k.startswith("pe#"):
                continue
            if kn.get(k, 0) >= v:
                continue
            kn[k] = v
            self.q[eng].append(("wait", k, v))

    def _done(self, ev, R, W):
        k, v = ev
        for b in W:
            b.w = ev
            b.r = {}
        for b in R:
            if b.r.get(k, 0) < v:
                b.r[k] = v

    def op(self, eng, fn, R=(), W=()):
        self._need(eng, R, W)
        if self.cnt[self.ekey[eng]] >= self.SEM_LIMIT:
            self._new_epoch(eng)
        key = self.ekey[eng]
        self.cnt[key] += 1
        self.q[eng].append(("ins", fn, key, 1))
        self._done((key, self.cnt[key]), R, W)
        self.nins += 1

    def dma(self, eng, out, in_, R=(), W=(), **kw):
        self._need(eng, R, W)
        pool = self.dpool[eng]
        k = pool[self.dnext[eng] % len(pool)]
        self.dnext[eng] += 1
        if self.cnt[k] > 0 and self.known[eng].get(k, 0) < self.cnt[k]:
            self.known[eng][k] = self.cnt[k]
            self.q[eng].append(("wait", k, self.cnt[k]))
        self.cnt[k] += 16
        self.q[eng].append(("ins", lambda e: e.dma_start(out=out, in_=in_, **kw), k, 16))
        self._done((k, self.cnt[k]), R, W)
        self.nins += 1

    def barrier(self):
        for e in ENGS:
            kn = self.known[e]
            for k, c in self.cnt.items():
                if k.split("#")[0] == e or c == 0:
                    continue
                if kn.get(k, 0) < c:
                    kn[k] = c
                    self.q[e].append(("wait", k, c))

    def cut(self):
        self.barrier()
        for e in ENGS:
            self.q[e].append(("cut",))

    def finish(self):
        self.barrier()
        nc = self.nc
        sems = self.sems
        segs = {}
        nseg = 1
        for e in ENGS:
            cur = []
            segs[e] = [cur]
            for it in self.q[e]:
                if it[0] == "cut":
                    cur = []
                    segs[e].append(cur)
                else:
                    cur.append(it)
            nseg = max(nseg, len(segs[e]))

        def replay(items):
            def f(e):
                for it in items:
                    if it[0] == "wait":
                        e.wait_ge(sems[it[1]], it[2])
                    else:
                        it[1](e).then_inc(sems[it[2]], it[3])
            return f

        for s in range(nseg):
            if not any(len(segs[e][s]) for e in ENGS):
                continue
            with nc.Block() as block:
                block.tensor(replay(segs["pe"][s]))
                block.scalar(replay(segs["act"][s]))
                block.vector(replay(segs["dve"][s]))
                block.gpsimd(replay(segs["pool"][s]))
                block.sync(replay(segs["sp"][s]))


class Arena:
    def __init__(self, t, size):
        self.t = t
        self.size = size
        self.off = 0

    def alloc(self, shape, dtype, parts=None):
        p = shape[0] if parts is None else parts
        n = 1
        for s in shape[1:]:
            n *= s
        nb = n * DTSIZE[dtype]
        nb_al = (nb + 63) // 64 * 64
        assert self.off + nb_al <= self.size, f"SBUF arena overflow {self.off}+{nb_al}>{self.size}"
        v = self.t[0:p, self.off:self.off + nb].bitcast(dtype)
        self.off += nb_al
        if len(shape) == 3:
            v = v.rearrange("p (a b) -> p a b", a=shape[1])
        elif len(shape) == 4:
            v = v.rearrange("p (a b c) -> p a b c", a=shape[1], b=shape[2])
        return Tl(v)

    def mark(self):
        return self.off

    def reset(self, m):
        self.off = m


def t5_bucket_np(d):
    d = np.asarray(d)
    dd = np.maximum(d, 0)
    lr = np.log(np.maximum(dd, 1).astype(np.float32) / np.float32(16)) / np.float32(math.log(8.0))
    large = np.minimum(16 + (lr * 16).astype(np.int32), 31)
    return np.where(dd < 16, dd, large)


def host_consts():
    c = {}
    c["ident"] = np.eye(128, dtype=np.float32)
    oh = np.zeros((33, 384), np.float32)
    for i in range(383):
        d = i - 127
        if d >= 0:
            oh[int(t5_bucket_np(d)), i] += 1.0
            oh[31, i] -= 1.0
        else:
            oh[32, i] = NEG
    c["oh_v"] = oh
    ohc = np.zeros((33, 128 * 16), np.float32)
    for ql in range(128):
        for cp in range(16):
            d = ql - 16 * cp + 97
            j = ql * 16 + cp
            if d >= 0:
                ohc[int(t5_bucket_np(d)), j] += 1.0
                ohc[31, j] -= 1.0
            else:
                ohc[32, j] = NEG
    c["oh_c"] = ohc
    kl = np.arange(128)[:, None]
    ql = np.arange(128)[None, :]
    c["emask"] = np.where(ql >= kl, NEG, 0.0).astype(np.float32)
    c["cmask"] = np.where(ql < kl, NEG, 0.0).astype(np.float32)
    eb = np.zeros((64, S), np.float32)
    for j in range(64):
        eb[j, j * 64:(j + 1) * 64] = 1.0
    c["ebig"] = eb
    cs = 16 * np.arange(255)
    ss = 64 * np.arange(64)
    cov = ((cs[:, None] < ss[None, :] + 64) & (cs[:, None] + 32 > ss[None, :])).astype(np.float32)
    covp = np.zeros((256, 64), np.float32)
    covp[:255] = cov
    c["cover"] = covp
    t = np.arange(S)[:, None]
    j = np.arange(64)[None, :]
    cur = t // 64
    forced = (j == 0) | (j == cur) | (j == cur - 1)
    causal = (64 * j) <= t
    c["selA"] = np.where(forced, 0.0, np.where(causal, 1.0, 0.0)).astype(np.float32)
    c["selB"] = np.where(forced, 1e4 + j, np.where(causal, 0.0, -1.0)).astype(np.float32)
    c["cflag"] = (np.arange(S) >= 31).astype(np.float32).reshape(S, 1)
    half = 32
    inv = (np.float32(10000.0) ** (-np.arange(half, dtype=np.float32) / np.float32(half))).astype(np.float32)
    c["invfreq"] = (inv / np.float32(2 * math.pi)).astype(np.float32).reshape(32, 1)
    return c


CONST_SHAPES = {
    "ident": [128, 128], "oh_v": [33, 384], "oh_c": [33, 2048], "emask": [128, 128], "cmask": [128, 128],
    "ebig": [64, S], "cover": [256, 64], "selA": [S, 64], "selB": [S, 64], "cflag": [S, 1], "invfreq": [32, 1],
}

WEIGHT_SHAPES = {
    "x": [S, D], "mem": [256, D], "positions": [S], "rel_bias": [32, 8],
    "norm_mix": [2, D], "norm_xattn": [2, D], "norm_mem": [2, D], "norm_ffn": [2, D], "norm_final": [D],
    "w_in": [2, D, IN_COLS], "conv_w": [2, 31, 512], "conv_b": [2, 512], "conv_ln_g": [2, 512], "conv_ln_b": [2, 512],
    "w_branch_conv": [2, 512, D],
    "cmp_pos_k": [2, 32, 64], "cmp_w1_k": [2, 2048, 256], "cmp_b1_k": [2, 256], "cmp_w2_k": [2, 256, 64],
    "cmp_pos_v": [2, 32, 64], "cmp_w1_v": [2, 2048, 256], "cmp_b1_v": [2, 256], "cmp_w2_v": [2, 256, 64],
    "w_branch_nsa": [2, 512, D], "mla_norm_q": [2, 384], "mla_norm_kv": [2, 256],
    "w_uq": [2, 384, 768], "w_ukv": [2, 256, 1024], "w_branch_mla": [2, 512, D], "w_out": [2, D, D],
    "w_xq": [2, D, 512], "w_xkv": [2, D, D], "w_xo": [2, 512, D], "w_gate_up": [2, D, 2 * FFN], "w_down": [2, FFN, D],
}


class K:
    def __init__(self, nc, es, debug=None):
        self.nc = nc
        self.es = es
        self.debug = debug or []
        self.skip = []
        self.P = Prog(nc, es)
        self.I = {}
        for n, sh in WEIGHT_SHAPES.items():
            dt = I32 if n == "positions" else F32
            self.I[n] = nc.dram_tensor(n, sh, dt, kind="ExternalInput").ap()
        self.C = {}
        for n, sh in CONST_SHAPES.items():
            self.C[n] = nc.dram_tensor("c_" + n, sh, F32, kind="ExternalInput").ap()
        self.out = nc.dram_tensor("y", [S, D], F32, kind="ExternalOutput").ap()
        self.Db = {}
        self.Dr = {}
        arena_t = es.enter_context(nc.sbuf_tensor("arena", [128, 204800], U8))
        self.A = Arena(arena_t, 204800)
        ps = es.enter_context(nc.psum_tensor("ps", [128, 8, 512], F32))
        self.ps = ps
        self.pb = [Buf() for _ in range(8)]

    def dram(self, name, shape, dtype):
        kind = "ExternalOutput" if name in self.debug else "Internal"
        t = self.nc.dram_tensor(name, list(shape), dtype, kind=kind)
        self.Dr[name] = t
        self.Db[name] = Buf()
        return t.ap()

    def MM(self, out, lhsT, rhs, start, stop, R, W):
        self.P.op("pe", lambda e: e.matmul(out, lhsT=lhsT, rhs=rhs, start=start, stop=stop), R, W)

    def TR(self, out, in_, R, W):
        ident = self.identb.ap
        self.P.op("pe", lambda e: e.transpose(out=out, in_=in_, identity=ident), list(R) + [self.identb.b], W)

    def ACT(self, out, in_, func, R, W, **kw):
        self.P.op("act", lambda e: e.activation(out=out, in_=in_, func=func, **kw), R, W)

    def V(self, name, R, W, **kw):
        self.P.op("dve", lambda e: getattr(e, name)(**kw), R, W)

    def G(self, name, R, W, **kw):
        self.P.op("pool", lambda e: getattr(e, name)(**kw), R, W)

    def E(self, eng, name, R, W, **kw):
        self.P.op(eng, lambda e: getattr(e, name)(**kw), R, W)

    def LD(self, out, in_, R=(), W=(), **kw):
        self.P.dma("sp", out, in_, R, W, **kw)

    def ST(self, out, in_, R=(), W=(), **kw):
        self.P.dma("pool", out, in_, R, W, **kw)

    def psf(self, i):
        return self.ps[:, i, :]

    def psb(self, i):
        return self.ps[:, i, :].bitcast(BF16)

    def setup(self):
        A = self.A
        self.identf = A.alloc([128, 128], F32)
        self.identb = A.alloc([128, 128], BF16)
        self.eps = A.alloc([128, 1], F32)
        self.onesf = A.alloc([128, 128], F32)
        self.onesb = A.alloc([128, 128], BF16)
        self.LD(self.identf.ap, self.C["ident"], W=[self.identf.b])
        self.V("tensor_copy", [self.identf.b], [self.identb.b], out=self.identb.ap, in_=self.identf.ap)
        self.V("memset", [], [self.eps.b], ap=self.eps.ap, constant=1e-6)
        self.V("memset", [], [self.onesf.b], ap=self.onesf.ap, constant=1.0)
        self.V("memset", [], [self.onesb.b], ap=self.onesb.ap, constant=1.0)
        self.base_mark = A.mark()

    def phase_end(self):
        self.P.cut()
        self.A.reset(self.base_mark)

    def release(self, m):
        self.P.barrier()
        self.A.reset(m)

    def load_weight_bf16(self, dst, src, stage, kchunks, ncols, parts=128):
        sv = stage.ap[0:parts, 0:kchunks * ncols].rearrange("p (k n) -> p k n", k=kchunks)
        self.LD(sv, src.rearrange("(k p) n -> p k n", p=parts), W=[stage.b])
        self.G("tensor_copy", [stage.b], [dst.b], out=dst.ap, in_=sv)

    def rms_rstd(self, xt, junk, ss, rstd, n):
        self.ACT(junk.ap, xt.ap, AF.Square, [xt.b], [junk.b, ss.b], scale=float(n) ** -0.5, accum_out=ss.ap)
        self.ACT(rstd.ap, ss.ap, AF.Sqrt, [ss.b, self.eps.b], [rstd.b], bias=self.eps.ap, scale=1.0)
        self.V("reciprocal", [rstd.b], [rstd.b], out=rstd.ap, in_=rstd.ap)

    def norm_to_hT(self, src, gain, hT, ntile=NT, width=D, pbank=0):
        A = self.A
        m = A.mark()
        kc = width // 128
        gb = A.alloc([128, width], F32)
        self.LD(gb.ap, gain.partition_broadcast(128), W=[gb.b])
        xts = [A.alloc([128, width], F32) for _ in range(2)]
        junk = A.alloc([128, width], F32)
        hs = [A.alloc([128, width], BF16) for _ in range(2)]
        sss = [A.alloc([128, 1], F32) for _ in range(2)]
        rss = [A.alloc([128, 1], F32) for _ in range(2)]
        for t in range(ntile):
            xt, h, ss, rstd = xts[t % 2], hs[t % 2], sss[t % 2], rss[t % 2]
            self.LD(xt.ap, src[t * 128:(t + 1) * 128, :], R=[self.cur_x_b], W=[xt.b])
            self.rms_rstd(xt, junk, ss, rstd, width)
            self.V("scalar_tensor_tensor", [xt.b, rstd.b, gb.b], [h.b], out=h.ap, in0=xt.ap, scalar=rstd.ap[:, 0:1],
                   in1=gb.ap, op0=ALU.mult, op1=ALU.mult)
            bank = pbank + (t % 2)
            pt = self.psb(bank)[:, 0:kc * 128].rearrange("p (k n) -> p k n", k=kc)
            for k in range(kc):
                self.TR(pt[:, k, :], h.ap[:, k * 128:(k + 1) * 128], [h.b], [self.pb[bank]])
            self.ACT(hT.ap[:, :, t * 128:(t + 1) * 128], pt, AF.Copy, [self.pb[bank]], [hT.b])
        self.release(m)

    def ph_inproj(self, l, xsrc):
        A = self.A
        I = self.I
        w = I["w_in"][l]
        hT = A.alloc([128, 8, S], BF16)
        self.norm_to_hT(xsrc, I["norm_mix"][l], hT)
        wst = [A.alloc([128, 8 * 512], F32) for _ in range(2)]
        wbf = [A.alloc([128, 8, 512], BF16) for _ in range(2)]
        stg = [A.alloc([128, S], F32) for _ in range(2)]
        sig = [A.alloc([128, 512], F32) for _ in range(2)]
        self._u = 0
        self._s = 0
        self._pbk = 0

        def load_unit(ranges):
            i = self._u % 2
            self._u += 1
            off = 0
            ws, wb = wst[i], wbf[i]
            tot = sum(n for _, n in ranges)
            sv = ws.ap[:, 0:8 * tot].rearrange("p (k n) -> p k n", k=8)
            for (c0, n) in ranges:
                self.LD(sv[:, :, off:off + n], w[:, c0:c0 + n].rearrange("(k p) n -> p k n", p=128),
                        R=[wb.b], W=[ws.b])
                off += n
            self.G("tensor_copy", [ws.b], [wb.b], out=wb.ap[:, :, 0:tot], in_=sv)
            return wb

        def fm_mm(wb, off, m, tc, bank):
            for k in range(8):
                self.MM(self.psf(bank)[0:m, :], wb.ap[:, k, off:off + m], hT.ap[:, k, tc * 512:(tc + 1) * 512],
                        k == 0, k == 7, [wb.b, hT.b], [self.pb[bank]])

        def nb():
            b = self._pbk % 4
            self._pbk += 1
            return b

        def nstg():
            s = stg[self._s % 2]
            self._s += 1
            return s

        uT, qnT, kvT, krT, gmT = self.d_uT, self.d_qnT, self.d_kvT, self.d_krT, self.d_gmT
        for c in range(4):
            wb = load_unit([(c * 128, 128), (512 + c * 128, 128)])
            st = nstg()
            for tc in range(NQC):
                ba, bb = nb(), nb()
                fm_mm(wb, 0, 128, tc, ba)
                fm_mm(wb, 128, 128, tc, bb)
                sg = sig[tc % 2]
                self.ACT(sg.ap, self.psf(bb), AF.Sigmoid, [self.pb[bb]], [sg.b])
                self.V("tensor_tensor", [self.pb[ba], sg.b], [st.b], out=st.ap[:, tc * 512:(tc + 1) * 512],
                       in0=self.psf(ba), in1=sg.ap, op=ALU.mult)
            self.ST(uT[c * 128:(c + 1) * 128, :], st.ap, R=[st.b], W=[self.Db["uT"]])
        for hp in range(2):
            wb = load_unit([(1024 + hp * 256, 256)])
            for hh in range(4):
                h = hp * 4 + hh
                st = nstg()
                sb = st.ap[0:64, :].bitcast(BF16)
                for tc in range(NQC):
                    bk = nb()
                    fm_mm(wb, hh * 64, 64, tc, bk)
                    self.ACT(sb[:, tc * 512:(tc + 1) * 512], self.psf(bk)[0:64, :], AF.Copy, [self.pb[bk]], [st.b], scale=0.125)
                self.ST(qnT[h], sb[:, 0:S], R=[st.b], W=[self.Db["qnT"]])
        for si, slot in enumerate((0, 1, 2, 4)):
            wb = load_unit([(1536 + slot * 128, 128)])
            for g in range(2):
                st = nstg()
                sb = st.ap[0:64, :].bitcast(BF16)
                for tc in range(NQC):
                    bk = nb()
                    fm_mm(wb, g * 64, 64, tc, bk)
                    self.V("tensor_copy", [self.pb[bk]], [st.b], out=sb[:, tc * 512:(tc + 1) * 512], in_=self.psf(bk)[0:64, :])
                self.ST(kvT[si, g], sb[:, 0:S], R=[st.b], W=[self.Db["kvT"]])
        wb = load_unit([(2968, 64)])
        for hf in range(2):
            st = nstg()
            for tc in range(NQC):
                bk = nb()
                fm_mm(wb, hf * 32, 32, tc, bk)
                self.V("tensor_copy", [self.pb[bk]], [st.b], out=st.ap[0:32, tc * 512:(tc + 1) * 512], in_=self.psf(bk)[0:32, :])
            self.ST(krT[hf], st.ap[0:32, :], R=[st.b], W=[self.Db["krT"]])
        for cg in range(6):
            wb = load_unit([(3032 + cg * 512, 512)])
            for j in range(4):
                st = nstg()
                sb = st.ap.bitcast(BF16)
                for tc in range(NQC):
                    bk = nb()
                    fm_mm(wb, j * 128, 128, tc, bk)
                    self.ACT(sb[:, tc * 512:(tc + 1) * 512], self.psf(bk), AF.Sigmoid, [self.pb[bk]], [st.b])
                r0 = (cg * 4 + j) * 128
                self.ST(gmT[r0:r0 + 128, :], sb[:, 0:S], R=[st.b], W=[self.Db["gmT"]])
        wbv = load_unit([(1536 + 3 * 128, 128), (1536 + 5 * 128, 128)])
        wbm = load_unit([(2304, 512)])
        m = A.mark()
        ws3 = A.alloc([128, 8 * 152], F32)
        wb3 = A.alloc([128, 8, 152], BF16)
        sv3 = ws3.ap.rearrange("p (k n) -> p k n", k=8)
        self.LD(sv3, w[:, 2816:2968].rearrange("(k p) n -> p k n", p=128), W=[ws3.b])
        self.G("tensor_copy", [ws3.b], [wb3.b], out=wb3.ap, in_=sv3)
        vst = [A.alloc([128, 4, 65], BF16) for _ in range(2)]
        mst = [A.alloc([128, 664], F32) for _ in range(2)]
        for v_ in vst:
            self.V("memset", [], [v_.b], ap=v_.ap, constant=1.0)
        for t in range(NT):
            bv, bm, b3 = 4 + (t % 2) * 2, 5 + (t % 2) * 2, nb()
            vs, ms = vst[t % 2], mst[t % 2]
            for k in range(8):
                self.MM(self.psf(bv)[:, 0:256], hT.ap[:, k, t * 128:(t + 1) * 128], wbv.ap[:, k, 0:256], k == 0, k == 7,
                        [wbv.b, hT.b], [self.pb[bv]])
            for k in range(8):
                self.MM(self.psf(bm), hT.ap[:, k, t * 128:(t + 1) * 128], wbm.ap[:, k, 0:512], k == 0, k == 7,
                        [wbm.b, hT.b], [self.pb[bm]])
            for k in range(8):
                self.MM(self.psf(b3)[:, 0:152], hT.ap[:, k, t * 128:(t + 1) * 128], wb3.ap[:, k, :], k == 0, k == 7,
                        [wb3.b, hT.b], [self.pb[b3]])
            self.V("tensor_copy", [self.pb[bv]], [vs.b], out=vs.ap[:, :, 0:64],
                   in_=self.psf(bv)[:, 0:256].rearrange("p (a d) -> p a d", a=4))
            self.ACT(ms.ap[:, 0:24], self.psf(bm)[:, 0:24], AF.Sigmoid, [self.pb[bm]], [ms.b])
            self.V("tensor_copy", [self.pb[bm]], [ms.b], out=ms.ap[:, 24:512], in_=self.psf(bm)[:, 24:512])
            self.ACT(ms.ap[:, 512:664], self.psf(b3)[:, 0:152], AF.Copy, [self.pb[b3]], [ms.b])
            self.ST(self.d_vtm[t * 128:(t + 1) * 128], vs.ap, R=[vs.b], W=[self.Db["vtm"]])
            self.ST(self.d_misc[t * 128:(t + 1) * 128, :], ms.ap, R=[ms.b], W=[self.Db["misc"]])
        self.phase_end()

    def col_load(self, dst, src1d, c0, n=128):
        self.LD(dst, src1d[c0:c0 + n].rearrange("(p o) -> p o", o=1))

    def ph_conv(self, l):
        A, I = self.A, self.I
        cwn = A.alloc([31, 512], F32)
        self.LD(cwn.ap, I["conv_w"][l], W=[cwn.b])
        cwT = A.alloc([128, 4, 32], F32)
        for c in range(4):
            self.MM(self.psf(0)[:, c * 32:c * 32 + 31], cwn.ap[:, c * 128:(c + 1) * 128], self.identf.ap[0:31, 0:31],
                    True, True, [cwn.b, self.identf.b], [self.pb[0]])
        self.V("tensor_copy", [self.pb[0]], [cwT.b], out=cwT.ap[:, :, 0:31],
               in_=self.psf(0)[:, 0:128].rearrange("p (c k) -> p c k", c=4)[:, :, 0:31])
        vecs = A.alloc([128, 3, 4], F32)
        for vi, nm in enumerate(("conv_b", "conv_ln_g", "conv_ln_b")):
            for c in range(4):
                self.P.dma("sp", vecs.ap[:, vi, c:c + 1], I[nm][l][c * 128:(c + 1) * 128].rearrange("(p o) -> p o", o=1),
                           W=[vecs.b])
        accs = A.alloc([128, 4, S], F32)
        accb = [Buf() for _ in range(4)]
        ubuf = [A.alloc([128, 30 + S], F32) for _ in range(2)]
        for u in ubuf:
            self.G("memset", [], [u.b], ap=u.ap[:, 0:30], constant=0.0)
        for c in range(4):
            ub = ubuf[c % 2]
            self.LD(ub.ap[:, 30:30 + S], self.d_uT[c * 128:(c + 1) * 128, :], R=[self.Db["uT"]], W=[ub.b])
            acc = accs.ap[:, c, :]
            self.V("tensor_scalar", [ub.b, cwT.b, vecs.b], [accb[c]], out=acc, in0=ub.ap[:, 0:S], scalar1=cwT.ap[:, c, 0:1],
                   scalar2=vecs.ap[:, 0, c:c + 1], op0=ALU.mult, op1=ALU.add)
            for k in range(1, 31):
                self.V("scalar_tensor_tensor", [ub.b, cwT.b, accb[c]], [accb[c]], out=acc, in0=ub.ap[:, k:k + S],
                       scalar=cwT.ap[:, c, k:k + 1], in1=acc, op0=ALU.mult, op1=ALU.add)
        uact = A.alloc([128, 4, S], BF16)
        tmp = {n: [A.alloc([128, 512], F32) for _ in range(2)] for n in ("sq", "mean", "msq", "var", "t")}
        for tc in range(NQC):
            sl = slice(tc * 512, (tc + 1) * 512)
            i2 = tc % 2
            b1, b2 = 1 + 2 * i2, 2 + 2 * i2
            for c in range(4):
                self.MM(self.psf(b1), self.onesf.ap, accs.ap[:, c, sl], c == 0, c == 3, [self.onesf.b, accb[c]], [self.pb[b1]])
            for c in range(4):
                sq = tmp["sq"][c % 2]
                self.ACT(sq.ap, accs.ap[:, c, sl], AF.Square, [accb[c]], [sq.b])
                self.MM(self.psf(b2), self.onesf.ap, sq.ap, c == 0, c == 3, [self.onesf.b, sq.b], [self.pb[b2]])
            mean, msq, var = tmp["mean"][i2], tmp["msq"][i2], tmp["var"][i2]
            self.ACT(mean.ap, self.psf(b1), AF.Copy, [self.pb[b1]], [mean.b], scale=1.0 / 512)
            self.V("tensor_tensor", [mean.b], [msq.b], out=msq.ap, in0=mean.ap, in1=mean.ap, op=ALU.mult)
            self.V("scalar_tensor_tensor", [self.pb[b2], msq.b], [var.b], out=var.ap, in0=self.psf(b2), scalar=1.0 / 512,
                   in1=msq.ap, op0=ALU.mult, op1=ALU.subtract)
            self.ACT(var.ap, var.ap, AF.Sqrt, [var.b, self.eps.b], [var.b], bias=self.eps.ap, scale=1.0)
            self.V("reciprocal", [var.b], [var.b], out=var.ap, in_=var.ap)
            for c in range(4):
                t = tmp["t"][c % 2]
                self.V("tensor_tensor", [accb[c], mean.b], [t.b], out=t.ap, in0=accs.ap[:, c, sl], in1=mean.ap, op=ALU.subtract)
                self.V("tensor_tensor", [t.b, var.b], [t.b], out=t.ap, in0=t.ap, in1=var.ap, op=ALU.mult)
                self.ACT(uact.ap[:, c, sl], t.ap, AF.Silu, [t.b, vecs.b], [uact.b], scale=vecs.ap[:, 1, c:c + 1],
                         bias=vecs.ap[:, 2, c:c + 1])
        for c in range(4):
            self.ST(self.d_uactT[c * 128:(c + 1) * 128, :], uact.ap[:, c, :], R=[uact.b], W=[self.Db["uactT"]])
        self.phase_end()

    def ph_bias_tables(self):
        A, I, C = self.A, self.I, self.C
        rb = A.alloc([33, 8], F32)
        self.V("memset", [], [rb.b], ap=rb.ap, constant=1.0)
        self.LD(rb.ap[0:32, :], I["rel_bias"], R=[rb.b], W=[rb.b])
        ohv = A.alloc([33, 384], F32)
        self.LD(ohv.ap, C["oh_v"], W=[ohv.b])
        ohc = A.alloc([33, 2048], F32)
        self.LD(ohc.ap, C["oh_c"], W=[ohc.b])
        vrep = [A.alloc([128, 384], BF16) for _ in range(2)]
        for h in range(8):
            bk = h % 2
            vr = vrep[h % 2]
            self.MM(self.psf(bk)[:, 0:384], rb.ap[:, h:h + 1].to_broadcast([33, 128]), ohv.ap, True, True, [rb.b, ohv.b], [self.pb[bk]])
            self.V("tensor_copy", [self.pb[bk]], [vr.b], out=vr.ap, in_=self.psf(bk)[:, 0:384])
            self.ST(self.d_vs[h], vr.ap, R=[vr.b], W=[self.Db["vs"]])
        bcs = A.alloc([8, 2048], BF16)
        for j in range(4):
            bk = 2 + j % 2
            self.MM(self.psf(bk)[0:8, :], rb.ap, ohc.ap[:, j * 512:(j + 1) * 512], True, True, [rb.b, ohc.b], [self.pb[bk]])
            self.V("tensor_copy", [self.pb[bk]], [bcs.b], out=bcs.ap[:, j * 512:(j + 1) * 512], in_=self.psf(bk)[0:8, :])
        self.ST(self.d_bc, bcs.ap, R=[bcs.b], W=[self.Db["bc"]])
        self.phase_end()

    def ph_rope_tables(self):
        A, I, C = self.A, self.I, self.C
        posi = A.alloc([32, S], I32)
        self.LD(posi.ap, I["positions"].partition_broadcast(32), W=[posi.b])
        inv = A.alloc([32, 1], F32)
        self.LD(inv.ap, C["invfreq"], W=[inv.b])
        turns = A.alloc([32, S], F32)
        self.V("tensor_copy", [posi.b], [turns.b], out=turns.ap, in_=posi.ap)
        self.V("tensor_scalar", [turns.b, inv.b], [turns.b], out=turns.ap, in0=turns.ap, scalar1=inv.ap[:, 0:1], scalar2=None,
               op0=ALU.mult)
        r = A.alloc([32, S], F32)
        ti = A.alloc([32, S], I32)
        tf = A.alloc([32, S], F32)
        fl = A.alloc([32, S], F32)
        res = A.alloc([32, S], F32)
        qs = 192.0 ** -0.5
        for idx, shift in ((1, 0.0), (0, 0.25)):
            self.V("tensor_scalar", [turns.b], [r.b], out=r.ap, in0=turns.ap, scalar1=shift, scalar2=None, op0=ALU.add)
            self.V("tensor_copy", [r.b], [ti.b], out=ti.ap, in_=r.ap)
            self.V("tensor_copy", [ti.b], [tf.b], out=tf.ap, in_=ti.ap)
            self.V("tensor_tensor", [r.b, tf.b], [r.b], out=r.ap, in0=r.ap, in1=tf.ap, op=ALU.subtract)
            self.V("tensor_scalar", [r.b], [fl.b], out=fl.ap, in0=r.ap, scalar1=0.5, scalar2=None, op0=ALU.is_gt)
            self.V("tensor_tensor", [r.b, fl.b], [r.b], out=r.ap, in0=r.ap, in1=fl.ap, op=ALU.subtract)
            self.V("tensor_scalar", [r.b], [fl.b], out=fl.ap, in0=r.ap, scalar1=-0.5, scalar2=None, op0=ALU.is_lt)
            self.V("tensor_tensor", [r.b, fl.b], [r.b], out=r.ap, in0=r.ap, in1=fl.ap, op=ALU.add)
            self.ACT(res.ap, r.ap, AF.Sin, [r.b], [res.b], scale=2.0 * math.pi)
            self.ST(self.d_cs[idx], res.ap, R=[res.b], W=[self.Db["cs"]])
            self.V("tensor_scalar", [res.b], [tf.b], out=tf.ap, in0=res.ap, scalar1=qs, scalar2=None, op0=ALU.mult)
            self.ST(self.d_cs[2 + idx], tf.ap, R=[tf.b], W=[self.Db["cs"]])
        self.phase_end()
    def gelu_tanh(self, out_bf, ps_in, bias_col, n, tmps):
        x, x2, s = tmps
        self.ACT(x.ap[:, 0:n], ps_in, AF.Identity, [self._gb, self._gc], [x.b], bias=bias_col, scale=1.0)
        self.V("tensor_tensor", [x.b], [x2.b], out=x2.ap[:, 0:n], in0=x.ap[:, 0:n], in1=x.ap[:, 0:n], op=ALU.mult)
        self.V("tensor_scalar", [x2.b], [x2.b], out=x2.ap[:, 0:n], in0=x2.ap[:, 0:n], scalar1=0.044715, scalar2=1.0,
               op0=ALU.mult, op1=ALU.add)
        self.V("tensor_tensor", [x2.b, x.b], [x2.b], out=x2.ap[:, 0:n], in0=x2.ap[:, 0:n], in1=x.ap[:, 0:n], op=ALU.mult)
        self.ACT(s.ap[:, 0:n], x2.ap[:, 0:n], AF.Sigmoid, [x2.b], [s.b], scale=2.0 * math.sqrt(2.0 / math.pi))
        self.V("tensor_tensor", [x.b, s.b], [self._go], out=out_bf, in0=x.ap[:, 0:n], in1=s.ap[:, 0:n], op=ALU.mult)

    def ph_nsa(self, l):
        A, I, C = self.A, self.I, self.C
        kcmpT = A.alloc([64, 2, 256], BF16)
        vcmp = A.alloc([128, 2, 2, 64], BF16)
        self.G("memset", [], [kcmpT.b], ap=kcmpT.ap, constant=0.0)
        self.G("memset", [], [vcmp.b], ap=vcmp.ap, constant=0.0)
        onsa = A.alloc([128, NT, 512], F32)
        onsab = [Buf() for _ in range(NT)]
        m0 = A.mark()
        stg = A.alloc([64, 32 * 256], F32)
        w1s = A.alloc([64, 32, 256], BF16)
        w2f = A.alloc([128, 2, 64], F32)
        w2s = A.alloc([128, 2, 64], BF16)
        posn = A.alloc([32, 64], F32)
        posT = A.alloc([64, 32], BF16)
        b1 = A.alloc([128, 2], F32)
        c1 = A.alloc([128, 2], F32)
        srcT = [A.alloc([64, S], BF16) for _ in range(2)]
        hact = A.alloc([128, 2, 256], BF16)
        tmps = [A.alloc([128, 256], F32) for _ in range(3)]
        for kv_i, nm in enumerate(("k", "v")):
            sv = stg.ap.rearrange("p (l h) -> p l h", l=32)
            self.LD(sv, I[f"cmp_w1_{nm}"][l].rearrange("(l d) h -> d l h", d=64), R=[w1s.b], W=[stg.b])
            self.G("tensor_copy", [stg.b], [w1s.b], out=w1s.ap, in_=sv)
            self.LD(w2f.ap, I[f"cmp_w2_{nm}"][l].rearrange("(c p) n -> p c n", p=128), R=[w2s.b], W=[w2f.b])
            self.G("tensor_copy", [w2f.b], [w2s.b], out=w2s.ap, in_=w2f.ap)
            self.LD(posn.ap, I[f"cmp_pos_{nm}"][l], W=[posn.b])
            self.MM(self.psf(0)[0:64, 0:32], posn.ap, self.identf.ap[0:32, 0:32], True, True, [posn.b, self.identf.b], [self.pb[0]])
            self.V("tensor_copy", [self.pb[0]], [posT.b], out=posT.ap, in_=self.psf(0)[0:64, 0:32])
            for hc in range(2):
                self.P.dma("sp", b1.ap[:, hc:hc + 1], I[f"cmp_b1_{nm}"][l][hc * 128:(hc + 1) * 128].rearrange("(p o) -> p o", o=1),
                           R=[b1.b], W=[b1.b])
            for hc in range(2):
                for li in range(32):
                    self.MM(self.psf(1)[:, hc:hc + 1], w1s.ap[:, li, hc * 128:(hc + 1) * 128], posT.ap[:, li:li + 1],
                            li == 0, li == 31, [w1s.b, posT.b], [self.pb[1]])
            self.V("tensor_tensor", [self.pb[1], b1.b], [c1.b], out=c1.ap, in0=self.psf(1)[:, 0:2], in1=b1.ap, op=ALU.add)
            for g in range(2):
                st = srcT[g]
                self.LD(st.ap, self.d_kvT[kv_i, g], R=[self.Db["kvT"]], W=[st.b])
                s3 = st.ap.rearrange("p (c r) -> p c r", r=16)
                for hc in range(2):
                    bk = 2 + hc
                    for li in range(32):
                        a, r = li // 16, li % 16
                        self.MM(self.psf(bk)[:, 0:255], w1s.ap[:, li, hc * 128:(hc + 1) * 128], s3[:, a:a + 255, r],
                                li == 0, li == 31, [w1s.b, st.b], [self.pb[bk]])
                    self._gb, self._gc, self._go = self.pb[bk], c1.b, hact.b
                    self.gelu_tanh(hact.ap[:, hc, 0:255], self.psf(bk)[:, 0:255], c1.ap[:, hc:hc + 1], 255, tmps)
                if nm == "k":
                    for hc in range(2):
                        self.MM(self.psf(4)[0:64, 0:255], w2s.ap[:, hc, :], hact.ap[:, hc, 0:255], hc == 0, hc == 1,
                                [w2s.b, hact.b], [self.pb[4]])
                    self.V("tensor_copy", [self.pb[4]], [kcmpT.b], out=kcmpT.ap[:, g, 0:255], in_=self.psf(4)[0:64, 0:255])
                else:
                    for cb in range(2):
                        n = 128 if cb == 0 else 127
                        for hc in range(2):
                            self.MM(self.psf(5 + cb)[0:n, 0:64], hact.ap[:, hc, cb * 128:cb * 128 + n], w2s.ap[:, hc, :], hc == 0,
                                    hc == 1, [w2s.b, hact.b], [self.pb[5 + cb]])
                        self.V("tensor_copy", [self.pb[5 + cb]], [vcmp.b], out=vcmp.ap[0:n, g, cb, :], in_=self.psf(5 + cb)[0:n, 0:64])
        self.release(m0)
        gates = A.alloc([128, NT, 24], F32)
        self.LD(gates.ap, self.d_misc.rearrange("(t p) c -> p t c", p=128)[:, :, 0:24], R=[self.Db["misc"]], W=[gates.b])
        ebig = A.alloc([64, S], BF16)
        cover = A.alloc([128, 2, 64], BF16)
        emask = A.alloc([128, 128], BF16)
        cflag = A.alloc([128, 1], F32)
        self.LD(cflag.ap, C["cflag"][0:128, :], W=[cflag.b])
        m1 = A.mark()
        tmpf = A.alloc([64, S], F32)
        self.LD(tmpf.ap, C["ebig"], W=[tmpf.b])
        self.G("tensor_copy", [tmpf.b], [ebig.b], out=ebig.ap, in_=tmpf.ap)
        cvf = A.alloc([128, 2, 64], F32)
        self.LD(cvf.ap, C["cover"].rearrange("(c p) n -> p c n", p=128), W=[cvf.b])
        self.G("tensor_copy", [cvf.b], [cover.b], out=cover.ap, in_=cvf.ap)
        emf = A.alloc([128, 128], F32)
        self.LD(emf.ap, C["emask"], W=[emf.b])
        self.G("tensor_copy", [emf.b], [emask.b], out=emask.ap, in_=emf.ap)
        self.release(m1)
        qT = A.alloc([64, 4, S], BF16)
        kslc = A.alloc([64, S], BF16)
        kwin = A.alloc([64, S], BF16)
        vslc = A.alloc([128, NT, 65], BF16)
        vwin = A.alloc([128, NT, 65], BF16)
        strip = A.alloc([128, 4, 1024], BF16)
        bct = A.alloc([128, 4, 16], BF16)
        selbT = A.alloc([64, S], BF16)
        pTs = [A.alloc([128, 512], BF16) for _ in range(3)]
        pex = [A.alloc([128, 256], F32) for _ in range(2)]
        pcs = [A.alloc([128, 256], BF16) for _ in range(2)]
        pcT = [A.alloc([128, 2, 128], BF16) for _ in range(2)]
        sm = [A.alloc([128, 8], F32) for _ in range(4)]
        selA = [A.alloc([128, 64], F32) for _ in range(2)]
        selB = [A.alloc([128, 64], F32) for _ in range(2)]
        sc_t = [A.alloc([128, 64], F32) for _ in range(2)]
        s2_t = [A.alloc([128, 64], F32) for _ in range(2)]
        m8 = [A.alloc([128, 8], F32) for _ in range(2)]
        sbf = [A.alloc([128, 64], BF16) for _ in range(2)]
        self._smi = 0

        def small():
            t = sm[self._smi % 4]
            self._smi += 1
            return t
        vview = self.d_vtm.rearrange("(t p) a e -> p t a e", p=128)
        for g in range(2):
            for hh in range(4):
                self.LD(qT.ap[:, hh, :], self.d_qnT[g * 4 + hh], R=[self.Db["qnT"]], W=[qT.b])
            self.LD(kslc.ap, self.d_kvT[2, g], R=[self.Db["kvT"]], W=[kslc.b])
            self.LD(kwin.ap, self.d_kvT[3, g], R=[self.Db["kvT"]], W=[kwin.b])
            self.LD(vslc.ap, vview[:, :, g, :], R=[self.Db["vtm"]], W=[vslc.b])
            self.LD(vwin.ap, vview[:, :, 2 + g, :], R=[self.Db["vtm"]], W=[vwin.b])
            self.G("memset", [], [strip.b], ap=strip.ap[:, :, 0:384], constant=NEG)
            self.G("memset", [], [strip.b], ap=strip.ap[:, :, 640:1024], constant=0.0)
            for hh in range(4):
                h = g * 4 + hh
                for o in range(2):
                    src = AP(self.Dr["vs"], h * 128 * 384 + o * 128 + 127, [[383, 128], [1, 128]])
                    self.LD(strip.ap[:, hh, (3 + o) * 128:(4 + o) * 128], src, R=[self.Db["vs"]], W=[strip.b])
                self.LD(bct.ap[:, hh, :], self.d_bc[h].rearrange("(q c) -> q c", c=16), R=[self.Db["bc"]], W=[bct.b])
            for qt in range(NT):
                ncq = min(8 * qt + 8, 255)
                bs = 8 * qt - 8
                c_lo, c_hi = max(bs, 0), min(bs + 16, 255)
                nblk = (ncq + 127) // 128
                bimp = 6 + qt % 2
                sa, sb_ = selA[qt % 2], selB[qt % 2]
                self.LD(sa.ap, C["selA"][qt * 128:(qt + 1) * 128, :], W=[sa.b])
                self.LD(sb_.ap, C["selB"][qt * 128:(qt + 1) * 128, :], W=[sb_.b])
                for hh in range(4):
                    h = g * 4 + hh
                    i2 = (qt * 4 + hh) % 2
                    bsc, btr, bo = i2, 2 + i2, 4 + i2
                    sc = self.psf(bsc)
                    self.MM(sc[:, 0:ncq], qT.ap[:, hh, qt * 128:(qt + 1) * 128], kcmpT.ap[:, g, 0:ncq], True, False,
                            [qT.b, kcmpT.b], [self.pb[bsc]])
                    self.MM(sc[:, c_lo:c_hi], self.identb.ap, bct.ap[:, hh, c_lo - bs:c_hi - bs], False, True,
                            [self.identb.b, bct.b], [self.pb[bsc]])
                    mx = small()
                    self.V("reduce_max", [self.pb[bsc]], [mx.b], out=mx.ap[:, 0:1], in_=sc[:, 0:ncq], axis=AX.X)
                    self.V("tensor_scalar", [mx.b], [mx.b], out=mx.ap[:, 1:2], in0=mx.ap[:, 0:1], scalar1=-1.0, scalar2=None, op0=ALU.mult)
                    pe_, pc_, pt_ = pex[i2], pcs[i2], pcT[i2]
                    self.ACT(pe_.ap[:, 0:ncq], sc[:, 0:ncq], AF.Exp, [self.pb[bsc], mx.b], [pe_.b, mx.b], bias=mx.ap[:, 1:2], scale=1.0,
                             accum_out=mx.ap[:, 2:3])
                    self.V("reciprocal", [mx.b], [mx.b], out=mx.ap[:, 3:4], in_=mx.ap[:, 2:3])
                    if qt == 0:
                        self.V("tensor_tensor", [mx.b, cflag.b], [mx.b], out=mx.ap[:, 3:4], in0=mx.ap[:, 3:4], in1=cflag.ap, op=ALU.mult)
                    self.G("memset", [], [pc_.b], ap=pc_.ap, constant=0.0)
                    self.V("tensor_scalar", [pe_.b, mx.b], [pc_.b], out=pc_.ap[:, 0:ncq], in0=pe_.ap[:, 0:ncq], scalar1=mx.ap[:, 3:4],
                           scalar2=None, op0=ALU.mult)
                    ptb = self.psb(btr)[:, 0:256].rearrange("p (j n) -> p j n", j=2)
                    for j in range(nblk):
                        self.TR(ptb[:, j, :], pc_.ap[:, j * 128:(j + 1) * 128], [pc_.b], [self.pb[btr]])
                    self.ACT(pt_.ap[:, 0:nblk, :], ptb[:, 0:nblk, :], AF.Copy, [self.pb[btr]], [pt_.b])
                    for j in range(nblk):
                        self.MM(self.psf(bo)[:, 0:64], pt_.ap[:, j, :], vcmp.ap[:, g, j, :], j == 0, j == nblk - 1,
                                [pt_.b, vcmp.b], [self.pb[bo]])
                    for j in range(nblk):
                        self.MM(self.psf(bimp)[:, 0:64], pt_.ap[:, j, :], cover.ap[:, j, :], hh == 0 and j == 0,
                                hh == 3 and j == nblk - 1, [pt_.b, cover.b], [self.pb[bimp]])
                    self.ACT(onsa.ap[:, qt, h * 64:(h + 1) * 64], self.psf(bo)[:, 0:64], AF.Copy, [self.pb[bo], gates.b], [onsab[qt]],
                             scale=gates.ap[:, qt, h * 3:h * 3 + 1])
                sct, s2, m8a, sbb = sc_t[qt % 2], s2_t[qt % 2], m8[qt % 2], sbf[qt % 2]
                t = small()
                self.V("tensor_tensor", [self.pb[bimp], sa.b], [sct.b], out=sct.ap, in0=self.psf(bimp)[:, 0:64], in1=sa.ap, op=ALU.mult)
                self.V("tensor_tensor", [sct.b, sb_.b], [sct.b], out=sct.ap, in0=sct.ap, in1=sb_.ap, op=ALU.add)
                self.V("max", [sct.b], [m8a.b], out=m8a.ap, in_=sct.ap)
                self.V("tensor_reduce", [m8a.b], [t.b], out=t.ap[:, 0:1], in_=m8a.ap, axis=AX.X, op=ALU.min)
                self.V("tensor_scalar", [sct.b, t.b], [s2.b], out=s2.ap, in0=sct.ap, scalar1=t.ap[:, 0:1], scalar2=-1e9,
                       op0=ALU.is_ge, op1=ALU.mult)
                self.V("tensor_tensor", [s2.b, sct.b], [s2.b], out=s2.ap, in0=s2.ap, in1=sct.ap, op=ALU.add)
                self.V("max", [s2.b], [m8a.b], out=m8a.ap, in_=s2.ap)
                self.V("tensor_reduce", [m8a.b], [t.b], out=t.ap[:, 1:2], in_=m8a.ap, axis=AX.X, op=ALU.min)
                self.V("tensor_scalar", [t.b], [t.b], out=t.ap[:, 2:3], in0=t.ap[:, 1:2], scalar1=0.0, scalar2=None, op0=ALU.max)
                self.V("tensor_scalar", [sct.b, t.b], [s2.b], out=s2.ap, in0=sct.ap, scalar1=t.ap[:, 2:3], scalar2=None, op0=ALU.is_ge)
                self.V("tensor_scalar", [s2.b], [sbb.b], out=sbb.ap, in0=s2.ap, scalar1=-1.0, scalar2=-NEG, op0=ALU.add, op1=ALU.mult)
                btr = 2 + qt % 2
                self.TR(self.psb(btr)[0:64, 0:128], sbb.ap, [sbb.b], [self.pb[btr]])
                self.V("tensor_copy", [self.pb[btr]], [selbT.b], out=selbT.ap[:, qt * 128:(qt + 1) * 128], in_=self.psb(btr)[0:64, 0:128])
            for hh in range(4):
                h = g * 4 + hh
                for qc in range(NQC):
                    bo = 4 + qc % 2
                    O = self.psf(bo)[:, 0:260].rearrange("p (j e) -> p j e", j=4)
                    nk = 4 * qc + 4
                    for kt in range(nk):
                        bsc = kt % 3
                        sc = self.psf(bsc)
                        near = kt >= 4 * qc - 1
                        self.MM(sc, kslc.ap[:, kt * 128:(kt + 1) * 128], qT.ap[:, hh, qc * 512:(qc + 1) * 512], True, False,
                                [kslc.b, qT.b], [self.pb[bsc]])
                        self.MM(sc, ebig.ap[:, kt * 128:(kt + 1) * 128], selbT.ap[:, qc * 512:(qc + 1) * 512], False, not near,
                                [ebig.b, selbT.b], [self.pb[bsc]])
                        if near:
                            o = kt - 4 * qc
                            self.MM(sc, self.identb.ap, strip.ap[:, hh, (3 - o) * 128:(3 - o) * 128 + 512], False, True,
                                    [self.identb.b, strip.b], [self.pb[bsc]])
                        pT = pTs[kt % 3]
                        self.ACT(pT.ap, sc, AF.Exp, [self.pb[bsc]], [pT.b])
                        for j in range(4):
                            qt = 4 * qc + j
                            if kt > qt:
                                continue
                            self.MM(O[:, j, :], pT.ap[:, j * 128:(j + 1) * 128], vslc.ap[:, kt, :], kt == 0 and j == 0, kt == qt,
                                    [pT.b, vslc.b], [self.pb[bo]])
                    for j in range(4):
                        qt = 4 * qc + j
                        t = small()
                        self.V("reciprocal", [self.pb[bo]], [t.b], out=t.ap[:, 0:1], in_=O[:, j, 64:65])
                        self.V("tensor_tensor", [t.b, gates.b], [t.b], out=t.ap[:, 1:2], in0=t.ap[:, 0:1],
                               in1=gates.ap[:, qt, h * 3 + 1:h * 3 + 2], op=ALU.mult)
                        osl = onsa.ap[:, qt, h * 64:(h + 1) * 64]
                        self.V("scalar_tensor_tensor", [self.pb[bo], t.b, onsab[qt]], [onsab[qt]], out=osl, in0=O[:, j, 0:64],
                               scalar=t.ap[:, 1:2], in1=osl, op0=ALU.mult, op1=ALU.add)
                for qt in range(NT):
                    bo = 6 + qt % 2
                    O = self.psf(bo)[:, 0:65]
                    main = [kt for kt in range(qt - 3, qt + 1) if kt >= 0]
                    far = qt - 4
                    nmm = len(main) + (1 if far >= 0 else 0)
                    done = 0
                    if far >= 0:
                        bsc = 3
                        sc = self.psf(bsc)[:, 0:128]
                        self.MM(sc, kwin.ap[:, far * 128:(far + 1) * 128], qT.ap[:, hh, qt * 128:(qt + 1) * 128], True, False,
                                [kwin.b, qT.b], [self.pb[bsc]])
                        self.MM(sc, self.identb.ap, emask.ap, False, True, [self.identb.b, emask.b], [self.pb[bsc]])
                        pT = pTs[2]
                        self.ACT(pT.ap[:, 0:128], sc, AF.Exp, [self.pb[bsc]], [pT.b])
                        self.MM(O, pT.ap[:, 0:128], vwin.ap[:, far, :], True, False, [pT.b, vwin.b], [self.pb[bo]])
                        done = 1
                    bsc = qt % 3
                    sc = self.psf(bsc)
                    for i, kt in enumerate(main):
                        nb_ = kt >= qt - 1
                        self.MM(sc[:, i * 128:(i + 1) * 128], kwin.ap[:, kt * 128:(kt + 1) * 128], qT.ap[:, hh, qt * 128:(qt + 1) * 128],
                                True, not nb_, [kwin.b, qT.b], [self.pb[bsc]])
                        if nb_:
                            o = qt - kt
                            self.MM(sc[:, i * 128:(i + 1) * 128], self.identb.ap, strip.ap[:, hh, (3 + o) * 128:(4 + o) * 128], False, True,
                                    [self.identb.b, strip.b], [self.pb[bsc]])
                    pT = pTs[qt % 2]
                    nm_ = len(main) * 128
                    self.ACT(pT.ap[:, 0:nm_], sc[:, 0:nm_], AF.Exp, [self.pb[bsc]], [pT.b])
                    for i, kt in enumerate(main):
                        self.MM(O, pT.ap[:, i * 128:(i + 1) * 128], vwin.ap[:, kt, :], done == 0, done == nmm - 1,
                                [pT.b, vwin.b], [self.pb[bo]])
                        done += 1
                    t = small()
                    self.V("reciprocal", [self.pb[bo]], [t.b], out=t.ap[:, 0:1], in_=O[:, 64:65])
                    self.V("tensor_tensor", [t.b, gates.b], [t.b], out=t.ap[:, 1:2], in0=t.ap[:, 0:1],
                           in1=gates.ap[:, qt, h * 3 + 2:h * 3 + 3], op=ALU.mult)
                    osl = onsa.ap[:, qt, h * 64:(h + 1) * 64]
                    self.V("scalar_tensor_tensor", [self.pb[bo], t.b, onsab[qt]], [onsab[qt]], out=osl, in0=O[:, 0:64],
                           scalar=t.ap[:, 1:2], in1=osl, op0=ALU.mult, op1=ALU.add)
        self.release(m0)
        oT = A.alloc([128, 4, S], BF16)
        ob = [A.alloc([128, 512], BF16) for _ in range(2)]
        for qt in range(NT):
            o_ = ob[qt % 2]
            self.V("tensor_copy", [onsab[qt]], [o_.b], out=o_.ap, in_=onsa.ap[:, qt, :])
            bk = qt % 2
            pt = self.psb(bk)[:, 0:512].rearrange("p (k n) -> p k n", k=4)
            for k in range(4):
                self.TR(pt[:, k, :], o_.ap[:, k * 128:(k + 1) * 128], [o_.b], [self.pb[bk]])
            self.ACT(oT.ap[:, :, qt * 128:(qt + 1) * 128], pt, AF.Copy, [self.pb[bk]], [oT.b])
        for k in range(4):
            self.ST(self.d_onsaT[k * 128:(k + 1) * 128, :], oT.ap[:, k, :], R=[oT.b], W=[self.Db["onsaT"]])
        if "onsa_dbg" in self.debug:
            self.ST(self.d_onsa_dbg.rearrange("(t p) c -> p t c", p=128), onsa.ap, R=onsab, W=[Buf()])
        self.phase_end()
    def ph_mla(self, l):
        A, I, C = self.A, self.I, self.C
        qs = 192.0 ** -0.5
        cqT = A.alloc([128, 3, S], BF16)
        ckvT = A.alloc([128, 2, S], BF16)
        self.cur_x_b = self.Db["misc"]
        self.norm_to_hT(self.d_misc[:, 24:408], I["mla_norm_q"][l], cqT, width=384)
        self.norm_to_hT(self.d_misc[:, 408:664], I["mla_norm_kv"][l], ckvT, width=256)
        tt = [A.alloc([32, 512], F32) for _ in range(4)]
        cs = A.alloc([32, 2, S], F32)
        rst = [A.alloc([32, 2, S], BF16) for _ in range(2)]

        def rope(x1, x2, R1, R2, co, si, o1, o2, ob, n):
            t1, t2, t3, t4 = tt
            self.V("tensor_tensor", R1 + [cs.b], [t1.b], out=t1.ap[:, 0:n], in0=x1, in1=co, op=ALU.mult)
            self.V("tensor_tensor", R2 + [cs.b], [t2.b], out=t2.ap[:, 0:n], in0=x2, in1=si, op=ALU.mult)
            self.V("tensor_tensor", [t1.b, t2.b], [ob], out=o1, in0=t1.ap[:, 0:n], in1=t2.ap[:, 0:n], op=ALU.subtract)
            self.V("tensor_tensor", R2 + [cs.b], [t3.b], out=t3.ap[:, 0:n], in0=x2, in1=co, op=ALU.mult)
            self.V("tensor_tensor", R1 + [cs.b], [t4.b], out=t4.ap[:, 0:n], in0=x1, in1=si, op=ALU.mult)
            self.V("tensor_tensor", [t3.b, t4.b], [ob], out=o2, in0=t3.ap[:, 0:n], in1=t4.ap[:, 0:n], op=ALU.add)

        m0 = A.mark()
        for i in range(2):
            self.LD(cs.ap[:, i, :], self.d_cs[i], R=[self.Db["cs"]], W=[cs.b])
        kr = A.alloc([32, 2, S], F32)
        for hf in range(2):
            self.LD(kr.ap[:, hf, :], self.d_krT[hf], R=[self.Db["krT"]], W=[kr.b])
        st = rst[0]
        for tc in range(NQC):
            sl = slice(tc * 512, (tc + 1) * 512)
            rope(kr.ap[:, 0, sl], kr.ap[:, 1, sl], [kr.b], [kr.b], cs.ap[:, 0, sl], cs.ap[:, 1, sl], st.ap[:, 0, sl], st.ap[:, 1, sl],
                 st.b, 512)
        self.ST(self.d_kpe[0], st.ap[:, 0, :], R=[st.b], W=[self.Db["kpe"]])
        self.ST(self.d_kpe[1], st.ap[:, 1, :], R=[st.b], W=[self.Db["kpe"]])
        self.release(m0)
        for i in range(2):
            self.LD(cs.ap[:, i, :], self.d_cs[2 + i], R=[self.Db["cs"]], W=[cs.b])
        stgw = A.alloc([128, 3 * 768], F32)
        wq = A.alloc([128, 3, 768], BF16)
        self.load_weight_bf16(wq, I["w_uq"][l], stgw, 3, 768)
        stgk = A.alloc([128, 2 * 1024], F32)
        wkv = A.alloc([128, 2, 1024], BF16)
        self.load_weight_bf16(wkv, I["w_ukv"][l], stgk, 2, 1024)
        stg = [A.alloc([128, S], BF16) for _ in range(2)]
        self._s = 0

        def nstg():
            s = stg[self._s % 2]
            self._s += 1
            return s
        self._pbk = 0

        def nb():
            b = self._pbk % 4
            self._pbk += 1
            return b

        for h in range(4):
            st = nstg()
            for tc in range(NQC):
                bk = nb()
                for k in range(3):
                    self.MM(self.psf(bk), wq.ap[:, k, h * 192:h * 192 + 128], cqT.ap[:, k, tc * 512:(tc + 1) * 512], k == 0, k == 2,
                            [wq.b, cqT.b], [self.pb[bk]])
                self.ACT(st.ap[:, tc * 512:(tc + 1) * 512], self.psf(bk), AF.Copy, [self.pb[bk]], [st.b], scale=qs)
            self.ST(self.d_qn[h], st.ap, R=[st.b], W=[self.Db["qn"]])
            st = rst[h % 2]
            for tc in range(NQC):
                b1, b2 = nb(), nb()
                sl = slice(tc * 512, (tc + 1) * 512)
                for hf, bk in ((0, b1), (1, b2)):
                    c0 = h * 192 + 128 + hf * 32
                    for k in range(3):
                        self.MM(self.psf(bk)[0:32, :], wq.ap[:, k, c0:c0 + 32], cqT.ap[:, k, sl], k == 0, k == 2,
                                [wq.b, cqT.b], [self.pb[bk]])
                rope(self.psf(b1)[0:32, :], self.psf(b2)[0:32, :], [self.pb[b1]], [self.pb[b2]], cs.ap[:, 0, sl], cs.ap[:, 1, sl],
                     st.ap[:, 0, sl], st.ap[:, 1, sl], st.b, 512)
            self.ST(self.d_qpe[h, 0], st.ap[:, 0, :], R=[st.b], W=[self.Db["qpe"]])
            self.ST(self.d_qpe[h, 1], st.ap[:, 1, :], R=[st.b], W=[self.Db["qpe"]])
            st = nstg()
            for tc in range(NQC):
                bk = nb()
                for k in range(2):
                    self.MM(self.psf(bk), wkv.ap[:, k, h * 256:h * 256 + 128], ckvT.ap[:, k, tc * 512:(tc + 1) * 512], k == 0, k == 1,
                            [wkv.b, ckvT.b], [self.pb[bk]])
                self.V("tensor_copy", [self.pb[bk]], [st.b], out=st.ap[:, tc * 512:(tc + 1) * 512], in_=self.psf(bk))
            self.ST(self.d_kn[h], st.ap, R=[st.b], W=[self.Db["kn"]])
        wv = A.alloc([128, 2, 512], BF16)
        for h in range(4):
            self.G("tensor_copy", [wkv.b], [wv.b], out=wv.ap[:, :, h * 128:(h + 1) * 128], in_=wkv.ap[:, :, h * 256 + 128:h * 256 + 256])
        vst = [A.alloc([128, 4, 129], BF16) for _ in range(2)]
        for v_ in vst:
            self.V("memset", [], [v_.b], ap=v_.ap, constant=1.0)
        for t in range(NT):
            bk = 4 + t % 2
            vs = vst[t % 2]
            for k in range(2):
                self.MM(self.psf(bk), ckvT.ap[:, k, t * 128:(t + 1) * 128], wv.ap[:, k, :], k == 0, k == 1, [ckvT.b, wv.b], [self.pb[bk]])
            self.V("tensor_copy", [self.pb[bk]], [vs.b], out=vs.ap[:, :, 0:128], in_=self.psf(bk).rearrange("p (h d) -> p h d", h=4))
            self.ST(self.d_vmla[t * 128:(t + 1) * 128], vs.ap, R=[vs.b], W=[self.Db["vmla"]])
        self.phase_end()
        cstrip = A.alloc([128, 896], BF16)
        cmf = A.alloc([128, 128], F32)
        self.LD(cmf.ap, C["cmask"], W=[cmf.b])
        self.G("memset", [], [cstrip.b], ap=cstrip.ap[:, 0:384], constant=NEG)
        self.G("memset", [], [cstrip.b], ap=cstrip.ap[:, 512:896], constant=0.0)
        self.G("tensor_copy", [cmf.b], [cstrip.b], out=cstrip.ap[:, 384:512], in_=cmf.ap)
        kpe = A.alloc([32, 2, S], BF16)
        for hf in range(2):
            self.LD(kpe.ap[:, hf, :], self.d_kpe[hf], R=[self.Db["kpe"]], W=[kpe.b])
        qn = A.alloc([128, S], BF16)
        kn = A.alloc([128, S], BF16)
        qpe = A.alloc([32, 2, S], BF16)
        vv = A.alloc([128, NT, 129], BF16)
        oT = A.alloc([128, S], BF16)
        pTs = [A.alloc([128, 512], BF16) for _ in range(3)]
        ob = [A.alloc([128, 128], BF16) for _ in range(2)]
        sm = [A.alloc([128, 4], F32) for _ in range(4)]
        vview = self.d_vmla.rearrange("(t p) h e -> p t h e", p=128)
        smi = 0
        for h in range(4):
            self.LD(qn.ap, self.d_qn[h], R=[self.Db["qn"]], W=[qn.b])
            self.LD(kn.ap, self.d_kn[h], R=[self.Db["kn"]], W=[kn.b])
            for hf in range(2):
                self.LD(qpe.ap[:, hf, :], self.d_qpe[h, hf], R=[self.Db["qpe"]], W=[qpe.b])
            self.LD(vv.ap, vview[:, :, h, :], R=[self.Db["vmla"]], W=[vv.b])
            for qc in range(NQC):
                ba = 4 + 2 * (qc % 2)
                Oj = [self.psf(ba + j // 2)[:, (j % 2) * 129:(j % 2) * 129 + 129] for j in range(4)]
                Ob = [self.pb[ba + j // 2] for j in range(4)]
                qsl = slice(qc * 512, (qc + 1) * 512)
                for kt in range(4 * qc + 4):
                    bsc = kt % 3
                    sc = self.psf(bsc)
                    ksl = slice(kt * 128, (kt + 1) * 128)
                    diag = kt >= 4 * qc
                    self.MM(sc, kn.ap[:, ksl], qn.ap[:, qsl], True, False, [kn.b, qn.b], [self.pb[bsc]])
                    self.MM(sc, kpe.ap[:, 0, ksl], qpe.ap[:, 0, qsl], False, False, [kpe.b, qpe.b], [self.pb[bsc]])
                    self.MM(sc, kpe.ap[:, 1, ksl], qpe.ap[:, 1, qsl], False, not diag, [kpe.b, qpe.b], [self.pb[bsc]])
                    if diag:
                        o = kt - 4 * qc
                        self.MM(sc, self.identb.ap, cstrip.ap[:, (3 - o) * 128:(3 - o) * 128 + 512], False, True,
                                [self.identb.b, cstrip.b], [self.pb[bsc]])
                    pT = pTs[kt % 3]
                    self.ACT(pT.ap, sc, AF.Exp, [self.pb[bsc]], [pT.b])
                    for j in range(4):
                        qt = 4 * qc + j
                        if kt > qt:
                            continue
                        self.MM(Oj[j], pT.ap[:, j * 128:(j + 1) * 128], vv.ap[:, kt, :], kt == 0 and j % 2 == 0, kt == qt, [pT.b, vv.b], [Ob[j]])
                for j in range(4):
                    qt = 4 * qc + j
                    t = sm[smi % 4]
                    smi += 1
                    o_ = ob[qt % 2]
                    self.V("reciprocal", [Ob[j]], [t.b], out=t.ap[:, 0:1], in_=Oj[j][:, 128:129])
                    self.V("tensor_scalar", [Ob[j], t.b], [o_.b], out=o_.ap, in0=Oj[j][:, 0:128], scalar1=t.ap[:, 0:1], scalar2=None,
                           op0=ALU.mult)
                    bt = 3
                    self.TR(self.psb(bt)[:, 0:128], o_.ap, [o_.b], [self.pb[bt]])
                    self.ACT(oT.ap[:, qt * 128:(qt + 1) * 128], self.psb(bt)[:, 0:128], AF.Copy, [self.pb[bt]], [oT.b])
            self.ST(self.d_omlaT[h * 128:(h + 1) * 128, :], oT.ap, R=[oT.b], W=[self.Db["omlaT"]])
        self.phase_end()

    def ph_merge(self, l, xsrc, xsrc_b):
        A, I = self.A, self.I
        stg = A.alloc([128, 4096], F32)
        wb = {}
        for nm in ("w_branch_conv", "w_branch_nsa", "w_branch_mla"):
            wb[nm] = A.alloc([128, 4, 1024], BF16)
            self.load_weight_bf16(wb[nm], I[nm][l], stg, 4, 1024)
        wo = A.alloc([128, 8, 1024], BF16)
        for hf in range(2):
            sv = stg.ap.rearrange("p (k n) -> p k n", k=4)
            self.LD(sv, I["w_out"][l][hf * 512:(hf + 1) * 512, :].rearrange("(k p) n -> p k n", p=128), R=[wo.b], W=[stg.b])
            self.G("tensor_copy", [stg.b], [wo.b], out=wo.ap[:, hf * 4:(hf + 1) * 4, :], in_=sv)
        srcs = [("w_branch_conv", self.d_uactT, "uactT"), ("w_branch_nsa", self.d_onsaT, "onsaT"), ("w_branch_mla", self.d_omlaT, "omlaT")]
        acts = [[A.alloc([128, 4, 512], BF16) for _ in range(2)] for _ in range(3)]
        gms = [A.alloc([128, 24, 512], BF16) for _ in range(2)]
        mT = [A.alloc([128, 8, 512], BF16) for _ in range(2)]
        ta = [A.alloc([128, 512], F32) for _ in range(2)]
        tb = [A.alloc([128, 512], F32) for _ in range(2)]
        xts = [A.alloc([128, 1024], F32) for _ in range(2)]
        xos = [A.alloc([128, 1024], F32) for _ in range(2)]
        gv = self.d_gmT.rearrange("(b p) s -> p b s", p=128)
        n = 0
        for tc in range(NQC):
            sl = slice(tc * 512, (tc + 1) * 512)
            i2 = tc % 2
            for si, (wn, dsrc, dn) in enumerate(srcs):
                self.LD(acts[si][i2].ap, dsrc.rearrange("(k p) s -> p k s", p=128)[:, :, sl], R=[self.Db[dn]], W=[acts[si][i2].b])
            gm = gms[i2]
            self.LD(gm.ap, gv[:, :, sl], R=[self.Db["gmT"]], W=[gm.b])
            m_ = mT[i2]
            for fc in range(8):
                a_, b_ = ta[fc % 2], tb[fc % 2]
                for si, (wn, dsrc, dn) in enumerate(srcs):
                    bk = n % 4
                    n += 1
                    for k in range(4):
                        self.MM(self.psf(bk), wb[wn].ap[:, k, fc * 128:(fc + 1) * 128], acts[si][i2].ap[:, k, :], k == 0, k == 3,
                                [wb[wn].b, acts[si][i2].b], [self.pb[bk]])
                    dst = a_ if si == 0 else b_
                    self.V("tensor_tensor", [self.pb[bk], gm.b], [dst.b], out=dst.ap, in0=self.psf(bk), in1=gm.ap[:, si * 8 + fc, :], op=ALU.mult)
                    if si == 1:
                        self.G("tensor_tensor", [a_.b, b_.b], [a_.b], out=a_.ap, in0=a_.ap, in1=b_.ap, op=ALU.add)
                    if si == 2:
                        self.G("tensor_tensor", [a_.b, b_.b], [m_.b], out=m_.ap[:, fc, :], in0=a_.ap, in1=b_.ap, op=ALU.add)
            for tt_ in range(4):
                t = tc * 4 + tt_
                xt, xo = xts[t % 2], xos[t % 2]
                self.LD(xt.ap, xsrc[t * 128:(t + 1) * 128, :], R=[xsrc_b], W=[xt.b])
                for cg in range(2):
                    bk = 4 + (t * 2 + cg) % 4
                    for k in range(8):
                        self.MM(self.psf(bk), m_.ap[:, k, tt_ * 128:(tt_ + 1) * 128], wo.ap[:, k, cg * 512:(cg + 1) * 512], k == 0, k == 7,
                                [m_.b, wo.b], [self.pb[bk]])
                    self.V("tensor_tensor", [self.pb[bk], xt.b], [xo.b], out=xo.ap[:, cg * 512:(cg + 1) * 512], in0=self.psf(bk),
                           in1=xt.ap[:, cg * 512:(cg + 1) * 512], op=ALU.add)
                self.ST(self.d_xres[t * 128:(t + 1) * 128, :], xo.ap, R=[xo.b], W=[self.Db["xres"]])
        self.phase_end()

    def ph_xattn(self, l):
        A, I = self.A, self.I
        xs = 128.0 ** -0.5
        hT = A.alloc([128, 8, S], BF16)
        self.cur_x_b = self.Db["xres"]
        self.norm_to_hT(self.d_xres, I["norm_xattn"][l], hT)
        memT = A.alloc([128, 8, 256], BF16)
        self.cur_x_b = Buf()
        self.norm_to_hT(I["mem"], I["norm_mem"][l], memT, ntile=2, pbank=2)
        stg = A.alloc([128, 8 * 512], F32)
        wq = A.alloc([128, 8, 512], BF16)
        self.load_weight_bf16(wq, I["w_xq"][l], stg, 8, 512)
        wkv = A.alloc([128, 8, 1024], BF16)
        for hf in range(2):
            sv = stg.ap.rearrange("p (k n) -> p k n", k=8)
            self.LD(sv, I["w_xkv"][l][:, hf * 512:(hf + 1) * 512].rearrange("(k p) n -> p k n", p=128), R=[wkv.b], W=[stg.b])
            self.G("tensor_copy", [stg.b], [wkv.b], out=wkv.ap[:, :, hf * 512:(hf + 1) * 512], in_=sv)
        wo = A.alloc([128, 4, 1024], BF16)
        self.load_weight_bf16(wo, I["w_xo"][l], stg, 4, 1024)
        kT = A.alloc([128, 4, 256], BF16)
        vv = A.alloc([128, 2, 4, 129], BF16)
        self.V("memset", [], [vv.b], ap=vv.ap, constant=1.0)
        for h in range(4):
            for k in range(8):
                self.MM(self.psf(0)[:, 0:256], wkv.ap[:, k, h * 128:(h + 1) * 128], memT.ap[:, k, :], k == 0, k == 7, [wkv.b, memT.b], [self.pb[0]])
            self.V("tensor_copy", [self.pb[0]], [kT.b], out=kT.ap[:, h, :], in_=self.psf(0)[:, 0:256])
        for mt in range(2):
            for k in range(8):
                self.MM(self.psf(1), memT.ap[:, k, mt * 128:(mt + 1) * 128], wkv.ap[:, k, 512:1024], k == 0, k == 7, [wkv.b, memT.b], [self.pb[1]])
            self.V("tensor_copy", [self.pb[1]], [vv.b], out=vv.ap[:, mt, :, 0:128], in_=self.psf(1).rearrange("p (h d) -> p h d", h=4))
        qTs = [A.alloc([128, 512], BF16) for _ in range(2)]
        pTs = [A.alloc([128, 2, 512], BF16) for _ in range(2)]
        self._ox = [A.alloc([128, 512], BF16) for _ in range(4)]
        oxT = [A.alloc([128, 4, 128], BF16) for _ in range(2)]
        sm = [A.alloc([128, 4], F32) for _ in range(4)]
        xts = [A.alloc([128, 1024], F32) for _ in range(2)]
        xos = [A.alloc([128, 1024], F32) for _ in range(2)]
        smi = 0
        n = 0
        for tc in range(NQC):
            sl = slice(tc * 512, (tc + 1) * 512)
            for h in range(4):
                qT = qTs[h % 2]
                for k in range(8):
                    self.MM(self.psf(0), wq.ap[:, k, h * 128:(h + 1) * 128], hT.ap[:, k, sl], k == 0, k == 7, [wq.b, hT.b], [self.pb[0]])
                self.ACT(qT.ap, self.psf(0), AF.Copy, [self.pb[0]], [qT.b], scale=xs)
                pT = pTs[h % 2]
                for mt in range(2):
                    bsc = 1 + mt
                    self.MM(self.psf(bsc), kT.ap[:, h, mt * 128:(mt + 1) * 128], qT.ap, True, True, [kT.b, qT.b], [self.pb[bsc]])
                    self.ACT(pT.ap[:, mt, :], self.psf(bsc), AF.Exp, [self.pb[bsc]], [pT.b])
                for j in range(4):
                    bk = 4 + (h % 2) * 2 + j // 2
                    Oj = self.psf(bk)[:, (j % 2) * 129:(j % 2) * 129 + 129]
                    for mt in range(2):
                        self.MM(Oj, pT.ap[:, mt, j * 128:(j + 1) * 128], vv.ap[:, mt, h, :], mt == 0 and j % 2 == 0, mt == 1, [pT.b, vv.b], [self.pb[bk]])
                    t = sm[smi % 4]
                    smi += 1
                    self.V("reciprocal", [self.pb[bk]], [t.b], out=t.ap[:, 0:1], in_=Oj[:, 128:129])
                    ox = self._ox[j]
                    self.V("tensor_scalar", [self.pb[bk], t.b], [ox.b], out=ox.ap[:, h * 128:(h + 1) * 128], in0=Oj[:, 0:128],
                           scalar1=t.ap[:, 0:1], scalar2=None, op0=ALU.mult)
            for j in range(4):
                t = tc * 4 + j
                ox = self._ox[j]
                oT_ = oxT[t % 2]
                bt = 3
                pt = self.psb(bt)[:, 0:512].rearrange("p (k n) -> p k n", k=4)
                for k in range(4):
                    self.TR(pt[:, k, :], ox.ap[:, k * 128:(k + 1) * 128], [ox.b], [self.pb[bt]])
                self.ACT(oT_.ap, pt, AF.Copy, [self.pb[bt]], [oT_.b])
                xt, xo = xts[t % 2], xos[t % 2]
                self.LD(xt.ap, self.d_xres[t * 128:(t + 1) * 128, :], R=[self.Db["xres"]], W=[xt.b])
                for cg in range(2):
                    bk = 6 + cg
                    for k in range(4):
                        self.MM(self.psf(bk), oT_.ap[:, k, :], wo.ap[:, k, cg * 512:(cg + 1) * 512], k == 0, k == 3, [oT_.b, wo.b], [self.pb[bk]])
                    self.V("tensor_tensor", [self.pb[bk], xt.b], [xo.b], out=xo.ap[:, cg * 512:(cg + 1) * 512], in0=self.psf(bk),
                           in1=xt.ap[:, cg * 512:(cg + 1) * 512], op=ALU.add)
                self.ST(self.d_xres[t * 128:(t + 1) * 128, :], xo.ap, R=[xo.b], W=[self.Db["xres"]])
        self.phase_end()

    def ph_ffn(self, l):
        A, I = self.A, self.I
        hT = A.alloc([128, 8, S], BF16)
        self.cur_x_b = self.Db["xres"]
        self.norm_to_hT(self.d_xres, I["norm_ffn"][l], hT)
        wg = A.alloc([128, 8, 1408], BF16)
        wu = A.alloc([128, 8, 1408], BF16)
        wd = A.alloc([128, 11, 1024], BF16)
        stg = [A.alloc([128, 3072], F32) for _ in range(2)]
        actT = [A.alloc([128, 11, 512], BF16) for _ in range(1)]
        sg = [A.alloc([128, 512], F32) for _ in range(2)]
        xts = [A.alloc([128, 1024], F32) for _ in range(2)]
        xos = [A.alloc([128, 1024], F32) for _ in range(2)]
        w_gu = I["w_gate_up"][l]
        w_dn = I["w_down"][l]
        for ps_ in range(2):
            f0 = ps_ * 1408
            u = 0
            for dst, base in ((wg, 0), (wu, FFN)):
                for q4 in range(4):
                    s_ = stg[u % 2]
                    u += 1
                    sv = s_.ap[:, 0:2816].rearrange("p (k n) -> p k n", k=8)
                    c0 = base + f0 + q4 * 352
                    self.LD(sv, w_gu[:, c0:c0 + 352].rearrange("(k p) n -> p k n", p=128), R=[dst.b], W=[s_.b])
                    self.G("tensor_copy", [s_.b], [dst.b], out=dst.ap[:, :, q4 * 352:(q4 + 1) * 352], in_=sv)
            for q4 in range(4):
                s_ = stg[u % 2]
                u += 1
                nk = 3 if q4 < 3 else 2
                sv = s_.ap[:, 0:nk * 1024].rearrange("p (k n) -> p k n", k=nk)
                r0 = f0 + q4 * 384
                self.LD(sv, w_dn[r0:r0 + nk * 128, :].rearrange("(k p) n -> p k n", p=128), R=[wd.b], W=[s_.b])
                self.G("tensor_copy", [s_.b], [wd.b], out=wd.ap[:, q4 * 3:q4 * 3 + nk, :], in_=sv)
            n = 0
            for tc in range(NQC):
                sl = slice(tc * 512, (tc + 1) * 512)
                aT = actT[0]
                for f in range(11):
                    bg, bu = (n % 2) * 2, (n % 2) * 2 + 1
                    n += 1
                    for k in range(8):
                        self.MM(self.psf(bg), wg.ap[:, k, f * 128:(f + 1) * 128], hT.ap[:, k, sl], k == 0, k == 7, [wg.b, hT.b], [self.pb[bg]])
                    for k in range(8):
                        self.MM(self.psf(bu), wu.ap[:, k, f * 128:(f + 1) * 128], hT.ap[:, k, sl], k == 0, k == 7, [wu.b, hT.b], [self.pb[bu]])
                    s = sg[f % 2]
                    self.ACT(s.ap, self.psf(bg), AF.Silu, [self.pb[bg]], [s.b])
                    self.V("tensor_tensor", [self.pb[bu], s.b], [aT.b], out=aT.ap[:, f, :], in0=self.psf(bu), in1=s.ap, op=ALU.mult)
                for tt_ in range(4):
                    t = tc * 4 + tt_
                    xt, xo = xts[t % 2], xos[t % 2]
                    self.LD(xt.ap, self.d_xres[t * 128:(t + 1) * 128, :], R=[self.Db["xres"]], W=[xt.b])
                    for cg in range(2):
                        bk = 4 + (t * 2 + cg) % 4
                        for f in range(11):
                            self.MM(self.psf(bk), aT.ap[:, f, tt_ * 128:(tt_ + 1) * 128], wd.ap[:, f, cg * 512:(cg + 1) * 512], f == 0, f == 10,
                                    [aT.b, wd.b], [self.pb[bk]])
                        self.V("tensor_tensor", [self.pb[bk], xt.b], [xo.b], out=xo.ap[:, cg * 512:(cg + 1) * 512], in0=self.psf(bk),
                               in1=xt.ap[:, cg * 512:(cg + 1) * 512], op=ALU.add)
                    self.ST(self.d_xres[t * 128:(t + 1) * 128, :], xo.ap, R=[xo.b], W=[self.Db["xres"]])
            self.P.barrier()
        self.phase_end()

    def ph_final(self):
        A, I = self.A, self.I
        gb = A.alloc([128, D], F32)
        self.LD(gb.ap, I["norm_final"].partition_broadcast(128), W=[gb.b])
        xts = [A.alloc([128, D], F32) for _ in range(2)]
        junk = A.alloc([128, D], F32)
        ys = [A.alloc([128, D], F32) for _ in range(2)]
        sss = [A.alloc([128, 1], F32) for _ in range(2)]
        rss = [A.alloc([128, 1], F32) for _ in range(2)]
        yb = Buf()
        for t in range(NT):
            xt, y_, ss, rstd = xts[t % 2], ys[t % 2], sss[t % 2], rss[t % 2]
            self.LD(xt.ap, self.d_xres[t * 128:(t + 1) * 128, :], R=[self.Db["xres"]], W=[xt.b])
            self.rms_rstd(xt, junk, ss, rstd, D)
            self.V("scalar_tensor_tensor", [xt.b, rstd.b, gb.b], [y_.b], out=y_.ap, in0=xt.ap, scalar=rstd.ap[:, 0:1], in1=gb.ap,
                   op0=ALU.mult, op1=ALU.mult)
            self.ST(self.out[t * 128:(t + 1) * 128, :], y_.ap, R=[y_.b], W=[yb])
        self.phase_end()
    def build(self, n_layers=2, stop_after=None):
        d = self.dram
        self.d_xres = d("xres", [S, D], F32)
        self.d_uT = d("uT", [512, S], F32)
        self.d_qnT = d("qnT", [8, 64, S], BF16)
        self.d_kvT = d("kvT", [4, 2, 64, S], BF16)
        self.d_krT = d("krT", [2, 32, S], F32)
        self.d_gmT = d("gmT", [3072, S], BF16)
        self.d_vtm = d("vtm", [S, 4, 65], BF16)
        self.d_misc = d("misc", [S, 664], F32)
        self.d_uactT = d("uactT", [512, S], BF16)
        self.d_onsaT = d("onsaT", [512, S], BF16)
        self.d_omlaT = d("omlaT", [512, S], BF16)
        self.d_vs = d("vs", [8, 128, 384], BF16)
        self.d_bc = d("bc", [8, 2048], BF16)
        self.d_cs = d("cs", [4, 32, S], F32)
        self.d_qn = d("qn", [4, 128, S], BF16)
        self.d_kn = d("kn", [4, 128, S], BF16)
        self.d_qpe = d("qpe", [4, 2, 32, S], BF16)
        self.d_kpe = d("kpe", [2, 32, S], BF16)
        self.d_vmla = d("vmla", [S, 4, 129], BF16)
        if "onsa_dbg" in self.debug:
            self.d_onsa_dbg = d("onsa_dbg", [S, 512], F32)
        self.cur_x_b = Buf()
        self.setup()
        phases = []
        phases.append(("tables", lambda: (self.ph_bias_tables(), self.ph_rope_tables())))
        for l in range(n_layers):
            xsrc = self.I["x"] if l == 0 else self.d_xres
            xb = Buf() if l == 0 else self.Db["xres"]
            phases.append((f"inproj{l}", lambda l=l, xsrc=xsrc, xb=xb: (setattr(self, "cur_x_b", xb), self.ph_inproj(l, xsrc))))
            phases.append((f"conv{l}", lambda l=l: self.ph_conv(l)))
            phases.append((f"nsa{l}", lambda l=l: self.ph_nsa(l)))
            phases.append((f"mla{l}", lambda l=l: self.ph_mla(l)))
            phases.append((f"merge{l}", lambda l=l, xsrc=xsrc, xb=xb: self.ph_merge(l, xsrc, xb)))
            phases.append((f"xattn{l}", lambda l=l: self.ph_xattn(l)))
            phases.append((f"ffn{l}", lambda l=l: self.ph_ffn(l)))
        phases.append(("final", lambda: self.ph_final()))
        skip = set(self.skip)
        for name, fn in phases:
            if name not in skip:
                fn()
            if stop_after == name:
                break
        self.P.finish()


def build_nc(debug=None, n_layers=2, stop_after=None, skip=()):
    nc = bass.Bass("TRN2", target_bir_lowering=False)
    with contextlib.ExitStack() as es:
        k = K(nc, es, debug)
        k.skip = list(skip)
        k.build(n_layers, stop_after)
    return nc, k


def make_in_maps(inputs, consts):
    maps = []
    for b in range(8):
        m = {}
        for n in WEIGHT_SHAPES:
            a = np.asarray(inputs[n])
            if n in ("x", "mem", "positions"):
                a = a[b]
            m[n] = np.ascontiguousarray(a)
        for n, v in consts.items():
            m["c_" + n] = v
        maps.append(m)
    return maps


def kernel(**inputs):
    nc, k = build_nc()
    consts = host_consts()
    in_maps = make_in_maps(inputs, consts)
    res = run_bass_kernel_spmd(nc, in_maps, core_ids=list(range(8)))
    return np.stack([np.asarray(r["y"]) for r in res.results], axis=0).astype(np.float32)
```

```python
import contextlib
import math
import numpy as np
import concourse.bass as bass
import concourse.mybir as mybir
from concourse.ap import AP
from concourse.bass_utils import run_bass_kernel_spmd

F32 = mybir.dt.float32
BF16 = mybir.dt.bfloat16
I32 = mybir.dt.int32
U8 = mybir.dt.uint8
ALU = mybir.AluOpType
AF = mybir.ActivationFunctionType
AX = mybir.AxisListType

ENGS = ["pe", "act", "dve", "pool", "sp"]
N_DSEM = 8
DTSIZE = {F32: 4, BF16: 2, I32: 4, U8: 1}

S = 4096
D = 1024
NT = S // 128
NQC = S // 512
IN_COLS = 6104
FFN = 2816
NEG = -30000.0


class Buf:
    __slots__ = ("w", "r")

    def __init__(self):
        self.w = None
        self.r = {}


class Tl:
    __slots__ = ("ap", "b")

    def __init__(self, ap, b=None):
        self.ap = ap
        self.b = b if b is not None else Buf()


class Prog:
    def __init__(self, nc, es):
        self.nc = nc
        self.es = es
        self.q = {e: [] for e in ENGS}
        self.cnt = {}
        self.sems = {}
        self.known = {e: {} for e in ENGS}
        self.ep = {}
        self.ekey = {}
        for e in ENGS:
            self.ep[e] = 0
            self._new_epoch(e)
        self.dpool = {}
        self.dnext = {}
        for e in ("sp", "pool", "act"):
            self.dpool[e] = []
            for i in range(N_DSEM):
                k = f"d_{e}_{i}"
                self.sems[k] = es.enter_context(nc.semaphore(k))
                self.cnt[k] = 0
                self.dpool[e].append(k)
            self.dnext[e] = 0
        self.nins = 0

    SEM_LIMIT = 12000

    def _new_epoch(self, e):
        key = f"{e}#{self.ep[e]}"
        self.ep[e] += 1
        self.sems[key] = self.es.enter_context(self.nc.semaphore(f"sem_{e}_{self.ep[e]}"))
        self.cnt[key] = 0
        self.ekey[e] = key

    def _need(self, eng, R, W):
        need = {}

        def add(ev):
            if ev is None:
                return
            k, v = ev
            if need.get(k, 0) < v:
                need[k] = v
        for b in R:
            add(b.w)
        for b in W:
            add(b.w)
            for k, v in b.r.items():
                add((k, v))
        kn = self.known[eng]
        for k, v in need.items():
            if eng == "pe" and k.startswith("pe#"):
                continue
            if kn.get(k, 0) >= v:
                continue
            kn[k] = v
            self.q[eng].append(("wait", k, v))

    def _done(self, ev, R, W):
        k, v = ev
        for b in W:
            b.w = ev
            b.r = {}
        for b in R:
            if b.r.get(k, 0) < v:
                b.r[k] = v

    def op(self, eng, fn, R=(), W=()):
        self._need(eng, R, W)
        if self.cnt[self.ekey[eng]] >= self.SEM_LIMIT:
            self._new_epoch(eng)
        key = self.ekey[eng]
        self.cnt[key] += 1
        self.q[eng].append(("ins", fn, key, 1))
        self._done((key, self.cnt[key]), R, W)
        self.nins += 1

    def dma(self, eng, out, in_, R=(), W=(), **kw):
        self._need(eng, R, W)
        pool = self.dpool[eng]
        k = pool[self.dnext[eng] % len(pool)]
        self.dnext[eng] += 1
        if self.cnt[k] > 0 and self.known[eng].get(k, 0) < self.cnt[k]:
            self.known[eng][k] = self.cnt[k]
            self.q[eng].append(("wait", k, self.cnt[k]))
        self.cnt[k] += 16
        self.q[eng].append(("ins", lambda e: e.dma_start(out=out, in_=in_, **kw), k, 16))
        self._done((k, self.cnt[k]), R, W)
        self.nins += 1

    def barrier(self):
        for e in ENGS:
            kn = self.known[e]
            for k, c in self.cnt.items():
                if k.split("#")[0] == e or c == 0:
                    continue
                if kn.get(k, 0) < c:
                    kn[k] = c
                    self.q[e].append(("wait", k, c))

    def cut(self):
        self.barrier()
        for e in ENGS:
            self.q[e].append(("cut",))

    def finish(self):
        self.barrier()
        nc = self.nc
        sems = self.sems
        segs = {}
        nseg = 1
        for e in ENGS:
            cur = []
            segs[e] = [cur]
            for it in self.q[e]:
                if it[0] == "cut":
                    cur = []
                    segs[e].append(cur)
                else:
                    cur.append(it)
            nseg = max(nseg, len(segs[e]))

        def replay(items):
            def f(e):
                for it in items:
                    if it[0] == "wait":
                        e.wait_ge(sems[it[1]], it[2])
                    else:
                        it[1](e).then_inc(sems[it[2]], it[3])
            return f

        for s in range(nseg):
            if not any(len(segs[e][s]) for e in ENGS):
                continue
            with nc.Block() as block:
                block.tensor(replay(segs["pe"][s]))
                block.scalar(replay(segs["act"][s]))
                block.vector(replay(segs["dve"][s]))
                block.gpsimd(replay(segs["pool"][s]))
                block.sync(replay(segs["sp"][s]))


class Arena:
    def __init__(self, t, size):
        self.t = t
        self.size = size
        self.off = 0

    def alloc(self, shape, dtype, parts=None):
        p = shape[0] if parts is None else parts
        n = 1
        for s in shape[1:]:
            n *= s
        nb = n * DTSIZE[dtype]
        nb_al = (nb + 63) // 64 * 64
        assert self.off + nb_al <= self.size, f"SBUF arena overflow {self.off}+{nb_al}>{self.size}"
        v = self.t[0:p, self.off:self.off + nb].bitcast(dtype)
        self.off += nb_al
        if len(shape) == 3:
            v = v.rearrange("p (a b) -> p a b", a=shape[1])
        elif len(shape) == 4:
            v = v.rearrange("p (a b c) -> p a b c", a=shape[1], b=shape[2])
        return Tl(v)

    def mark(self):
        return self.off

    def reset(self, m):
        self.off = m


def t5_bucket_np(d):
    d = np.asarray(d)
    dd = np.maximum(d, 0)
    lr = np.log(np.maximum(dd, 1).astype(np.float32) / np.float32(16)) / np.float32(math.log(8.0))
    large = np.minimum(16 + (lr * 16).astype(np.int32), 31)
    return np.where(dd < 16, dd, large)


def host_consts():
    c = {}
    c["ident"] = np.eye(128, dtype=np.float32)
    oh = np.zeros((33, 384), np.float32)
    for i in range(383):
        d = i - 127
        if d >= 0:
            oh[int(t5_bucket_np(d)), i] += 1.0
            oh[31, i] -= 1.0
        else:
            oh[32, i] = NEG
    c["oh_v"] = oh
    ohc = np.zeros((33, 128 * 16), np.float32)
    for ql in range(128):
        for cp in range(16):
            d = ql - 16 * cp + 97
            j = ql * 16 + cp
            if d >= 0:
                ohc[int(t5_bucket_np(d)), j] += 1.0
                ohc[31, j] -= 1.0
            else:
                ohc[32, j] = NEG
    c["oh_c"] = ohc
    kl = np.arange(128)[:, None]
    ql = np.arange(128)[None, :]
    c["emask"] = np.where(ql >= kl, NEG, 0.0).astype(np.float32)
    c["cmask"] = np.where(ql < kl, NEG, 0.0).astype(np.float32)
    eb = np.zeros((64, S), np.float32)
    for j in range(64):
        eb[j, j * 64:(j + 1) * 64] = 1.0
    c["ebig"] = eb
    cs = 16 * np.arange(255)
    ss = 64 * np.arange(64)
    cov = ((cs[:, None] < ss[None, :] + 64) & (cs[:, None] + 32 > ss[None, :])).astype(np.float32)
    covp = np.zeros((256, 64), np.float32)
    covp[:255] = cov
    c["cover"] = covp
    t = np.arange(S)[:, None]
    j = np.arange(64)[None, :]
    cur = t // 64
    forced = (j == 0) | (j == cur) | (j == cur - 1)
    causal = (64 * j) <= t
    c["selA"] = np.where(forced, 0.0, np.where(causal, 1.0, 0.0)).astype(np.float32)
    c["selB"] = np.where(forced, 1e4 + j, np.where(causal, 0.0, -1.0)).astype(np.float32)
    c["cflag"] = (np.arange(S) >= 31).astype(np.float32).reshape(S, 1)
    half = 32
    inv = (np.float32(10000.0) ** (-np.arange(half, dtype=np.float32) / np.float32(half))).astype(np.float32)
    c["invfreq"] = (inv / np.float32(2 * math.pi)).astype(np.float32).reshape(32, 1)
    return c


CONST_SHAPES = {
    "ident": [128, 128], "oh_v": [33, 384], "oh_c": [33, 2048], "emask": [128, 128], "cmask": [128, 128],
    "ebig": [64, S], "cover": [256, 64], "selA": [S, 64], "selB": [S, 64], "cflag": [S, 1], "invfreq": [32, 1],
}

WEIGHT_SHAPES = {
    "x": [S, D], "mem": [256, D], "positions": [S], "rel_bias": [32, 8],
    "norm_mix": [2, D], "norm_xattn": [2, D], "norm_mem": [2, D], "norm_ffn": [2, D], "norm_final": [D],
    "w_in": [2, D, IN_COLS], "conv_w": [2, 31, 512], "conv_b": [2, 512], "conv_ln_g": [2, 512], "conv_ln_b": [2, 512],
    "w_branch_conv": [2, 512, D],
    "cmp_pos_k": [2, 32, 64], "cmp_w1_k": [2, 2048, 256], "cmp_b1_k": [2, 256], "cmp_w2_k": [2, 256, 64],
    "cmp_pos_v": [2, 32, 64], "cmp_w1_v": [2, 2048, 256], "cmp_b1_v": [2, 256], "cmp_w2_v": [2, 256, 64],
    "w_branch_nsa": [2, 512, D], "mla_norm_q": [2, 384], "mla_norm_kv": [2, 256],
    "w_uq": [2, 384, 768], "w_ukv": [2, 256, 1024], "w_branch_mla": [2, 512, D], "w_out": [2, D, D],
    "w_xq": [2, D, 512], "w_xkv": [2, D, D], "w_xo": [2, 512, D], "w_gate_up": [2, D, 2 * FFN], "w_down": [2, FFN, D],
}


class K:
    def __init__(self, nc, es, debug=None):
        self.nc = nc
        self.es = es
        self.debug = debug or []
        self.skip = []
        self.P = Prog(nc, es)
        self.I = {}
        for n, sh in WEIGHT_SHAPES.items():
            dt = I32 if n == "positions" else F32
            self.I[n] = nc.dram_tensor(n, sh, dt, kind="ExternalInput").ap()
        self.C = {}
        for n, sh in CONST_SHAPES.items():
            self.C[n] = nc.dram_tensor("c_" + n, sh, F32, kind="ExternalInput").ap()
        self.out = nc.dram_tensor("y", [S, D], F32, kind="ExternalOutput").ap()
        self.Db = {}
        self.Dr = {}
        arena_t = es.enter_context(nc.sbuf_tensor("arena", [128, 204800], U8))
        self.A = Arena(arena_t, 204800)
        ps = es.enter_context(nc.psum_tensor("ps", [128, 8, 512], F32))
        self.ps = ps
        self.pb = [Buf() for _ in range(8)]

    def dram(self, name, shape, dtype):
        kind = "ExternalOutput" if name in self.debug else "Internal"
        t = self.nc.dram_tensor(name, list(shape), dtype, kind=kind)
        self.Dr[name] = t
        self.Db[name] = Buf()
        return t.ap()

    def MM(self, out, lhsT, rhs, start, stop, R, W):
        self.P.op("pe", lambda e: e.matmul(out, lhsT=lhsT, rhs=rhs, start=start, stop=stop), R, W)

    def TR(self, out, in_, R, W):
        ident = self.identb.ap
        self.P.op("pe", lambda e: e.transpose(out=out, in_=in_, identity=ident), list(R) + [self.identb.b], W)

    def ACT(self, out, in_, func, R, W, **kw):
        self.P.op("act", lambda e: e.activation(out=out, in_=in_, func=func, **kw), R, W)

    def V(self, name, R, W, **kw):
        self.P.op("dve", lambda e: getattr(e, name)(**kw), R, W)

    def G(self, name, R, W, **kw):
        self.P.op("pool", lambda e: getattr(e, name)(**kw), R, W)

    def E(self, eng, name, R, W, **kw):
        self.P.op(eng, lambda e: getattr(e, name)(**kw), R, W)

    def LD(self, out, in_, R=(), W=(), **kw):
        self.P.dma("sp", out, in_, R, W, **kw)

    def ST(self, out, in_, R=(), W=(), **kw):
        self.P.dma("pool", out, in_, R, W, **kw)

    def psf(self, i):
        return self.ps[:, i, :]

    def psb(self, i):
        return self.ps[:, i, :].bitcast(BF16)

    def setup(self):
        A = self.A
        self.identf = A.alloc([128, 128], F32)
        self.identb = A.alloc([128, 128], BF16)
        self.eps = A.alloc([128, 1], F32)
        self.onesf = A.alloc([128, 128], F32)
        self.onesb = A.alloc([128, 128], BF16)
        self.LD(self.identf.ap, self.C["ident"], W=[self.identf.b])
        self.V("tensor_copy", [self.identf.b], [self.identb.b], out=self.identb.ap, in_=self.identf.ap)
        self.V("memset", [], [self.eps.b], ap=self.eps.ap, constant=1e-6)
        self.V("memset", [], [self.onesf.b], ap=self.onesf.ap, constant=1.0)
        self.V("memset", [], [self.onesb.b], ap=self.onesb.ap, constant=1.0)
        self.base_mark = A.mark()

    def phase_end(self):
        self.P.cut()
        self.A.reset(self.base_mark)

    def release(self, m):
        self.P.barrier()
        self.A.reset(m)

    def load_weight_bf16(self, dst, src, stage, kchunks, ncols, parts=128):
        sv = stage.ap[0:parts, 0:kchunks * ncols].rearrange("p (k n) -> p k n", k=kchunks)
        self.LD(sv, src.rearrange("(k p) n -> p k n", p=parts), W=[stage.b])
        self.G("tensor_copy", [stage.b], [dst.b], out=dst.ap, in_=sv)

    def rms_rstd(self, xt, junk, ss, rstd, n):
        self.ACT(junk.ap, xt.ap, AF.Square, [xt.b], [junk.b, ss.b], scale=float(n) ** -0.5, accum_out=ss.ap)
        self.ACT(rstd.ap, ss.ap, AF.Sqrt, [ss.b, self.eps.b], [rstd.b], bias=self.eps.ap, scale=1.0)
        self.V("reciprocal", [rstd.b], [rstd.b], out=rstd.ap, in_=rstd.ap)

    def norm_to_hT(self, src, gain, hT, ntile=NT, width=D, pbank=0):
        A = self.A
        m = A.mark()
        kc = width // 128
        gb = A.alloc([128, width], F32)
        self.LD(gb.ap, gain.partition_broadcast(128), W=[gb.b])
        xts = [A.alloc([128, width], F32) for _ in range(2)]
        junk = A.alloc([128, width], F32)
        hs = [A.alloc([128, width], BF16) for _ in range(2)]
        sss = [A.alloc([128, 1], F32) for _ in range(2)]
        rss = [A.alloc([128, 1], F32) for _ in range(2)]
        for t in range(ntile):
            xt, h, ss, rstd = xts[t % 2], hs[t % 2], sss[t % 2], rss[t % 2]
            self.LD(xt.ap, src[t * 128:(t + 1) * 128, :], R=[self.cur_x_b], W=[xt.b])
            self.rms_rstd(xt, junk, ss, rstd, width)
            self.V("scalar_tensor_tensor", [xt.b, rstd.b, gb.b], [h.b], out=h.ap, in0=xt.ap, scalar=rstd.ap[:, 0:1],
                   in1=gb.ap, op0=ALU.mult, op1=ALU.mult)
            bank = pbank + (t % 2)
            pt = self.psb(bank)[:, 0:kc * 128].rearrange("p (k n) -> p k n", k=kc)
            for k in range(kc):
                self.TR(pt[:, k, :], h.ap[:, k * 128:(k + 1) * 128], [h.b], [self.pb[bank]])
            self.ACT(hT.ap[:, :, t * 128:(t + 1) * 128], pt, AF.Copy, [self.pb[bank]], [hT.b])
        self.release(m)

    def ph_inproj(self, l, xsrc):
        A = self.A
        I = self.I
        w = I["w_in"][l]
        hT = A.alloc([128, 8, S], BF16)
        self.norm_to_hT(xsrc, I["norm_mix"][l], hT)
        wst = [A.alloc([128, 8 * 512], F32) for _ in range(2)]
        wbf = [A.alloc([128, 8, 512], BF16) for _ in range(2)]
        stg = [A.alloc([128, S], F32) for _ in range(2)]
        sig = [A.alloc([128, 512], F32) for _ in range(2)]
        self._u = 0
        self._s = 0
        self._pbk = 0

        def load_unit(ranges):
            i = self._u % 2
            self._u += 1
            off = 0
            ws, wb = wst[i], wbf[i]
            tot = sum(n for _, n in ranges)
            sv = ws.ap[:, 0:8 * tot].rearrange("p (k n) -> p k n", k=8)
            for (c0, n) in ranges:
                self.LD(sv[:, :, off:off + n], w[:, c0:c0 + n].rearrange("(k p) n -> p k n", p=128),
                        R=[wb.b], W=[ws.b])
                off += n
            self.G("tensor_copy", [ws.b], [wb.b], out=wb.ap[:, :, 0:tot], in_=sv)
            return wb

        def fm_mm(wb, off, m, tc, bank):
            for k in range(8):
                self.MM(self.psf(bank)[0:m, :], wb.ap[:, k, off:off + m], hT.ap[:, k, tc * 512:(tc + 1) * 512],
                        k == 0, k == 7, [wb.b, hT.b], [self.pb[bank]])

        def nb():
            b = self._pbk % 4
            self._pbk += 1
            return b

        def nstg():
            s = stg[self._s % 2]
            self._s += 1
            return s

        uT, qnT, kvT, krT, gmT = self.d_uT, self.d_qnT, self.d_kvT, self.d_krT, self.d_gmT
        for c in range(4):
            wb = load_unit([(c * 128, 128), (512 + c * 128, 128)])
            st = nstg()
            for tc in range(NQC):
                ba, bb = nb(), nb()
                fm_mm(wb, 0, 128, tc, ba)
                fm_mm(wb, 128, 128, tc, bb)
                sg = sig[tc % 2]
                self.ACT(sg.ap, self.psf(bb), AF.Sigmoid, [self.pb[bb]], [sg.b])
                self.V("tensor_tensor", [self.pb[ba], sg.b], [st.b], out=st.ap[:, tc * 512:(tc + 1) * 512],
                       in0=self.psf(ba), in1=sg.ap, op=ALU.mult)
            self.ST(uT[c * 128:(c + 1) * 128, :], st.ap, R=[st.b], W=[self.Db["uT"]])
        for hp in range(2):
            wb = load_unit([(1024 + hp * 256, 256)])
            for hh in range(4):
                h = hp * 4 + hh
                st = nstg()
                sb = st.ap[0:64, :].bitcast(BF16)
                for tc in range(NQC):
                    bk = nb()
                    fm_mm(wb, hh * 64, 64, tc, bk)
                    self.ACT(sb[:, tc * 512:(tc + 1) * 512], self.psf(bk)[0:64, :], AF.Copy, [self.pb[bk]], [st.b], scale=0.125)
                self.ST(qnT[h], sb[:, 0:S], R=[st.b], W=[self.Db["qnT"]])
        for si, slot in enumerate((0, 1, 2, 4)):
            wb = load_unit([(1536 + slot * 128, 128)])
            for g in range(2):
                st = nstg()
                sb = st.ap[0:64, :].bitcast(BF16)
                for tc in range(NQC):
                    bk = nb()
                    fm_mm(wb, g * 64, 64, tc, bk)
                    self.V("tensor_copy", [self.pb[bk]], [st.b], out=sb[:, tc * 512:(tc + 1) * 512], in_=self.psf(bk)[0:64, :])
                self.ST(kvT[si, g], sb[:, 0:S], R=[st.b], W=[self.Db["kvT"]])
        wb = load_unit([(2968, 64)])
        for hf in range(2):
            st = nstg()
            for tc in range(NQC):
                bk = nb()
                fm_mm(wb, hf * 32, 32, tc, bk)
                self.V("tensor_copy", [self.pb[bk]], [st.b], out=st.ap[0:32, tc * 512:(tc + 1) * 512], in_=self.psf(bk)[0:32, :])
            self.ST(krT[hf], st.ap[0:32, :], R=[st.b], W=[self.Db["krT"]])
        for cg in range(6):
            wb = load_unit([(3032 + cg * 512, 512)])
            for j in range(4):
                st = nstg()
                sb = st.ap.bitcast(BF16)
                for tc in range(NQC):
                    bk = nb()
                    fm_mm(wb, j * 128, 128, tc, bk)
                    self.ACT(sb[:, tc * 512:(tc + 1) * 512], self.psf(bk), AF.Sigmoid, [self.pb[bk]], [st.b])
                r0 = (cg * 4 + j) * 128
                self.ST(gmT[r0:r0 + 128, :], sb[:, 0:S], R=[st.b], W=[self.Db["gmT"]])
        wbv = load_unit([(1536 + 3 * 128, 128), (1536 + 5 * 128, 128)])
        wbm = load_unit([(2304, 512)])
        m = A.mark()
        ws3 = A.alloc([128, 8 * 152], F32)
        wb3 = A.alloc([128, 8, 152], BF16)
        sv3 = ws3.ap.rearrange("p (k n) -> p k n", k=8)
        self.LD(sv3, w[:, 2816:2968].rearrange("(k p) n -> p k n", p=128), W=[ws3.b])
        self.G("tensor_copy", [ws3.b], [wb3.b], out=wb3.ap, in_=sv3)
        vst = [A.alloc([128, 4, 65], BF16) for _ in range(2)]
        mst = [A.alloc([128, 664], F32) for _ in range(2)]
        for v_ in vst:
            self.V("memset", [], [v_.b], ap=v_.ap, constant=1.0)
        for t in range(NT):
            bv, bm, b3 = 4 + (t % 2) * 2, 5 + (t % 2) * 2, nb()
            vs, ms = vst[t % 2], mst[t % 2]
            for k in range(8):
                self.MM(self.psf(bv)[:, 0:256], hT.ap[:, k, t * 128:(t + 1) * 128], wbv.ap[:, k, 0:256], k == 0, k == 7,
                        [wbv.b, hT.b], [self.pb[bv]])
            for k in range(8):
                self.MM(self.psf(bm), hT.ap[:, k, t * 128:(t + 1) * 128], wbm.ap[:, k, 0:512], k == 0, k == 7,
                        [wbm.b, hT.b], [self.pb[bm]])
            for k in range(8):
                self.MM(self.psf(b3)[:, 0:152], hT.ap[:, k, t * 128:(t + 1) * 128], wb3.ap[:, k, :], k == 0, k == 7,
                        [wb3.b, hT.b], [self.pb[b3]])
            self.V("tensor_copy", [self.pb[bv]], [vs.b], out=vs.ap[:, :, 0:64],
                   in_=self.psf(bv)[:, 0:256].rearrange("p (a d) -> p a d", a=4))
            self.ACT(ms.ap[:, 0:24], self.psf(bm)[:, 0:24], AF.Sigmoid, [self.pb[bm]], [ms.b])
            self.V("tensor_copy", [self.pb[bm]], [ms.b], out=ms.ap[:, 24:512], in_=self.psf(bm)[:, 24:512])
            self.ACT(ms.ap[:, 512:664], self.psf(b3)[:, 0:152], AF.Copy, [self.pb[b3]], [ms.b])
            self.ST(self.d_vtm[t * 128:(t + 1) * 128], vs.ap, R=[vs.b], W=[self.Db["vtm"]])
            self.ST(self.d_misc[t * 128:(t + 1) * 128, :], ms.ap, R=[ms.b], W=[self.Db["misc"]])
        self.phase_end()

    def col_load(self, dst, src1d, c0, n=128):
        self.LD(dst, src1d[c0:c0 + n].rearrange("(p o) -> p o", o=1))

    def ph_conv(self, l):
        A, I = self.A, self.I
        cwn = A.alloc([31, 512], F32)
        self.LD(cwn.ap, I["conv_w"][l], W=[cwn.b])
        cwT = A.alloc([128, 4, 32], F32)
        for c in range(4):
            self.MM(self.psf(0)[:, c * 32:c * 32 + 31], cwn.ap[:, c * 128:(c + 1) * 128], self.identf.ap[0:31, 0:31],
                    True, True, [cwn.b, self.identf.b], [self.pb[0]])
        self.V("tensor_copy", [self.pb[0]], [cwT.b], out=cwT.ap[:, :, 0:31],
               in_=self.psf(0)[:, 0:128].rearrange("p (c k) -> p c k", c=4)[:, :, 0:31])
        vecs = A.alloc([128, 3, 4], F32)
        for vi, nm in enumerate(("conv_b", "conv_ln_g", "conv_ln_b")):
            for c in range(4):
                self.P.dma("sp", vecs.ap[:, vi, c:c + 1], I[nm][l][c * 128:(c + 1) * 128].rearrange("(p o) -> p o", o=1),
                           W=[vecs.b])
        accs = A.alloc([128, 4, S], F32)
        accb = [Buf() for _ in range(4)]
        ubuf = [A.alloc([128, 30 + S], F32) for _ in range(2)]
        for u in ubuf:
            self.G("memset", [], [u.b], ap=u.ap[:, 0:30], constant=0.0)
        for c in range(4):
            ub = ubuf[c % 2]
            self.LD(ub.ap[:, 30:30 + S], self.d_uT[c * 128:(c + 1) * 128, :], R=[self.Db["uT"]], W=[ub.b])
            acc = accs.ap[:, c, :]
            self.V("tensor_scalar", [ub.b, cwT.b, vecs.b], [accb[c]], out=acc, in0=ub.ap[:, 0:S], scalar1=cwT.ap[:, c, 0:1],
                   scalar2=vecs.ap[:, 0, c:c + 1], op0=ALU.mult, op1=ALU.add)
            for k in range(1, 31):
                self.V("scalar_tensor_tensor", [ub.b, cwT.b, accb[c]], [accb[c]], out=acc, in0=ub.ap[:, k:k + S],
                       scalar=cwT.ap[:, c, k:k + 1], in1=acc, op0=ALU.mult, op1=ALU.add)
        uact = A.alloc([128, 4, S], BF16)
        tmp = {n: [A.alloc([128, 512], F32) for _ in range(2)] for n in ("sq", "mean", "msq", "var", "t")}
        for tc in range(NQC):
            sl = slice(tc * 512, (tc + 1) * 512)
            i2 = tc % 2
            b1, b2 = 1 + 2 * i2, 2 + 2 * i2
            for c in range(4):
                self.MM(self.psf(b1), self.onesf.ap, accs.ap[:, c, sl], c == 0, c == 3, [self.onesf.b, accb[c]], [self.pb[b1]])
            for c in range(4):
                sq = tmp["sq"][c % 2]
                self.ACT(sq.ap, accs.ap[:, c, sl], AF.Square, [accb[c]], [sq.b])
                self.MM(self.psf(b2), self.onesf.ap, sq.ap, c == 0, c == 3, [self.onesf.b, sq.b], [self.pb[b2]])
            mean, msq, var = tmp["mean"][i2], tmp["msq"][i2], tmp["var"][i2]
            self.ACT(mean.ap, self.psf(b1), AF.Copy, [self.pb[b1]], [mean.b], scale=1.0 / 512)
            self.V("tensor_tensor", [mean.b], [msq.b], out=msq.ap, in0=mean.ap, in1=mean.ap, op=ALU.mult)
            self.V("scalar_tensor_tensor", [self.pb[b2], msq.b], [var.b], out=var.ap, in0=self.psf(b2), scalar=1.0 / 512,
                   in1=msq.ap, op0=ALU.mult, op1=ALU.subtract)
            self.ACT(var.ap, var.ap, AF.Sqrt, [var.b, self.eps.b], [var.b], bias=self.eps.ap, scale=1.0)
            self.V("reciprocal", [var.b], [var.b], out=var.ap, in_=var.ap)
            for c in range(4):
                t = tmp["t"][c % 2]
                self.V("tensor_tensor", [accb[c], mean.b], [t.b], out=t.ap, in0=accs.ap[:, c, sl], in1=mean.ap, op=ALU.subtract)
                self.V("tensor_tensor", [t.b, var.b], [t.b], out=t.ap, in0=t.ap, in1=var.ap, op=ALU.mult)
                self.ACT(uact.ap[:, c, sl], t.ap, AF.Silu, [t.b, vecs.b], [uact.b], scale=vecs.ap[:, 1, c:c + 1],
                         bias=vecs.ap[:, 2, c:c + 1])
        for c in range(4):
            self.ST(self.d_uactT[c * 128:(c + 1) * 128, :], uact.ap[:, c, :], R=[uact.b], W=[self.Db["uactT"]])
        self.phase_end()

    def ph_bias_tables(self):
        A, I, C = self.A, self.I, self.C
        rb = A.alloc([33, 8], F32)
        self.V("memset", [], [rb.b], ap=rb.ap, constant=1.0)
        self.LD(rb.ap[0:32, :], I["rel_bias"], R=[rb.b], W=[rb.b])
        ohv = A.alloc([33, 384], F32)
        self.LD(ohv.ap, C["oh_v"], W=[ohv.b])
        ohc = A.alloc([33, 2048], F32)
        self.LD(ohc.ap, C["oh_c"], W=[ohc.b])
        vrep = [A.alloc([128, 384], BF16) for _ in range(2)]
        for h in range(8):
            bk = h % 2
            vr = vrep[h % 2]
            self.MM(self.psf(bk)[:, 0:384], rb.ap[:, h:h + 1].to_broadcast([33, 128]), ohv.ap, True, True, [rb.b, ohv.b], [self.pb[bk]])
            self.V("tensor_copy", [self.pb[bk]], [vr.b], out=vr.ap, in_=self.psf(bk)[:, 0:384])
            self.ST(self.d_vs[h], vr.ap, R=[vr.b], W=[self.Db["vs"]])
        bcs = A.alloc([8, 2048], BF16)
        for j in range(4):
            bk = 2 + j % 2
            self.MM(self.psf(bk)[0:8, :], rb.ap, ohc.ap[:, j * 512:(j + 1) * 512], True, True, [rb.b, ohc.b], [self.pb[bk]])
            self.V("tensor_copy", [self.pb[bk]], [bcs.b], out=bcs.ap[:, j * 512:(j + 1) * 512], in_=self.psf(bk)[0:8, :])
        self.ST(self.d_bc, bcs.ap, R=[bcs.b], W=[self.Db["bc"]])
        self.phase_end()

    def ph_rope_tables(self):
        A, I, C = self.A, self.I, self.C
        posi = A.alloc([32, S], I32)
        self.LD(posi.ap, I["positions"].partition_broadcast(32), W=[posi.b])
        inv = A.alloc([32, 1], F32)
        self.LD(inv.ap, C["invfreq"], W=[inv.b])
        turns = A.alloc([32, S], F32)
        self.V("tensor_copy", [posi.b], [turns.b], out=turns.ap, in_=posi.ap)
        self.V("tensor_scalar", [turns.b, inv.b], [turns.b], out=turns.ap, in0=turns.ap, scalar1=inv.ap[:, 0:1], scalar2=None,
               op0=ALU.mult)
        r = A.alloc([32, S], F32)
        ti = A.alloc([32, S], I32)
        tf = A.alloc([32, S], F32)
        fl = A.alloc([32, S], F32)
        res = A.alloc([32, S], F32)
        qs = 192.0 ** -0.5
        for idx, shift in ((1, 0.0), (0, 0.25)):
            self.V("tensor_scalar", [turns.b], [r.b], out=r.ap, in0=turns.ap, scalar1=shift, scalar2=None, op0=ALU.add)
            self.V("tensor_copy", [r.b], [ti.b], out=ti.ap, in_=r.ap)
            self.V("tensor_copy", [ti.b], [tf.b], out=tf.ap, in_=ti.ap)
            self.V("tensor_tensor", [r.b, tf.b], [r.b], out=r.ap, in0=r.ap, in1=tf.ap, op=ALU.subtract)
            self.V("tensor_scalar", [r.b], [fl.b], out=fl.ap, in0=r.ap, scalar1=0.5, scalar2=None, op0=ALU.is_gt)
            self.V("tensor_tensor", [r.b, fl.b], [r.b], out=r.ap, in0=r.ap, in1=fl.ap, op=ALU.subtract)
            self.V("tensor_scalar", [r.b], [fl.b], out=fl.ap, in0=r.ap, scalar1=-0.5, scalar2=None, op0=ALU.is_lt)
            self.V("tensor_tensor", [r.b, fl.b], [r.b], out=r.ap, in0=r.ap, in1=fl.ap, op=ALU.add)
            self.ACT(res.ap, r.ap, AF.Sin, [r.b], [res.b], scale=2.0 * math.pi)
            self.ST(self.d_cs[idx], res.ap, R=[res.b], W=[self.Db["cs"]])
            self.V("tensor_scalar", [res.b], [tf.b], out=tf.ap, in0=res.ap, scalar1=qs, scalar2=None, op0=ALU.mult)
            self.ST(self.d_cs[2 + idx], tf.ap, R=[tf.b], W=[self.Db["cs"]])
        self.phase_end()
    def gelu_tanh(self, out_bf, ps_in, bias_col, n, tmps):
        x, x2, s = tmps
        self.ACT(x.ap[:, 0:n], ps_in, AF.Identity, [self._gb, self._gc], [x.b], bias=bias_col, scale=1.0)
        self.V("tensor_tensor", [x.b], [x2.b], out=x2.ap[:, 0:n], in0=x.ap[:, 0:n], in1=x.ap[:, 0:n], op=ALU.mult)
        self.V("tensor_scalar", [x2.b], [x2.b], out=x2.ap[:, 0:n], in0=x2.ap[:, 0:n], scalar1=0.044715, scalar2=1.0,
               op0=ALU.mult, op1=ALU.add)
        self.V("tensor_tensor", [x2.b, x.b], [x2.b], out=x2.ap[:, 0:n], in0=x2.ap[:, 0:n], in1=x.ap[:, 0:n], op=ALU.mult)
        self.ACT(s.ap[:, 0:n], x2.ap[:, 0:n], AF.Sigmoid, [x2.b], [s.b], scale=2.0 * math.sqrt(2.0 / math.pi))
        self.V("tensor_tensor", [x.b, s.b], [self._go], out=out_bf, in0=x.ap[:, 0:n], in1=s.ap[:, 0:n], op=ALU.mult)

    def ph_nsa(self, l):
        A, I, C = self.A, self.I, self.C
        kcmpT = A.alloc([64, 2, 256], BF16)
        vcmp = A.alloc([128, 2, 2, 64], BF16)
        self.G("memset", [], [kcmpT.b], ap=kcmpT.ap, constant=0.0)
        self.G("memset", [], [vcmp.b], ap=vcmp.ap, constant=0.0)
        onsa = A.alloc([128, NT, 512], F32)
        onsab = [Buf() for _ in range(NT)]
        m0 = A.mark()
        stg = A.alloc([64, 32 * 256], F32)
        w1s = A.alloc([64, 32, 256], BF16)
        w2f = A.alloc([128, 2, 64], F32)
        w2s = A.alloc([128, 2, 64], BF16)
        posn = A.alloc([32, 64], F32)
        posT = A.alloc([64, 32], BF16)
        b1 = A.alloc([128, 2], F32)
        c1 = A.alloc([128, 2], F32)
        srcT = [A.alloc([64, S], BF16) for _ in range(2)]
        hact = A.alloc([128, 2, 256], BF16)
        tmps = [A.alloc([128, 256], F32) for _ in range(3)]
        for kv_i, nm in enumerate(("k", "v")):
            sv = stg.ap.rearrange("p (l h) -> p l h", l=32)
            self.LD(sv, I[f"cmp_w1_{nm}"][l].rearrange("(l d) h -> d l h", d=64), R=[w1s.b], W=[stg.b])
            self.G("tensor_copy", [stg.b], [w1s.b], out=w1s.ap, in_=sv)
            self.LD(w2f.ap, I[f"cmp_w2_{nm}"][l].rearrange("(c p) n -> p c n", p=128), R=[w2s.b], W=[w2f.b])
            self.G("tensor_copy", [w2f.b], [w2s.b], out=w2s.ap, in_=w2f.ap)
            self.LD(posn.ap, I[f"cmp_pos_{nm}"][l], W=[posn.b])
            self.MM(self.psf(0)[0:64, 0:32], posn.ap, self.identf.ap[0:32, 0:32], True, True, [posn.b, self.identf.b], [self.pb[0]])
            self.V("tensor_copy", [self.pb[0]], [posT.b], out=posT.ap, in_=self.psf(0)[0:64, 0:32])
            for hc in range(2):
                self.P.dma("sp", b1.ap[:, hc:hc + 1], I[f"cmp_b1_{nm}"][l][hc * 128:(hc + 1) * 128].rearrange("(p o) -> p o", o=1),
                           R=[b1.b], W=[b1.b])
            for hc in range(2):
                for li in range(32):
                    self.MM(self.psf(1)[:, hc:hc + 1], w1s.ap[:, li, hc * 128:(hc + 1) * 128], posT.ap[:, li:li + 1],
                            li == 0, li == 31, [w1s.b, posT.b], [self.pb[1]])
            self.V("tensor_tensor", [self.pb[1], b1.b], [c1.b], out=c1.ap, in0=self.psf(1)[:, 0:2], in1=b1.ap, op=ALU.add)
            for g in range(2):
                st = srcT[g]
                self.LD(st.ap, self.d_kvT[kv_i, g], R=[self.Db["kvT"]], W=[st.b])
                s3 = st.ap.rearrange("p (c r) -> p c r", r=16)
                for hc in range(2):
                    bk = 2 + hc
                    for li in range(32):
                        a, r = li // 16, li % 16
                        self.MM(self.psf(bk)[:, 0:255], w1s.ap[:, li, hc * 128:(hc + 1) * 128], s3[:, a:a + 255, r],
                                li == 0, li == 31, [w1s.b, st.b], [self.pb[bk]])
                    self._gb, self._gc, self._go = self.pb[bk], c1.b, hact.b
                    self.gelu_tanh(hact.ap[:, hc, 0:255], self.psf(bk)[:, 0:255], c1.ap[:, hc:hc + 1], 255, tmps)
                if nm == "k":
                    for hc in range(2):
                        self.MM(self.psf(4)[0:64, 0:255], w2s.ap[:, hc, :], hact.ap[:, hc, 0:255], hc == 0, hc == 1,
                                [w2s.b, hact.b], [self.pb[4]])
                    self.V("tensor_copy", [self.pb[4]], [kcmpT.b], out=kcmpT.ap[:, g, 0:255], in_=self.psf(4)[0:64, 0:255])
                else:
                    for cb in range(2):
                        n = 128 if cb == 0 else 127
                        for hc in range(2):
                            self.MM(self.psf(5 + cb)[0:n, 0:64], hact.ap[:, hc, cb * 128:cb * 128 + n], w2s.ap[:, hc, :], hc == 0,
                                    hc == 1, [w2s.b, hact.b], [self.pb[5 + cb]])
                        self.V("tensor_copy", [self.pb[5 + cb]], [vcmp.b], out=vcmp.ap[0:n, g, cb, :], in_=self.psf(5 + cb)[0:n, 0:64])
        self.release(m0)
        gates = A.alloc([128, NT, 24], F32)
        self.LD(gates.ap, self.d_misc.rearrange("(t p) c -> p t c", p=128)[:, :, 0:24], R=[self.Db["misc"]], W=[gates.b])
        ebig = A.alloc([64, S], BF16)
        cover = A.alloc([128, 2, 64], BF16)
        emask = A.alloc([128, 128], BF16)
        cflag = A.alloc([128, 1], F32)
        self.LD(cflag.ap, C["cflag"][0:128, :], W=[cflag.b])
        m1 = A.mark()
        tmpf = A.alloc([64, S], F32)
        self.LD(tmpf.ap, C["ebig"], W=[tmpf.b])
        self.G("tensor_copy", [tmpf.b], [ebig.b], out=ebig.ap, in_=tmpf.ap)
        cvf = A.alloc([128, 2, 64], F32)
        self.LD(cvf.ap, C["cover"].rearrange("(c p) n -> p c n", p=128), W=[cvf.b])
        self.G("tensor_copy", [cvf.b], [cover.b], out=cover.ap, in_=cvf.ap)
        emf = A.alloc([128, 128], F32)
        self.LD(emf.ap, C["emask"], W=[emf.b])
        self.G("tensor_copy", [emf.b], [emask.b], out=emask.ap, in_=emf.ap)
        self.release(m1)
        qT = A.alloc([64, 4, S], BF16)
        kslc = A.alloc([64, S], BF16)
        kwin = A.alloc([64, S], BF16)
        vslc = A.alloc([128, NT, 65], BF16)
        vwin = A.alloc([128, NT, 65], BF16)
        strip = A.alloc([128, 4, 1024], BF16)
        bct = A.alloc([128, 4, 16], BF16)
        selbT = A.alloc([64, S], BF16)
        pTs = [A.alloc([128, 512], BF16) for _ in range(3)]
        pTfar = [A.alloc([128, 128], BF16) for _ in range(2)]
        pex = [A.alloc([128, 256], F32) for _ in range(2)]
        pcs = [A.alloc([128, 256], BF16) for _ in range(2)]
        pcT = [A.alloc([128, 2, 128], BF16) for _ in range(2)]
        sm = [A.alloc([128, 8], F32) for _ in range(4)]
        selA = [A.alloc([128, 64], F32) for _ in range(2)]
        selB = [A.alloc([128, 64], F32) for _ in range(2)]
        sc_t = [A.alloc([128, 64], F32) for _ in range(2)]
        s2_t = [A.alloc([128, 64], F32) for _ in range(2)]
        m8 = [A.alloc([128, 8], F32) for _ in range(2)]
        sbf = [A.alloc([128, 64], BF16) for _ in range(2)]
        self._smi = 0

        def small():
            t = sm[self._smi % 4]
            self._smi += 1
            return t
        vview = self.d_vtm.rearrange("(t p) a e -> p t a e", p=128)
        for g in range(2):
            for hh in range(4):
                self.LD(qT.ap[:, hh, :], self.d_qnT[g * 4 + hh], R=[self.Db["qnT"]], W=[qT.b])
            self.LD(kslc.ap, self.d_kvT[2, g], R=[self.Db["kvT"]], W=[kslc.b])
            self.LD(kwin.ap, self.d_kvT[3, g], R=[self.Db["kvT"]], W=[kwin.b])
            self.LD(vslc.ap, vview[:, :, g, :], R=[self.Db["vtm"]], W=[vslc.b])
            self.LD(vwin.ap, vview[:, :, 2 + g, :], R=[self.Db["vtm"]], W=[vwin.b])
            self.G("memset", [], [strip.b], ap=strip.ap[:, :, 0:384], constant=NEG)
            self.G("memset", [], [strip.b], ap=strip.ap[:, :, 640:1024], constant=0.0)
            for hh in range(4):
                h = g * 4 + hh
                for o in range(2):
                    src = AP(self.Dr["vs"], h * 128 * 384 + o * 128 + 127, [[383, 128], [1, 128]])
                    self.LD(strip.ap[:, hh, (3 + o) * 128:(4 + o) * 128], src, R=[self.Db["vs"]], W=[strip.b])
                self.LD(bct.ap[:, hh, :], self.d_bc[h].rearrange("(q c) -> q c", c=16), R=[self.Db["bc"]], W=[bct.b])
            for qt in range(NT):
                ncq = min(8 * qt + 8, 255)
                bs = 8 * qt - 8
                c_lo, c_hi = max(bs, 0), min(bs + 16, 255)
                nblk = (ncq + 127) // 128
                bimp = 6 + qt % 2
                sa, sb_ = selA[qt % 2], selB[qt % 2]
                self.LD(sa.ap, C["selA"][qt * 128:(qt + 1) * 128, :], W=[sa.b])
                self.LD(sb_.ap, C["selB"][qt * 128:(qt + 1) * 128, :], W=[sb_.b])
                for hh in range(4):
                    h = g * 4 + hh
                    i2 = (qt * 4 + hh) % 2
                    bsc, btr, bo = i2, 2 + i2, 4 + i2
                    sc = self.psf(bsc)
                    self.MM(sc[:, 0:ncq], qT.ap[:, hh, qt * 128:(qt + 1) * 128], kcmpT.ap[:, g, 0:ncq], True, False,
                            [qT.b, kcmpT.b], [self.pb[bsc]])
                    self.MM(sc[:, c_lo:c_hi], self.identb.ap, bct.ap[:, hh, c_lo - bs:c_hi - bs], False, True,
                            [self.identb.b, bct.b], [self.pb[bsc]])
                    mx = small()
                    self.V("reduce_max", [self.pb[bsc]], [mx.b], out=mx.ap[:, 0:1], in_=sc[:, 0:ncq], axis=AX.X)
                    self.V("tensor_scalar", [mx.b], [mx.b], out=mx.ap[:, 1:2], in0=mx.ap[:, 0:1], scalar1=-1.0, scalar2=None, op0=ALU.mult)
                    pe_, pc_, pt_ = pex[i2], pcs[i2], pcT[i2]
                    self.ACT(pe_.ap[:, 0:ncq], sc[:, 0:ncq], AF.Exp, [self.pb[bsc], mx.b], [pe_.b, mx.b], bias=mx.ap[:, 1:2], scale=1.0,
                             accum_out=mx.ap[:, 2:3])
                    self.V("reciprocal", [mx.b], [mx.b], out=mx.ap[:, 3:4], in_=mx.ap[:, 2:3])
                    if qt == 0:
                        self.V("tensor_tensor", [mx.b, cflag.b], [mx.b], out=mx.ap[:, 3:4], in0=mx.ap[:, 3:4], in1=cflag.ap, op=ALU.mult)
                    self.G("memset", [], [pc_.b], ap=pc_.ap, constant=0.0)
                    self.V("tensor_scalar", [pe_.b, mx.b], [pc_.b], out=pc_.ap[:, 0:ncq], in0=pe_.ap[:, 0:ncq], scalar1=mx.ap[:, 3:4],
                           scalar2=None, op0=ALU.mult)
                    ptb = self.psb(btr)[:, 0:256].rearrange("p (j n) -> p j n", j=2)
                    for j in range(nblk):
                        self.TR(ptb[:, j, :], pc_.ap[:, j * 128:(j + 1) * 128], [pc_.b], [self.pb[btr]])
                    self.ACT(pt_.ap[:, 0:nblk, :], ptb[:, 0:nblk, :], AF.Copy, [self.pb[btr]], [pt_.b])
                    for j in range(nblk):
                        self.MM(self.psf(bo)[:, 0:64], pt_.ap[:, j, :], vcmp.ap[:, g, j, :], j == 0, j == nblk - 1,
                                [pt_.b, vcmp.b], [self.pb[bo]])
                    for j in range(nblk):
                        self.MM(self.psf(bimp)[:, 0:64], pt_.ap[:, j, :], cover.ap[:, j, :], hh == 0 and j == 0,
                                hh == 3 and j == nblk - 1, [pt_.b, cover.b], [self.pb[bimp]])
                    self.ACT(onsa.ap[:, qt, h * 64:(h + 1) * 64], self.psf(bo)[:, 0:64], AF.Copy, [self.pb[bo], gates.b], [onsab[qt]],
                             scale=gates.ap[:, qt, h * 3:h * 3 + 1])
                sct, s2, m8a, sbb = sc_t[qt % 2], s2_t[qt % 2], m8[qt % 2], sbf[qt % 2]
                t = small()
                self.V("tensor_tensor", [self.pb[bimp], sa.b], [sct.b], out=sct.ap, in0=self.psf(bimp)[:, 0:64], in1=sa.ap, op=ALU.mult)
                self.V("tensor_tensor", [sct.b, sb_.b], [sct.b], out=sct.ap, in0=sct.ap, in1=sb_.ap, op=ALU.add)
                self.V("max", [sct.b], [m8a.b], out=m8a.ap, in_=sct.ap)
                self.V("tensor_reduce", [m8a.b], [t.b], out=t.ap[:, 0:1], in_=m8a.ap, axis=AX.X, op=ALU.min)
                self.V("tensor_scalar", [sct.b, t.b], [s2.b], out=s2.ap, in0=sct.ap, scalar1=t.ap[:, 0:1], scalar2=-1e9,
                       op0=ALU.is_ge, op1=ALU.mult)
                self.V("tensor_tensor", [s2.b, sct.b], [s2.b], out=s2.ap, in0=s2.ap, in1=sct.ap, op=ALU.add)
                self.V("max", [s2.b], [m8a.b], out=m8a.ap, in_=s2.ap)
                self.V("tensor_reduce", [m8a.b], [t.b], out=t.ap[:, 1:2], in_=m8a.ap, axis=AX.X, op=ALU.min)
                self.V("tensor_scalar", [t.b], [t.b], out=t.ap[:, 2:3], in0=t.ap[:, 1:2], scalar1=0.0, scalar2=None, op0=ALU.max)
                self.V("tensor_scalar", [sct.b, t.b], [s2.b], out=s2.ap, in0=sct.ap, scalar1=t.ap[:, 2:3], scalar2=None, op0=ALU.is_ge)
                self.V("tensor_scalar", [s2.b], [sbb.b], out=sbb.ap, in0=s2.ap, scalar1=-1.0, scalar2=-NEG, op0=ALU.add, op1=ALU.mult)
                btr = 2 + qt % 2
                self.TR(self.psb(btr)[0:64, 0:128], sbb.ap, [sbb.b], [self.pb[btr]])
                self.V("tensor_copy", [self.pb[btr]], [selbT.b], out=selbT.ap[:, qt * 128:(qt + 1) * 128], in_=self.psb(btr)[0:64, 0:128])
            for hh in range(4):
                h = g * 4 + hh
                for qc in range(NQC):
                    bo = 4 + qc % 2
                    O = self.psf(bo)[:, 0:260].rearrange("p (j e) -> p j e", j=4)
                    nk = 4 * qc + 4

                    def emit_sc(kt, hh=hh, qc=qc):
                        bsc = kt % 3
                        sc = self.psf(bsc)
                        near = kt >= 4 * qc - 1
                        self.MM(sc, kslc.ap[:, kt * 128:(kt + 1) * 128], qT.ap[:, hh, qc * 512:(qc + 1) * 512], True, False,
                                [kslc.b, qT.b], [self.pb[bsc]])
                        self.MM(sc, ebig.ap[:, kt * 128:(kt + 1) * 128], selbT.ap[:, qc * 512:(qc + 1) * 512], False, not near,
                                [ebig.b, selbT.b], [self.pb[bsc]])
                        if near:
                            o = kt - 4 * qc
                            self.MM(sc, self.identb.ap, strip.ap[:, hh, (3 - o) * 128:(3 - o) * 128 + 512], False, True,
                                    [self.identb.b, strip.b], [self.pb[bsc]])
                        pT = pTs[kt % 3]
                        self.ACT(pT.ap, sc, AF.Exp, [self.pb[bsc]], [pT.b])
                        return pT

                    def emit_pv(kt, pT, qc=qc, O=O, bo=bo):
                        for j in range(4):
                            qt = 4 * qc + j
                            if kt > qt:
                                continue
                            self.MM(O[:, j, :], pT.ap[:, j * 128:(j + 1) * 128], vslc.ap[:, kt, :], kt == 0 and j == 0, kt == qt,
                                    [pT.b, vslc.b], [self.pb[bo]])
                    pend = None
                    for kt in range(nk):
                        pT = emit_sc(kt)
                        if pend is not None:
                            emit_pv(*pend)
                        pend = (kt, pT)
                    emit_pv(*pend)
                    for j in range(4):
                        qt = 4 * qc + j
                        t = small()
                        self.V("reciprocal", [self.pb[bo]], [t.b], out=t.ap[:, 0:1], in_=O[:, j, 64:65])
                        self.V("tensor_tensor", [t.b, gates.b], [t.b], out=t.ap[:, 1:2], in0=t.ap[:, 0:1],
                               in1=gates.ap[:, qt, h * 3 + 1:h * 3 + 2], op=ALU.mult)
                        osl = onsa.ap[:, qt, h * 64:(h + 1) * 64]
                        self.V("scalar_tensor_tensor", [self.pb[bo], t.b, onsab[qt]], [onsab[qt]], out=osl, in0=O[:, j, 0:64],
                               scalar=t.ap[:, 1:2], in1=osl, op0=ALU.mult, op1=ALU.add)
                def win_a(qt, hh=hh):
                    main = [kt for kt in range(qt - 3, qt + 1) if kt >= 0]
                    far = qt - 4
                    pTf = None
                    if far >= 0:
                        bsc = 3
                        sc = self.psf(bsc)[:, 0:128]
                        self.MM(sc, kwin.ap[:, far * 128:(far + 1) * 128], qT.ap[:, hh, qt * 128:(qt + 1) * 128], True, False,
                                [kwin.b, qT.b], [self.pb[bsc]])
                        self.MM(sc, self.identb.ap, emask.ap, False, True, [self.identb.b, emask.b], [self.pb[bsc]])
                        pTf = pTfar[qt % 2]
                        self.ACT(pTf.ap, sc, AF.Exp, [self.pb[bsc]], [pTf.b])
                    bsc = qt % 3
                    sc = self.psf(bsc)
                    for i, kt in enumerate(main):
                        nb_ = kt >= qt - 1
                        self.MM(sc[:, i * 128:(i + 1) * 128], kwin.ap[:, kt * 128:(kt + 1) * 128], qT.ap[:, hh, qt * 128:(qt + 1) * 128],
                                True, not nb_, [kwin.b, qT.b], [self.pb[bsc]])
                        if nb_:
                            o = qt - kt
                            self.MM(sc[:, i * 128:(i + 1) * 128], self.identb.ap, strip.ap[:, hh, (3 + o) * 128:(4 + o) * 128], False, True,
                                    [self.identb.b, strip.b], [self.pb[bsc]])
                    pT = pTs[qt % 3]
                    nm_ = len(main) * 128
                    self.ACT(pT.ap[:, 0:nm_], sc[:, 0:nm_], AF.Exp, [self.pb[bsc]], [pT.b])
                    return (qt, main, far, pTf, pT)

                def win_b(qt, main, far, pTf, pT, hh=hh, h=h):
                    bo = 6 + qt % 2
                    O = self.psf(bo)[:, 0:65]
                    nmm = len(main) + (1 if far >= 0 else 0)
                    done = 0
                    if far >= 0:
                        self.MM(O, pTf.ap, vwin.ap[:, far, :], True, False, [pTf.b, vwin.b], [self.pb[bo]])
                        done = 1
                    for i, kt in enumerate(main):
                        self.MM(O, pT.ap[:, i * 128:(i + 1) * 128], vwin.ap[:, kt, :], done == 0, done == nmm - 1,
                                [pT.b, vwin.b], [self.pb[bo]])
                        done += 1
                    t = small()
                    self.V("reciprocal", [self.pb[bo]], [t.b], out=t.ap[:, 0:1], in_=O[:, 64:65])
                    self.V("tensor_tensor", [t.b, gates.b], [t.b], out=t.ap[:, 1:2], in0=t.ap[:, 0:1],
                           in1=gates.ap[:, qt, h * 3 + 2:h * 3 + 3], op=ALU.mult)
                    osl = onsa.ap[:, qt, h * 64:(h + 1) * 64]
                    self.V("scalar_tensor_tensor", [self.pb[bo], t.b, onsab[qt]], [onsab[qt]], out=osl, in0=O[:, 0:64],
                           scalar=t.ap[:, 1:2], in1=osl, op0=ALU.mult, op1=ALU.add)
                pend = None
                for qt in range(NT):
                    st_ = win_a(qt)
                    if pend is not None:
                        win_b(*pend)
                    pend = st_
                win_b(*pend)
        self.release(m0)
        oT = A.alloc([128, 4, S], BF16)
        ob = [A.alloc([128, 512], BF16) for _ in range(2)]
        for qt in range(NT):
            o_ = ob[qt % 2]
            self.V("tensor_copy", [onsab[qt]], [o_.b], out=o_.ap, in_=onsa.ap[:, qt, :])
            bk = qt % 2
            pt = self.psb(bk)[:, 0:512].rearrange("p (k n) -> p k n", k=4)
            for k in range(4):
                self.TR(pt[:, k, :], o_.ap[:, k * 128:(k + 1) * 128], [o_.b], [self.pb[bk]])
            self.ACT(oT.ap[:, :, qt * 128:(qt + 1) * 128], pt, AF.Copy, [self.pb[bk]], [oT.b])
        for k in range(4):
            self.ST(self.d_onsaT[k * 128:(k + 1) * 128, :], oT.ap[:, k, :], R=[oT.b], W=[self.Db["onsaT"]])
        if "onsa_dbg" in self.debug:
            self.ST(self.d_onsa_dbg.rearrange("(t p) c -> p t c", p=128), onsa.ap, R=onsab, W=[Buf()])
        self.phase_end()
    def ph_mla(self, l):
        A, I, C = self.A, self.I, self.C
        qs = 192.0 ** -0.5
        cqT = A.alloc([128, 3, S], BF16)
        ckvT = A.alloc([128, 2, S], BF16)
        self.cur_x_b = self.Db["misc"]
        self.norm_to_hT(self.d_misc[:, 24:408], I["mla_norm_q"][l], cqT, width=384)
        self.norm_to_hT(self.d_misc[:, 408:664], I["mla_norm_kv"][l], ckvT, width=256)
        tt = [A.alloc([32, 512], F32) for _ in range(4)]
        cs = A.alloc([32, 2, S], F32)
        rst = [A.alloc([32, 2, S], BF16) for _ in range(2)]

        def rope(x1, x2, R1, R2, co, si, o1, o2, ob, n):
            t1, t2, t3, t4 = tt
            self.V("tensor_tensor", R1 + [cs.b], [t1.b], out=t1.ap[:, 0:n], in0=x1, in1=co, op=ALU.mult)
            self.V("tensor_tensor", R2 + [cs.b], [t2.b], out=t2.ap[:, 0:n], in0=x2, in1=si, op=ALU.mult)
            self.V("tensor_tensor", [t1.b, t2.b], [ob], out=o1, in0=t1.ap[:, 0:n], in1=t2.ap[:, 0:n], op=ALU.subtract)
            self.V("tensor_tensor", R2 + [cs.b], [t3.b], out=t3.ap[:, 0:n], in0=x2, in1=co, op=ALU.mult)
            self.V("tensor_tensor", R1 + [cs.b], [t4.b], out=t4.ap[:, 0:n], in0=x1, in1=si, op=ALU.mult)
            self.V("tensor_tensor", [t3.b, t4.b], [ob], out=o2, in0=t3.ap[:, 0:n], in1=t4.ap[:, 0:n], op=ALU.add)

        m0 = A.mark()
        for i in range(2):
            self.LD(cs.ap[:, i, :], self.d_cs[i], R=[self.Db["cs"]], W=[cs.b])
        kr = A.alloc([32, 2, S], F32)
        for hf in range(2):
            self.LD(kr.ap[:, hf, :], self.d_krT[hf], R=[self.Db["krT"]], W=[kr.b])
        st = rst[0]
        for tc in range(NQC):
            sl = slice(tc * 512, (tc + 1) * 512)
            rope(kr.ap[:, 0, sl], kr.ap[:, 1, sl], [kr.b], [kr.b], cs.ap[:, 0, sl], cs.ap[:, 1, sl], st.ap[:, 0, sl], st.ap[:, 1, sl],
                 st.b, 512)
        self.ST(self.d_kpe[0], st.ap[:, 0, :], R=[st.b], W=[self.Db["kpe"]])
        self.ST(self.d_kpe[1], st.ap[:, 1, :], R=[st.b], W=[self.Db["kpe"]])
        self.release(m0)
        for i in range(2):
            self.LD(cs.ap[:, i, :], self.d_cs[2 + i], R=[self.Db["cs"]], W=[cs.b])
        stgw = A.alloc([128, 3 * 768], F32)
        wq = A.alloc([128, 3, 768], BF16)
        self.load_weight_bf16(wq, I["w_uq"][l], stgw, 3, 768)
        stgk = A.alloc([128, 2 * 1024], F32)
        wkv = A.alloc([128, 2, 1024], BF16)
        self.load_weight_bf16(wkv, I["w_ukv"][l], stgk, 2, 1024)
        stg = [A.alloc([128, S], BF16) for _ in range(2)]
        self._s = 0

        def nstg():
            s = stg[self._s % 2]
            self._s += 1
            return s
        self._pbk = 0

        def nb():
            b = self._pbk % 4
            self._pbk += 1
            return b

        for h in range(4):
            st = nstg()
            for tc in range(NQC):
                bk = nb()
                for k in range(3):
                    self.MM(self.psf(bk), wq.ap[:, k, h * 192:h * 192 + 128], cqT.ap[:, k, tc * 512:(tc + 1) * 512], k == 0, k == 2,
                            [wq.b, cqT.b], [self.pb[bk]])
                self.ACT(st.ap[:, tc * 512:(tc + 1) * 512], self.psf(bk), AF.Copy, [self.pb[bk]], [st.b], scale=qs)
            self.ST(self.d_qn[h], st.ap, R=[st.b], W=[self.Db["qn"]])
            st = rst[h % 2]
            for tc in range(NQC):
                b1, b2 = nb(), nb()
                sl = slice(tc * 512, (tc + 1) * 512)
                for hf, bk in ((0, b1), (1, b2)):
                    c0 = h * 192 + 128 + hf * 32
                    for k in range(3):
                        self.MM(self.psf(bk)[0:32, :], wq.ap[:, k, c0:c0 + 32], cqT.ap[:, k, sl], k == 0, k == 2,
                                [wq.b, cqT.b], [self.pb[bk]])
                rope(self.psf(b1)[0:32, :], self.psf(b2)[0:32, :], [self.pb[b1]], [self.pb[b2]], cs.ap[:, 0, sl], cs.ap[:, 1, sl],
                     st.ap[:, 0, sl], st.ap[:, 1, sl], st.b, 512)
            self.ST(self.d_qpe[h, 0], st.ap[:, 0, :], R=[st.b], W=[self.Db["qpe"]])
            self.ST(self.d_qpe[h, 1], st.ap[:, 1, :], R=[st.b], W=[self.Db["qpe"]])
            st = nstg()
            for tc in range(NQC):
                bk = nb()
                for k in range(2):
                    self.MM(self.psf(bk), wkv.ap[:, k, h * 256:h * 256 + 128], ckvT.ap[:, k, tc * 512:(tc + 1) * 512], k == 0, k == 1,
                            [wkv.b, ckvT.b], [self.pb[bk]])
                self.V("tensor_copy", [self.pb[bk]], [st.b], out=st.ap[:, tc * 512:(tc + 1) * 512], in_=self.psf(bk))
            self.ST(self.d_kn[h], st.ap, R=[st.b], W=[self.Db["kn"]])
        wv = A.alloc([128, 2, 512], BF16)
        for h in range(4):
            self.G("tensor_copy", [wkv.b], [wv.b], out=wv.ap[:, :, h * 128:(h + 1) * 128], in_=wkv.ap[:, :, h * 256 + 128:h * 256 + 256])
        vst = [A.alloc([128, 4, 129], BF16) for _ in range(2)]
        for v_ in vst:
            self.V("memset", [], [v_.b], ap=v_.ap, constant=1.0)
        for t in range(NT):
            bk = 4 + t % 2
            vs = vst[t % 2]
            for k in range(2):
                self.MM(self.psf(bk), ckvT.ap[:, k, t * 128:(t + 1) * 128], wv.ap[:, k, :], k == 0, k == 1, [ckvT.b, wv.b], [self.pb[bk]])
            self.V("tensor_copy", [self.pb[bk]], [vs.b], out=vs.ap[:, :, 0:128], in_=self.psf(bk).rearrange("p (h d) -> p h d", h=4))
            self.ST(self.d_vmla[t * 128:(t + 1) * 128], vs.ap, R=[vs.b], W=[self.Db["vmla"]])
        self.phase_end()
        cstrip = A.alloc([128, 896], BF16)
        cmf = A.alloc([128, 128], F32)
        self.LD(cmf.ap, C["cmask"], W=[cmf.b])
        self.G("memset", [], [cstrip.b], ap=cstrip.ap[:, 0:384], constant=NEG)
        self.G("memset", [], [cstrip.b], ap=cstrip.ap[:, 512:896], constant=0.0)
        self.G("tensor_copy", [cmf.b], [cstrip.b], out=cstrip.ap[:, 384:512], in_=cmf.ap)
        kpe = A.alloc([32, 2, S], BF16)
        for hf in range(2):
            self.LD(kpe.ap[:, hf, :], self.d_kpe[hf], R=[self.Db["kpe"]], W=[kpe.b])
        qn = A.alloc([128, S], BF16)
        kn = A.alloc([128, S], BF16)
        qpe = A.alloc([32, 2, S], BF16)
        vv = A.alloc([128, NT, 129], BF16)
        oT = A.alloc([128, S], BF16)
        pTs = [A.alloc([128, 512], BF16) for _ in range(3)]
        ob = [A.alloc([128, 128], BF16) for _ in range(2)]
        sm = [A.alloc([128, 4], F32) for _ in range(4)]
        vview = self.d_vmla.rearrange("(t p) h e -> p t h e", p=128)
        smi = 0
        for h in range(4):
            self.LD(qn.ap, self.d_qn[h], R=[self.Db["qn"]], W=[qn.b])
            self.LD(kn.ap, self.d_kn[h], R=[self.Db["kn"]], W=[kn.b])
            for hf in range(2):
                self.LD(qpe.ap[:, hf, :], self.d_qpe[h, hf], R=[self.Db["qpe"]], W=[qpe.b])
            self.LD(vv.ap, vview[:, :, h, :], R=[self.Db["vmla"]], W=[vv.b])
            for qc in range(NQC):
                ba = 4 + 2 * (qc % 2)
                Oj = [self.psf(ba + j // 2)[:, (j % 2) * 129:(j % 2) * 129 + 129] for j in range(4)]
                Ob = [self.pb[ba + j // 2] for j in range(4)]
                qsl = slice(qc * 512, (qc + 1) * 512)
                def emit_sc(kt, qc=qc, qsl=qsl):
                    bsc = kt % 3
                    sc = self.psf(bsc)
                    ksl = slice(kt * 128, (kt + 1) * 128)
                    diag = kt >= 4 * qc
                    self.MM(sc, kn.ap[:, ksl], qn.ap[:, qsl], True, False, [kn.b, qn.b], [self.pb[bsc]])
                    self.MM(sc, kpe.ap[:, 0, ksl], qpe.ap[:, 0, qsl], False, False, [kpe.b, qpe.b], [self.pb[bsc]])
                    self.MM(sc, kpe.ap[:, 1, ksl], qpe.ap[:, 1, qsl], False, not diag, [kpe.b, qpe.b], [self.pb[bsc]])
                    if diag:
                        o = kt - 4 * qc
                        self.MM(sc, self.identb.ap, cstrip.ap[:, (3 - o) * 128:(3 - o) * 128 + 512], False, True,
                                [self.identb.b, cstrip.b], [self.pb[bsc]])
                    pT = pTs[kt % 3]
                    self.ACT(pT.ap, sc, AF.Exp, [self.pb[bsc]], [pT.b])
                    return pT

                def emit_pv(kt, pT, qc=qc, Oj=Oj, Ob=Ob):
                    for j in range(4):
                        qt = 4 * qc + j
                        if kt > qt:
                            continue
                        self.MM(Oj[j], pT.ap[:, j * 128:(j + 1) * 128], vv.ap[:, kt, :], kt == 0 and j % 2 == 0, kt == qt, [pT.b, vv.b], [Ob[j]])
                pend = None
                for kt in range(4 * qc + 4):
                    pT = emit_sc(kt)
                    if pend is not None:
                        emit_pv(*pend)
                    pend = (kt, pT)
                emit_pv(*pend)
                for j in range(4):
                    qt = 4 * qc + j
                    t = sm[smi % 4]
                    smi += 1
                    o_ = ob[qt % 2]
                    self.V("reciprocal", [Ob[j]], [t.b], out=t.ap[:, 0:1], in_=Oj[j][:, 128:129])
                    self.V("tensor_scalar", [Ob[j], t.b], [o_.b], out=o_.ap, in0=Oj[j][:, 0:128], scalar1=t.ap[:, 0:1], scalar2=None,
                           op0=ALU.mult)
                    bt = 3
                    self.TR(self.psb(bt)[:, 0:128], o_.ap, [o_.b], [self.pb[bt]])
                    self.ACT(oT.ap[:, qt * 128:(qt + 1) * 128], self.psb(bt)[:, 0:128], AF.Copy, [self.pb[bt]], [oT.b])
            self.ST(self.d_omlaT[h * 128:(h + 1) * 128, :], oT.ap, R=[oT.b], W=[self.Db["omlaT"]])
        self.phase_end()

    def ph_merge(self, l, xsrc, xsrc_b):
        A, I = self.A, self.I
        stg = A.alloc([128, 4096], F32)
        wb = {}
        for nm in ("w_branch_conv", "w_branch_nsa", "w_branch_mla"):
            wb[nm] = A.alloc([128, 4, 1024], BF16)
            self.load_weight_bf16(wb[nm], I[nm][l], stg, 4, 1024)
        wo = A.alloc([128, 8, 1024], BF16)
        for hf in range(2):
            sv = stg.ap.rearrange("p (k n) -> p k n", k=4)
            self.LD(sv, I["w_out"][l][hf * 512:(hf + 1) * 512, :].rearrange("(k p) n -> p k n", p=128), R=[wo.b], W=[stg.b])
            self.G("tensor_copy", [stg.b], [wo.b], out=wo.ap[:, hf * 4:(hf + 1) * 4, :], in_=sv)
        srcs = [("w_branch_conv", self.d_uactT, "uactT"), ("w_branch_nsa", self.d_onsaT, "onsaT"), ("w_branch_mla", self.d_omlaT, "omlaT")]
        acts = [[A.alloc([128, 4, 512], BF16) for _ in range(2)] for _ in range(3)]
        gms = [A.alloc([128, 24, 512], BF16) for _ in range(2)]
        mT = [A.alloc([128, 8, 512], BF16) for _ in range(2)]
        ta = [A.alloc([128, 512], F32) for _ in range(2)]
        tb = [A.alloc([128, 512], F32) for _ in range(2)]
        xts = [A.alloc([128, 1024], F32) for _ in range(2)]
        xos = [A.alloc([128, 1024], F32) for _ in range(2)]
        gv = self.d_gmT.rearrange("(b p) s -> p b s", p=128)
        n = 0
        for tc in range(NQC):
            sl = slice(tc * 512, (tc + 1) * 512)
            i2 = tc % 2
            for si, (wn, dsrc, dn) in enumerate(srcs):
                self.LD(acts[si][i2].ap, dsrc.rearrange("(k p) s -> p k s", p=128)[:, :, sl], R=[self.Db[dn]], W=[acts[si][i2].b])
            gm = gms[i2]
            self.LD(gm.ap, gv[:, :, sl], R=[self.Db["gmT"]], W=[gm.b])
            m_ = mT[i2]
            for fc in range(8):
                a_, b_ = ta[fc % 2], tb[fc % 2]
                for si, (wn, dsrc, dn) in enumerate(srcs):
                    bk = n % 4
                    n += 1
                    for k in range(4):
                        self.MM(self.psf(bk), wb[wn].ap[:, k, fc * 128:(fc + 1) * 128], acts[si][i2].ap[:, k, :], k == 0, k == 3,
                                [wb[wn].b, acts[si][i2].b], [self.pb[bk]])
                    dst = a_ if si == 0 else b_
                    self.V("tensor_tensor", [self.pb[bk], gm.b], [dst.b], out=dst.ap, in0=self.psf(bk), in1=gm.ap[:, si * 8 + fc, :], op=ALU.mult)
                    if si == 1:
                        self.G("tensor_tensor", [a_.b, b_.b], [a_.b], out=a_.ap, in0=a_.ap, in1=b_.ap, op=ALU.add)
                    if si == 2:
                        self.G("tensor_tensor", [a_.b, b_.b], [m_.b], out=m_.ap[:, fc, :], in0=a_.ap, in1=b_.ap, op=ALU.add)
            for tt_ in range(4):
                t = tc * 4 + tt_
                xt, xo = xts[t % 2], xos[t % 2]
                self.LD(xt.ap, xsrc[t * 128:(t + 1) * 128, :], R=[xsrc_b], W=[xt.b])
                for cg in range(2):
                    bk = 4 + (t * 2 + cg) % 4
                    for k in range(8):
                        self.MM(self.psf(bk), m_.ap[:, k, tt_ * 128:(tt_ + 1) * 128], wo.ap[:, k, cg * 512:(cg + 1) * 512], k == 0, k == 7,
                                [m_.b, wo.b], [self.pb[bk]])
                    self.V("tensor_tensor", [self.pb[bk], xt.b], [xo.b], out=xo.ap[:, cg * 512:(cg + 1) * 512], in0=self.psf(bk),
                           in1=xt.ap[:, cg * 512:(cg + 1) * 512], op=ALU.add)
                self.ST(self.d_xres[t * 128:(t + 1) * 128, :], xo.ap, R=[xo.b], W=[self.Db["xres"]])
        self.phase_end()

    def ph_xattn(self, l):
        A, I = self.A, self.I
        xs = 128.0 ** -0.5
        hT = A.alloc([128, 8, S], BF16)
        self.cur_x_b = self.Db["xres"]
        self.norm_to_hT(self.d_xres, I["norm_xattn"][l], hT)
        memT = A.alloc([128, 8, 256], BF16)
        self.cur_x_b = Buf()
        self.norm_to_hT(I["mem"], I["norm_mem"][l], memT, ntile=2, pbank=2)
        stg = A.alloc([128, 8 * 512], F32)
        wq = A.alloc([128, 8, 512], BF16)
        self.load_weight_bf16(wq, I["w_xq"][l], stg, 8, 512)
        wkv = A.alloc([128, 8, 1024], BF16)
        for hf in range(2):
            sv = stg.ap.rearrange("p (k n) -> p k n", k=8)
            self.LD(sv, I["w_xkv"][l][:, hf * 512:(hf + 1) * 512].rearrange("(k p) n -> p k n", p=128), R=[wkv.b], W=[stg.b])
            self.G("tensor_copy", [stg.b], [wkv.b], out=wkv.ap[:, :, hf * 512:(hf + 1) * 512], in_=sv)
        wo = A.alloc([128, 4, 1024], BF16)
        self.load_weight_bf16(wo, I["w_xo"][l], stg, 4, 1024)
        kT = A.alloc([128, 4, 256], BF16)
        vv = A.alloc([128, 2, 4, 129], BF16)
        self.V("memset", [], [vv.b], ap=vv.ap, constant=1.0)
        for h in range(4):
            for k in range(8):
                self.MM(self.psf(0)[:, 0:256], wkv.ap[:, k, h * 128:(h + 1) * 128], memT.ap[:, k, :], k == 0, k == 7, [wkv.b, memT.b], [self.pb[0]])
            self.V("tensor_copy", [self.pb[0]], [kT.b], out=kT.ap[:, h, :], in_=self.psf(0)[:, 0:256])
        for mt in range(2):
            for k in range(8):
                self.MM(self.psf(1), memT.ap[:, k, mt * 128:(mt + 1) * 128], wkv.ap[:, k, 512:1024], k == 0, k == 7, [wkv.b, memT.b], [self.pb[1]])
            self.V("tensor_copy", [self.pb[1]], [vv.b], out=vv.ap[:, mt, :, 0:128], in_=self.psf(1).rearrange("p (h d) -> p h d", h=4))
        qTs = [A.alloc([128, 512], BF16) for _ in range(2)]
        pTs = [A.alloc([128, 2, 512], BF16) for _ in range(2)]
        self._ox = [A.alloc([128, 512], BF16) for _ in range(4)]
        oxT = [A.alloc([128, 4, 128], BF16) for _ in range(2)]
        sm = [A.alloc([128, 4], F32) for _ in range(4)]
        xts = [A.alloc([128, 1024], F32) for _ in range(2)]
        xos = [A.alloc([128, 1024], F32) for _ in range(2)]
        smi = 0
        n = 0
        for tc in range(NQC):
            sl = slice(tc * 512, (tc + 1) * 512)
            for h in range(4):
                qT = qTs[h % 2]
                for k in range(8):
                    self.MM(self.psf(0), wq.ap[:, k, h * 128:(h + 1) * 128], hT.ap[:, k, sl], k == 0, k == 7, [wq.b, hT.b], [self.pb[0]])
                self.ACT(qT.ap, self.psf(0), AF.Copy, [self.pb[0]], [qT.b], scale=xs)
                pT = pTs[h % 2]
                for mt in range(2):
                    bsc = 1 + mt
                    self.MM(self.psf(bsc), kT.ap[:, h, mt * 128:(mt + 1) * 128], qT.ap, True, True, [kT.b, qT.b], [self.pb[bsc]])
                    self.ACT(pT.ap[:, mt, :], self.psf(bsc), AF.Exp, [self.pb[bsc]], [pT.b])
                for j in range(4):
                    bk = 4 + (h % 2) * 2 + j // 2
                    Oj = self.psf(bk)[:, (j % 2) * 129:(j % 2) * 129 + 129]
                    for mt in range(2):
                        self.MM(Oj, pT.ap[:, mt, j * 128:(j + 1) * 128], vv.ap[:, mt, h, :], mt == 0 and j % 2 == 0, mt == 1, [pT.b, vv.b], [self.pb[bk]])
                    t = sm[smi % 4]
                    smi += 1
                    self.V("reciprocal", [self.pb[bk]], [t.b], out=t.ap[:, 0:1], in_=Oj[:, 128:129])
                    ox = self._ox[j]
                    self.V("tensor_scalar", [self.pb[bk], t.b], [ox.b], out=ox.ap[:, h * 128:(h + 1) * 128], in0=Oj[:, 0:128],
                           scalar1=t.ap[:, 0:1], scalar2=None, op0=ALU.mult)
            for j in range(4):
                t = tc * 4 + j
                ox = self._ox[j]
                oT_ = oxT[t % 2]
                bt = 3
                pt = self.psb(bt)[:, 0:512].rearrange("p (k n) -> p k n", k=4)
                for k in range(4):
                    self.TR(pt[:, k, :], ox.ap[:, k * 128:(k + 1) * 128], [ox.b], [self.pb[bt]])
                self.ACT(oT_.ap, pt, AF.Copy, [self.pb[bt]], [oT_.b])
                xt, xo = xts[t % 2], xos[t % 2]
                self.LD(xt.ap, self.d_xres[t * 128:(t + 1) * 128, :], R=[self.Db["xres"]], W=[xt.b])
                for cg in range(2):
                    bk = 6 + cg
                    for k in range(4):
                        self.MM(self.psf(bk), oT_.ap[:, k, :], wo.ap[:, k, cg * 512:(cg + 1) * 512], k == 0, k == 3, [oT_.b, wo.b], [self.pb[bk]])
                    self.V("tensor_tensor", [self.pb[bk], xt.b], [xo.b], out=xo.ap[:, cg * 512:(cg + 1) * 512], in0=self.psf(bk),
                           in1=xt.ap[:, cg * 512:(cg + 1) * 512], op=ALU.add)
                self.ST(self.d_xres[t * 128:(t + 1) * 128, :], xo.ap, R=[xo.b], W=[self.Db["xres"]])
        self.phase_end()

    def ph_ffn(self, l):
        A, I = self.A, self.I
        hT = A.alloc([128, 8, S], BF16)
        self.cur_x_b = self.Db["xres"]
        self.norm_to_hT(self.d_xres, I["norm_ffn"][l], hT)
        wg = A.alloc([128, 8, 1408], BF16)
        wu = A.alloc([128, 8, 1408], BF16)
        wd = A.alloc([128, 11, 1024], BF16)
        stg = [A.alloc([128, 3072], F32) for _ in range(2)]
        actT = [A.alloc([128, 11, 512], BF16) for _ in range(1)]
        sg = [A.alloc([128, 512], F32) for _ in range(2)]
        xts = [A.alloc([128, 1024], F32) for _ in range(2)]
        xos = [A.alloc([128, 1024], F32) for _ in range(2)]
        w_gu = I["w_gate_up"][l]
        w_dn = I["w_down"][l]
        for ps_ in range(2):
            f0 = ps_ * 1408
            u = 0
            for dst, base in ((wg, 0), (wu, FFN)):
                for q4 in range(4):
                    s_ = stg[u % 2]
                    u += 1
                    sv = s_.ap[:, 0:2816].rearrange("p (k n) -> p k n", k=8)
                    c0 = base + f0 + q4 * 352
                    self.LD(sv, w_gu[:, c0:c0 + 352].rearrange("(k p) n -> p k n", p=128), R=[dst.b], W=[s_.b])
                    self.G("tensor_copy", [s_.b], [dst.b], out=dst.ap[:, :, q4 * 352:(q4 + 1) * 352], in_=sv)
            for q4 in range(4):
                s_ = stg[u % 2]
                u += 1
                nk = 3 if q4 < 3 else 2
                sv = s_.ap[:, 0:nk * 1024].rearrange("p (k n) -> p k n", k=nk)
                r0 = f0 + q4 * 384
                self.LD(sv, w_dn[r0:r0 + nk * 128, :].rearrange("(k p) n -> p k n", p=128), R=[wd.b], W=[s_.b])
                self.G("tensor_copy", [s_.b], [wd.b], out=wd.ap[:, q4 * 3:q4 * 3 + nk, :], in_=sv)
            n = 0
            for tc in range(NQC):
                sl = slice(tc * 512, (tc + 1) * 512)
                aT = actT[0]
                for f in range(11):
                    bg, bu = (n % 2) * 2, (n % 2) * 2 + 1
                    n += 1
                    for k in range(8):
                        self.MM(self.psf(bg), wg.ap[:, k, f * 128:(f + 1) * 128], hT.ap[:, k, sl], k == 0, k == 7, [wg.b, hT.b], [self.pb[bg]])
                    for k in range(8):
                        self.MM(self.psf(bu), wu.ap[:, k, f * 128:(f + 1) * 128], hT.ap[:, k, sl], k == 0, k == 7, [wu.b, hT.b], [self.pb[bu]])
                    s = sg[f % 2]
                    self.ACT(s.ap, self.psf(bg), AF.Silu, [self.pb[bg]], [s.b])
                    self.V("tensor_tensor", [self.pb[bu], s.b], [aT.b], out=aT.ap[:, f, :], in0=self.psf(bu), in1=s.ap, op=ALU.mult)
                for tt_ in range(4):
                    t = tc * 4 + tt_
                    xt, xo = xts[t % 2], xos[t % 2]
                    self.LD(xt.ap, self.d_xres[t * 128:(t + 1) * 128, :], R=[self.Db["xres"]], W=[xt.b])
                    for cg in range(2):
                        bk = 4 + (t * 2 + cg) % 4
                        for f in range(11):
                            self.MM(self.psf(bk), aT.ap[:, f, tt_ * 128:(tt_ + 1) * 128], wd.ap[:, f, cg * 512:(cg + 1) * 512], f == 0, f == 10,
                                    [aT.b, wd.b], [self.pb[bk]])
                        self.V("tensor_tensor", [self.pb[bk], xt.b], [xo.b], out=xo.ap[:, cg * 512:(cg + 1) * 512], in0=self.psf(bk),
                               in1=xt.ap[:, cg * 512:(cg + 1) * 512], op=ALU.add)
                    self.ST(self.d_xres[t * 128:(t + 1) * 128, :], xo.ap, R=[xo.b], W=[self.Db["xres"]])
            self.P.barrier()
        self.phase_end()

    def ph_final(self):
        A, I = self.A, self.I
        gb = A.alloc([128, D], F32)
        self.LD(gb.ap, I["norm_final"].partition_broadcast(128), W=[gb.b])
        xts = [A.alloc([128, D], F32) for _ in range(2)]
        junk = A.alloc([128, D], F32)
        ys = [A.alloc([128, D], F32) for _ in range(2)]
        sss = [A.alloc([128, 1], F32) for _ in range(2)]
        rss = [A.alloc([128, 1], F32) for _ in range(2)]
        yb = Buf()
        for t in range(NT):
            xt, y_, ss, rstd = xts[t % 2], ys[t % 2], sss[t % 2], rss[t % 2]
            self.LD(xt.ap, self.d_xres[t * 128:(t + 1) * 128, :], R=[self.Db["xres"]], W=[xt.b])
            self.rms_rstd(xt, junk, ss, rstd, D)
            self.V("scalar_tensor_tensor", [xt.b, rstd.b, gb.b], [y_.b], out=y_.ap, in0=xt.ap, scalar=rstd.ap[:, 0:1], in1=gb.ap,
                   op0=ALU.mult, op1=ALU.mult)
            self.ST(self.out[t * 128:(t + 1) * 128, :], y_.ap, R=[y_.b], W=[yb])
        self.phase_end()
    def build(self, n_layers=2, stop_after=None):
        d = self.dram
        self.d_xres = d("xres", [S, D], F32)
        self.d_uT = d("uT", [512, S], F32)
        self.d_qnT = d("qnT", [8, 64, S], BF16)
        self.d_kvT = d("kvT", [4, 2, 64, S], BF16)
        self.d_krT = d("krT", [2, 32, S], F32)
        self.d_gmT = d("gmT", [3072, S], BF16)
        self.d_vtm = d("vtm", [S, 4, 65], BF16)
        self.d_misc = d("misc", [S, 664], F32)
        self.d_uactT = d("uactT", [512, S], BF16)
        self.d_onsaT = d("onsaT", [512, S], BF16)
        self.d_omlaT = d("omlaT", [512, S], BF16)
        self.d_vs = d("vs", [8, 128, 384], BF16)
        self.d_bc = d("bc", [8, 2048], BF16)
        self.d_cs = d("cs", [4, 32, S], F32)
        self.d_qn = d("qn", [4, 128, S], BF16)
        self.d_kn = d("kn", [4, 128, S], BF16)
        self.d_qpe = d("qpe", [4, 2, 32, S], BF16)
        self.d_kpe = d("kpe", [2, 32, S], BF16)
        self.d_vmla = d("vmla", [S, 4, 129], BF16)
        if "onsa_dbg" in self.debug:
            self.d_onsa_dbg = d("onsa_dbg", [S, 512], F32)
        self.cur_x_b = Buf()
        self.setup()
        phases = []
        phases.append(("tables", lambda: (self.ph_bias_tables(), self.ph_rope_tables())))
        for l in range(n_layers):
            xsrc = self.I["x"] if l == 0 else self.d_xres
            xb = Buf() if l == 0 else self.Db["xres"]
            phases.append((f"inproj{l}", lambda l=l, xsrc=xsrc, xb=xb: (setattr(self, "cur_x_b", xb), self.ph_inproj(l, xsrc))))
            phases.append((f"conv{l}", lambda l=l: self.ph_conv(l)))
            phases.append((f"nsa{l}", lambda l=l: self.ph_nsa(l)))
            phases.append((f"mla{l}", lambda l=l: self.ph_mla(l)))
            phases.append((f"merge{l}", lambda l=l, xsrc=xsrc, xb=xb: self.ph_merge(l, xsrc, xb)))
            phases.append((f"xattn{l}", lambda l=l: self.ph_xattn(l)))
            phases.append((f"ffn{l}", lambda l=l: self.ph_ffn(l)))
        phases.append(("final", lambda: self.ph_final()))
        skip = set(self.skip)
        for name, fn in phases:
            if name not in skip:
                fn()
            if stop_after == name:
                break
        self.P.finish()


def build_nc(debug=None, n_layers=2, stop_after=None, skip=()):
    nc = bass.Bass("TRN2", target_bir_lowering=False)
    with contextlib.ExitStack() as es:
        k = K(nc, es, debug)
        k.skip = list(skip)
        k.build(n_layers, stop_after)
    return nc, k


def make_in_maps(inputs, consts):
    maps = []
    for b in range(8):
        m = {}
        for n in WEIGHT_SHAPES:
            a = np.asarray(inputs[n])
            if n in ("x", "mem", "positions"):
                a = a[b]
            m[n] = np.ascontiguousarray(a)
        for n, v in consts.items():
            m["c_" + n] = v
        maps.append(m)
    return maps


def kernel(**inputs):
    nc, k = build_nc()
    consts = host_consts()
    in_maps = make_in_maps(inputs, consts)
    res = run_bass_kernel_spmd(nc, in_maps, core_ids=list(range(8)))
    return np.stack([np.asarray(r["y"]) for r in res.results], axis=0).astype(np.float32)
```

```python
import contextlib
import math
import numpy as np
import concourse.bass as bass
import concourse.mybir as mybir
from concourse.ap import AP
from concourse.bass_utils import run_bass_kernel_spmd

F32 = mybir.dt.float32
BF16 = mybir.dt.bfloat16
I32 = mybir.dt.int32
U8 = mybir.dt.uint8
ALU = mybir.AluOpType
AF = mybir.ActivationFunctionType
AX = mybir.AxisListType

ENGS = ["pe", "act", "dve", "pool", "sp"]
N_DSEM = 8
DTSIZE = {F32: 4, BF16: 2, I32: 4, U8: 1}

S = 4096
D = 1024
NT = S // 128
NQC = S // 512
IN_COLS = 6104
FFN = 2816
NEG = -30000.0


class Buf:
    __slots__ = ("w", "r")

    def __init__(self):
        self.w = None
        self.r = {}


class Tl:
    __slots__ = ("ap", "b")

    def __init__(self, ap, b=None):
        self.ap = ap
        self.b = b if b is not None else Buf()


class Prog:
    def __init__(self, nc, es):
        self.nc = nc
        self.es = es
        self.q = {e: [] for e in ENGS}
        self.cnt = {}
        self.sems = {}
        self.known = {e: {} for e in ENGS}
        self.ep = {}
        self.ekey = {}
        for e in ENGS:
            self.ep[e] = 0
            self._new_epoch(e)
        self.dpool = {}
        self.dnext = {}
        for e in ("sp", "pool", "act"):
            self.dpool[e] = []
            for i in range(N_DSEM):
                k = f"d_{e}_{i}"
                self.sems[k] = es.enter_context(nc.semaphore(k))
                self.cnt[k] = 0
                self.dpool[e].append(k)
            self.dnext[e] = 0
        self.nins = 0

    SEM_LIMIT = 12000

    def _new_epoch(self, e):
        key = f"{e}#{self.ep[e]}"
        self.ep[e] += 1
        self.sems[key] = self.es.enter_context(self.nc.semaphore(f"sem_{e}_{self.ep[e]}"))
        self.cnt[key] = 0
        self.ekey[e] = key

    def _need(self, eng, R, W):
        need = {}

        def add(ev):
            if ev is None:
                return
            k, v = ev
            if need.get(k, 0) < v:
                need[k] = v
        for b in R:
            add(b.w)
        for b in W:
            add(b.w)
            for k, v in b.r.items():
                add((k, v))
        kn = self.known[eng]
        for k, v in need.items():
            if eng == "pe" and k.startswith("pe#"):
                continue
            if kn.get(k, 0) >= v:
                continue
            kn[k] = v
            self.q[eng].append(("wait", k, v))

    def _done(self, ev, R, W):
        k, v = ev
        for b in W:
            b.w = ev
            b.r = {}
        for b in R:
            if b.r.get(k, 0) < v:
                b.r[k] = v

    def op(self, eng, fn, R=(), W=()):
        self._need(eng, R, W)
        if self.cnt[self.ekey[eng]] >= self.SEM_LIMIT:
            self._new_epoch(eng)
        key = self.ekey[eng]
        self.cnt[key] += 1
        self.q[eng].append(("ins", fn, key, 1))
        self._done((key, self.cnt[key]), R, W)
        self.nins += 1

    def dma(self, eng, out, in_, R=(), W=(), **kw):
        self._need(eng, R, W)
        pool = self.dpool[eng]
        k = pool[self.dnext[eng] % len(pool)]
        self.dnext[eng] += 1
        if self.cnt[k] > 0 and self.known[eng].get(k, 0) < self.cnt[k]:
            self.known[eng][k] = self.cnt[k]
            self.q[eng].append(("wait", k, self.cnt[k]))
        self.cnt[k] += 16
        self.q[eng].append(("ins", lambda e: e.dma_start(out=out, in_=in_, **kw), k, 16))
        self._done((k, self.cnt[k]), R, W)
        self.nins += 1

    def barrier(self):
        for e in ENGS:
            kn = self.known[e]
            for k, c in self.cnt.items():
                if k.split("#")[0] == e or c == 0:
                    continue
                if kn.get(k, 0) < c:
                    kn[k] = c
                    self.q[e].append(("wait", k, c))

    def cut(self):
        self.barrier()
        for e in ENGS:
            self.q[e].append(("cut",))

    def finish(self):
        self.barrier()
        nc = self.nc
        sems = self.sems
        segs = {}
        nseg = 1
        for e in ENGS:
            cur = []
            segs[e] = [cur]
            for it in self.q[e]:
                if it[0] == "cut":
                    cur = []
                    segs[e].append(cur)
                else:
                    cur.append(it)
            nseg = max(nseg, len(segs[e]))

        def replay(items):
            def f(e):
                for it in items:
                    if it[0] == "wait":
                        e.wait_ge(sems[it[1]], it[2])
                    else:
                        it[1](e).then_inc(sems[it[2]], it[3])
            return f

        for s in range(nseg):
            if not any(len(segs[e][s]) for e in ENGS):
                continue
            with nc.Block() as block:
                block.tensor(replay(segs["pe"][s]))
                block.scalar(replay(segs["act"][s]))
                block.vector(replay(segs["dve"][s]))
                block.gpsimd(replay(segs["pool"][s]))
                block.sync(replay(segs["sp"][s]))


class Arena:
    def __init__(self, t, size):
        self.t = t
        self.size = size
        self.off = 0

    def alloc(self, shape, dtype, parts=None):
        p = shape[0] if parts is None else parts
        n = 1
        for s in shape[1:]:
            n *= s
        nb = n * DTSIZE[dtype]
        nb_al = (nb + 63) // 64 * 64
        assert self.off + nb_al <= self.size, f"SBUF arena overflow {self.off}+{nb_al}>{self.size}"
        v = self.t[0:p, self.off:self.off + nb].bitcast(dtype)
        self.off += nb_al
        if len(shape) == 3:
            v = v.rearrange("p (a b) -> p a b", a=shape[1])
        elif len(shape) == 4:
            v = v.rearrange("p (a b c) -> p a b c", a=shape[1], b=shape[2])
        return Tl(v)

    def mark(self):
        return self.off

    def reset(self, m):
        self.off = m


def t5_bucket_np(d):
    d = np.asarray(d)
    dd = np.maximum(d, 0)
    lr = np.log(np.maximum(dd, 1).astype(np.float32) / np.float32(16)) / np.float32(math.log(8.0))
    large = np.minimum(16 + (lr * 16).astype(np.int32), 31)
    return np.where(dd < 16, dd, large)


def host_consts():
    c = {}
    c["ident"] = np.eye(128, dtype=np.float32)
    oh = np.zeros((33, 384), np.float32)
    for i in range(383):
        d = i - 127
        if d >= 0:
            oh[int(t5_bucket_np(d)), i] += 1.0
            oh[31, i] -= 1.0
        else:
            oh[32, i] = NEG
    c["oh_v"] = oh
    ohc = np.zeros((33, 128 * 16), np.float32)
    for ql in range(128):
        for cp in range(16):
            d = ql - 16 * cp + 97
            j = ql * 16 + cp
            if d >= 0:
                ohc[int(t5_bucket_np(d)), j] += 1.0
                ohc[31, j] -= 1.0
            else:
                ohc[32, j] = NEG
    c["oh_c"] = ohc
    kl = np.arange(128)[:, None]
    ql = np.arange(128)[None, :]
    c["emask"] = np.where(ql >= kl, NEG, 0.0).astype(np.float32)
    c["cmask"] = np.where(ql < kl, NEG, 0.0).astype(np.float32)
    eb = np.zeros((64, S), np.float32)
    for j in range(64):
        eb[j, j * 64:(j + 1) * 64] = 1.0
    c["ebig"] = eb
    cs = 16 * np.arange(255)
    ss = 64 * np.arange(64)
    cov = ((cs[:, None] < ss[None, :] + 64) & (cs[:, None] + 32 > ss[None, :])).astype(np.float32)
    covp = np.zeros((256, 64), np.float32)
    covp[:255] = cov
    c["cover"] = covp
    t = np.arange(S)[:, None]
    j = np.arange(64)[None, :]
    cur = t // 64
    forced = (j == 0) | (j == cur) | (j == cur - 1)
    causal = (64 * j) <= t
    c["selA"] = np.where(forced, 0.0, np.where(causal, 1.0, 0.0)).astype(np.float32)
    c["selB"] = np.where(forced, 1e4 + j, np.where(causal, 0.0, -1.0)).astype(np.float32)
    c["cflag"] = (np.arange(S) >= 31).astype(np.float32).reshape(S, 1)
    half = 32
    inv = (np.float32(10000.0) ** (-np.arange(half, dtype=np.float32) / np.float32(half))).astype(np.float32)
    c["invfreq"] = (inv / np.float32(2 * math.pi)).astype(np.float32).reshape(32, 1)
    return c


CONST_SHAPES = {
    "ident": [128, 128], "oh_v": [33, 384], "oh_c": [33, 2048], "emask": [128, 128], "cmask": [128, 128],
    "ebig": [64, S], "cover": [256, 64], "selA": [S, 64], "selB": [S, 64], "cflag": [S, 1], "invfreq": [32, 1],
}

WEIGHT_SHAPES = {
    "x": [S, D], "mem": [256, D], "positions": [S], "rel_bias": [32, 8],
    "norm_mix": [2, D], "norm_xattn": [2, D], "norm_mem": [2, D], "norm_ffn": [2, D], "norm_final": [D],
    "w_in": [2, D, IN_COLS], "conv_w": [2, 31, 512], "conv_b": [2, 512], "conv_ln_g": [2, 512], "conv_ln_b": [2, 512],
    "w_branch_conv": [2, 512, D],
    "cmp_pos_k": [2, 32, 64], "cmp_w1_k": [2, 2048, 256], "cmp_b1_k": [2, 256], "cmp_w2_k": [2, 256, 64],
    "cmp_pos_v": [2, 32, 64], "cmp_w1_v": [2, 2048, 256], "cmp_b1_v": [2, 256], "cmp_w2_v": [2, 256, 64],
    "w_branch_nsa": [2, 512, D], "mla_norm_q": [2, 384], "mla_norm_kv": [2, 256],
    "w_uq": [2, 384, 768], "w_ukv": [2, 256, 1024], "w_branch_mla": [2, 512, D], "w_out": [2, D, D],
    "w_xq": [2, D, 512], "w_xkv": [2, D, D], "w_xo": [2, 512, D], "w_gate_up": [2, D, 2 * FFN], "w_down": [2, FFN, D],
}


class K:
    def __init__(self, nc, es, debug=None):
        self.nc = nc
        self.es = es
        self.debug = debug or []
        self.skip = []
        self.P = Prog(nc, es)
        self.I = {}
        for n, sh in WEIGHT_SHAPES.items():
            dt = I32 if n == "positions" else F32
            self.I[n] = nc.dram_tensor(n, sh, dt, kind="ExternalInput").ap()
        self.C = {}
        for n, sh in CONST_SHAPES.items():
            self.C[n] = nc.dram_tensor("c_" + n, sh, F32, kind="ExternalInput").ap()
        self.out = nc.dram_tensor("y", [S, D], F32, kind="ExternalOutput").ap()
        self.Db = {}
        self.Dr = {}
        arena_t = es.enter_context(nc.sbuf_tensor("arena", [128, 204800], U8))
        self.A = Arena(arena_t, 204800)
        ps = es.enter_context(nc.psum_tensor("ps", [128, 8, 512], F32))
        self.ps = ps
        self.pb = [Buf() for _ in range(8)]

    def dram(self, name, shape, dtype):
        kind = "ExternalOutput" if name in self.debug else "Internal"
        t = self.nc.dram_tensor(name, list(shape), dtype, kind=kind)
        self.Dr[name] = t
        self.Db[name] = Buf()
        return t.ap()

    def MM(self, out, lhsT, rhs, start, stop, R, W):
        self.P.op("pe", lambda e: e.matmul(out, lhsT=lhsT, rhs=rhs, start=start, stop=stop), R, W)

    def TR(self, out, in_, R, W):
        ident = self.identb.ap
        self.P.op("pe", lambda e: e.transpose(out=out, in_=in_, identity=ident), list(R) + [self.identb.b], W)

    def ACT(self, out, in_, func, R, W, **kw):
        self.P.op("act", lambda e: e.activation(out=out, in_=in_, func=func, **kw), R, W)

    def V(self, name, R, W, **kw):
        self.P.op("dve", lambda e: getattr(e, name)(**kw), R, W)

    def G(self, name, R, W, **kw):
        self.P.op("pool", lambda e: getattr(e, name)(**kw), R, W)

    def E(self, eng, name, R, W, **kw):
        self.P.op(eng, lambda e: getattr(e, name)(**kw), R, W)

    def LD(self, out, in_, R=(), W=(), **kw):
        self.P.dma("sp", out, in_, R, W, **kw)

    def ST(self, out, in_, R=(), W=(), **kw):
        self.P.dma("pool", out, in_, R, W, **kw)

    def psf(self, i):
        return self.ps[:, i, :]

    def psb(self, i):
        return self.ps[:, i, :].bitcast(BF16)

    def setup(self):
        A = self.A
        self.identf = A.alloc([128, 128], F32)
        self.identb = A.alloc([128, 128], BF16)
        self.eps = A.alloc([128, 1], F32)
        self.onesf = A.alloc([128, 128], F32)
        self.onesb = A.alloc([128, 128], BF16)
        self.LD(self.identf.ap, self.C["ident"], W=[self.identf.b])
        self.V("tensor_copy", [self.identf.b], [self.identb.b], out=self.identb.ap, in_=self.identf.ap)
        self.V("memset", [], [self.eps.b], ap=self.eps.ap, constant=1e-6)
        self.V("memset", [], [self.onesf.b], ap=self.onesf.ap, constant=1.0)
        self.V("memset", [], [self.onesb.b], ap=self.onesb.ap, constant=1.0)
        self.base_mark = A.mark()

    def phase_end(self):
        self.P.cut()
        self.A.reset(self.base_mark)

    def release(self, m):
        self.P.barrier()
        self.A.reset(m)

    def load_weight_bf16(self, dst, src, stage, kchunks, ncols, parts=128):
        sv = stage.ap[0:parts, 0:kchunks * ncols].rearrange("p (k n) -> p k n", k=kchunks)
        self.LD(sv, src.rearrange("(k p) n -> p k n", p=parts), W=[stage.b])
        self.G("tensor_copy", [stage.b], [dst.b], out=dst.ap, in_=sv)

    def rms_rstd(self, xt, junk, ss, rstd, n):
        self.ACT(junk.ap, xt.ap, AF.Square, [xt.b], [junk.b, ss.b], scale=float(n) ** -0.5, accum_out=ss.ap)
        self.ACT(rstd.ap, ss.ap, AF.Sqrt, [ss.b, self.eps.b], [rstd.b], bias=self.eps.ap, scale=1.0)
        self.V("reciprocal", [rstd.b], [rstd.b], out=rstd.ap, in_=rstd.ap)

    def norm_to_hT(self, src, gain, hT, ntile=NT, width=D, pbank=0):
        A = self.A
        m = A.mark()
        kc = width // 128
        gb = A.alloc([128, width], F32)
        self.LD(gb.ap, gain.partition_broadcast(128), W=[gb.b])
        xts = [A.alloc([128, width], F32) for _ in range(2)]
        junk = A.alloc([128, width], F32)
        hs = [A.alloc([128, width], BF16) for _ in range(2)]
        sss = [A.alloc([128, 1], F32) for _ in range(2)]
        rss = [A.alloc([128, 1], F32) for _ in range(2)]
        for t in range(ntile):
            xt, h, ss, rstd = xts[t % 2], hs[t % 2], sss[t % 2], rss[t % 2]
            self.LD(xt.ap, src[t * 128:(t + 1) * 128, :], R=[self.cur_x_b], W=[xt.b])
            self.rms_rstd(xt, junk, ss, rstd, width)
            self.V("scalar_tensor_tensor", [xt.b, rstd.b, gb.b], [h.b], out=h.ap, in0=xt.ap, scalar=rstd.ap[:, 0:1],
                   in1=gb.ap, op0=ALU.mult, op1=ALU.mult)
            bank = pbank + (t % 2)
            pt = self.psb(bank)[:, 0:kc * 128].rearrange("p (k n) -> p k n", k=kc)
            for k in range(kc):
                self.TR(pt[:, k, :], h.ap[:, k * 128:(k + 1) * 128], [h.b], [self.pb[bank]])
            self.ACT(hT.ap[:, :, t * 128:(t + 1) * 128], pt, AF.Copy, [self.pb[bank]], [hT.b])
        self.release(m)

    def ph_inproj(self, l, xsrc):
        A = self.A
        I = self.I
        w = I["w_in"][l]
        hT = A.alloc([128, 8, S], BF16)
        self.norm_to_hT(xsrc, I["norm_mix"][l], hT)
        wst = [A.alloc([128, 8 * 512], F32) for _ in range(2)]
        wbf = [A.alloc([128, 8, 512], BF16) for _ in range(2)]
        stg = [A.alloc([128, S], F32) for _ in range(2)]
        sig = [A.alloc([128, 512], F32) for _ in range(2)]
        self._u = 0
        self._s = 0
        self._pbk = 0

        def load_unit(ranges):
            i = self._u % 2
            self._u += 1
            off = 0
            ws, wb = wst[i], wbf[i]
            tot = sum(n for _, n in ranges)
            sv = ws.ap[:, 0:8 * tot].rearrange("p (k n) -> p k n", k=8)
            for (c0, n) in ranges:
                self.LD(sv[:, :, off:off + n], w[:, c0:c0 + n].rearrange("(k p) n -> p k n", p=128),
                        R=[wb.b], W=[ws.b])
                off += n
            self.G("tensor_copy", [ws.b], [wb.b], out=wb.ap[:, :, 0:tot], in_=sv)
            return wb

        def fm_mm(wb, off, m, tc, bank):
            for k in range(8):
                self.MM(self.psf(bank)[0:m, :], wb.ap[:, k, off:off + m], hT.ap[:, k, tc * 512:(tc + 1) * 512],
                        k == 0, k == 7, [wb.b, hT.b], [self.pb[bank]])

        def nb():
            b = self._pbk % 4
            self._pbk += 1
            return b

        def nstg():
            s = stg[self._s % 2]
            self._s += 1
            return s

        uT, qnT, kvT, krT, gmT = self.d_uT, self.d_qnT, self.d_kvT, self.d_krT, self.d_gmT
        for c in range(4):
            wb = load_unit([(c * 128, 128), (512 + c * 128, 128)])
            st = nstg()
            for tc in range(NQC):
                ba, bb = nb(), nb()
                fm_mm(wb, 0, 128, tc, ba)
                fm_mm(wb, 128, 128, tc, bb)
                sg = sig[tc % 2]
                self.ACT(sg.ap, self.psf(bb), AF.Sigmoid, [self.pb[bb]], [sg.b])
                self.V("tensor_tensor", [self.pb[ba], sg.b], [st.b], out=st.ap[:, tc * 512:(tc + 1) * 512],
                       in0=self.psf(ba), in1=sg.ap, op=ALU.mult)
            self.ST(uT[c * 128:(c + 1) * 128, :], st.ap, R=[st.b], W=[self.Db["uT"]])
        for hp in range(2):
            wb = load_unit([(1024 + hp * 256, 256)])
            for hh in range(4):
                h = hp * 4 + hh
                st = nstg()
                sb = st.ap[0:64, :].bitcast(BF16)
                for tc in range(NQC):
                    bk = nb()
                    fm_mm(wb, hh * 64, 64, tc, bk)
                    self.ACT(sb[:, tc * 512:(tc + 1) * 512], self.psf(bk)[0:64, :], AF.Copy, [self.pb[bk]], [st.b], scale=0.125)
                self.ST(qnT[h], sb[:, 0:S], R=[st.b], W=[self.Db["qnT"]])
        for si, slot in enumerate((0, 1, 2, 4)):
            wb = load_unit([(1536 + slot * 128, 128)])
            for g in range(2):
                st = nstg()
                sb = st.ap[0:64, :].bitcast(BF16)
                for tc in range(NQC):
                    bk = nb()
                    fm_mm(wb, g * 64, 64, tc, bk)
                    self.V("tensor_copy", [self.pb[bk]], [st.b], out=sb[:, tc * 512:(tc + 1) * 512], in_=self.psf(bk)[0:64, :])
                self.ST(kvT[si, g], sb[:, 0:S], R=[st.b], W=[self.Db["kvT"]])
        wb = load_unit([(2968, 64)])
        for hf in range(2):
            st = nstg()
            for tc in range(NQC):
                bk = nb()
                fm_mm(wb, hf * 32, 32, tc, bk)
                self.V("tensor_copy", [self.pb[bk]], [st.b], out=st.ap[0:32, tc * 512:(tc + 1) * 512], in_=self.psf(bk)[0:32, :])
            self.ST(krT[hf], st.ap[0:32, :], R=[st.b], W=[self.Db["krT"]])
        for cg in range(6):
            wb = load_unit([(3032 + cg * 512, 512)])
            for j in range(4):
                st = nstg()
                sb = st.ap.bitcast(BF16)
                for tc in range(NQC):
                    bk = nb()
                    fm_mm(wb, j * 128, 128, tc, bk)
                    self.ACT(sb[:, tc * 512:(tc + 1) * 512], self.psf(bk), AF.Sigmoid, [self.pb[bk]], [st.b])
                r0 = (cg * 4 + j) * 128
                self.ST(gmT[r0:r0 + 128, :], sb[:, 0:S], R=[st.b], W=[self.Db["gmT"]])
        wbv = load_unit([(1536 + 3 * 128, 128), (1536 + 5 * 128, 128)])
        wbm = load_unit([(2304, 512)])
        m = A.mark()
        ws3 = A.alloc([128, 8 * 152], F32)
        wb3 = A.alloc([128, 8, 152], BF16)
        sv3 = ws3.ap.rearrange("p (k n) -> p k n", k=8)
        self.LD(sv3, w[:, 2816:2968].rearrange("(k p) n -> p k n", p=128), W=[ws3.b])
        self.G("tensor_copy", [ws3.b], [wb3.b], out=wb3.ap, in_=sv3)
        vst = [A.alloc([128, 4, 65], BF16) for _ in range(2)]
        mst = [A.alloc([128, 664], F32) for _ in range(2)]
        for v_ in vst:
            self.V("memset", [], [v_.b], ap=v_.ap, constant=1.0)
        for t in range(NT):
            bv, bm, b3 = 4 + (t % 2) * 2, 5 + (t % 2) * 2, nb()
            vs, ms = vst[t % 2], mst[t % 2]
            for k in range(8):
                self.MM(self.psf(bv)[:, 0:256], hT.ap[:, k, t * 128:(t + 1) * 128], wbv.ap[:, k, 0:256], k == 0, k == 7,
                        [wbv.b, hT.b], [self.pb[bv]])
            for k in range(8):
                self.MM(self.psf(bm), hT.ap[:, k, t * 128:(t + 1) * 128], wbm.ap[:, k, 0:512], k == 0, k == 7,
                        [wbm.b, hT.b], [self.pb[bm]])
            for k in range(8):
                self.MM(self.psf(b3)[:, 0:152], hT.ap[:, k, t * 128:(t + 1) * 128], wb3.ap[:, k, :], k == 0, k == 7,
                        [wb3.b, hT.b], [self.pb[b3]])
            self.V("tensor_copy", [self.pb[bv]], [vs.b], out=vs.ap[:, :, 0:64],
                   in_=self.psf(bv)[:, 0:256].rearrange("p (a d) -> p a d", a=4))
            self.ACT(ms.ap[:, 0:24], self.psf(bm)[:, 0:24], AF.Sigmoid, [self.pb[bm]], [ms.b])
            self.V("tensor_copy", [self.pb[bm]], [ms.b], out=ms.ap[:, 24:512], in_=self.psf(bm)[:, 24:512])
            self.ACT(ms.ap[:, 512:664], self.psf(b3)[:, 0:152], AF.Copy, [self.pb[b3]], [ms.b])
            self.ST(self.d_vtm[t * 128:(t + 1) * 128], vs.ap, R=[vs.b], W=[self.Db["vtm"]])
            self.ST(self.d_misc[t * 128:(t + 1) * 128, :], ms.ap, R=[ms.b], W=[self.Db["misc"]])
        self.phase_end()

    def col_load(self, dst, src1d, c0, n=128):
        self.LD(dst, src1d[c0:c0 + n].rearrange("(p o) -> p o", o=1))

    def ph_conv(self, l):
        A, I = self.A, self.I
        cwn = A.alloc([31, 512], F32)
        self.LD(cwn.ap, I["conv_w"][l], W=[cwn.b])
        cwT = A.alloc([128, 4, 32], F32)
        for c in range(4):
            self.MM(self.psf(0)[:, c * 32:c * 32 + 31], cwn.ap[:, c * 128:(c + 1) * 128], self.identf.ap[0:31, 0:31],
                    True, True, [cwn.b, self.identf.b], [self.pb[0]])
        self.V("tensor_copy", [self.pb[0]], [cwT.b], out=cwT.ap[:, :, 0:31],
               in_=self.psf(0)[:, 0:128].rearrange("p (c k) -> p c k", c=4)[:, :, 0:31])
        vecs = A.alloc([128, 3, 4], F32)
        for vi, nm in enumerate(("conv_b", "conv_ln_g", "conv_ln_b")):
            for c in range(4):
                self.P.dma("sp", vecs.ap[:, vi, c:c + 1], I[nm][l][c * 128:(c + 1) * 128].rearrange("(p o) -> p o", o=1),
                           W=[vecs.b])
        accs = A.alloc([128, 4, S], F32)
        accb = [Buf() for _ in range(4)]
        ubuf = [A.alloc([128, 30 + S], F32) for _ in range(2)]
        for u in ubuf:
            self.G("memset", [], [u.b], ap=u.ap[:, 0:30], constant=0.0)
        for c in range(4):
            ub = ubuf[c % 2]
            self.LD(ub.ap[:, 30:30 + S], self.d_uT[c * 128:(c + 1) * 128, :], R=[self.Db["uT"]], W=[ub.b])
            acc = accs.ap[:, c, :]
            self.V("tensor_scalar", [ub.b, cwT.b, vecs.b], [accb[c]], out=acc, in0=ub.ap[:, 0:S], scalar1=cwT.ap[:, c, 0:1],
                   scalar2=vecs.ap[:, 0, c:c + 1], op0=ALU.mult, op1=ALU.add)
            for k in range(1, 31):
                self.V("scalar_tensor_tensor", [ub.b, cwT.b, accb[c]], [accb[c]], out=acc, in0=ub.ap[:, k:k + S],
                       scalar=cwT.ap[:, c, k:k + 1], in1=acc, op0=ALU.mult, op1=ALU.add)
        uact = A.alloc([128, 4, S], BF16)
        tmp = {n: [A.alloc([128, 512], F32) for _ in range(2)] for n in ("sq", "mean", "msq", "var", "t")}
        for tc in range(NQC):
            sl = slice(tc * 512, (tc + 1) * 512)
            i2 = tc % 2
            b1, b2 = 1 + 2 * i2, 2 + 2 * i2
            for c in range(4):
                self.MM(self.psf(b1), self.onesf.ap, accs.ap[:, c, sl], c == 0, c == 3, [self.onesf.b, accb[c]], [self.pb[b1]])
            for c in range(4):
                sq = tmp["sq"][c % 2]
                self.ACT(sq.ap, accs.ap[:, c, sl], AF.Square, [accb[c]], [sq.b])
                self.MM(self.psf(b2), self.onesf.ap, sq.ap, c == 0, c == 3, [self.onesf.b, sq.b], [self.pb[b2]])
            mean, msq, var = tmp["mean"][i2], tmp["msq"][i2], tmp["var"][i2]
            self.ACT(mean.ap, self.psf(b1), AF.Copy, [self.pb[b1]], [mean.b], scale=1.0 / 512)
            self.V("tensor_tensor", [mean.b], [msq.b], out=msq.ap, in0=mean.ap, in1=mean.ap, op=ALU.mult)
            self.V("scalar_tensor_tensor", [self.pb[b2], msq.b], [var.b], out=var.ap, in0=self.psf(b2), scalar=1.0 / 512,
                   in1=msq.ap, op0=ALU.mult, op1=ALU.subtract)
            self.ACT(var.ap, var.ap, AF.Sqrt, [var.b, self.eps.b], [var.b], bias=self.eps.ap, scale=1.0)
            self.V("reciprocal", [var.b], [var.b], out=var.ap, in_=var.ap)
            for c in range(4):
                t = tmp["t"][c % 2]
                self.V("tensor_tensor", [accb[c], mean.b], [t.b], out=t.ap, in0=accs.ap[:, c, sl], in1=mean.ap, op=ALU.subtract)
                self.V("tensor_tensor", [t.b, var.b], [t.b], out=t.ap, in0=t.ap, in1=var.ap, op=ALU.mult)
                self.ACT(uact.ap[:, c, sl], t.ap, AF.Silu, [t.b, vecs.b], [uact.b], scale=vecs.ap[:, 1, c:c + 1],
                         bias=vecs.ap[:, 2, c:c + 1])
        for c in range(4):
            self.ST(self.d_uactT[c * 128:(c + 1) * 128, :], uact.ap[:, c, :], R=[uact.b], W=[self.Db["uactT"]])
        self.phase_end()

    def ph_bias_tables(self):
        A, I, C = self.A, self.I, self.C
        rb = A.alloc([33, 8], F32)
        self.V("memset", [], [rb.b], ap=rb.ap, constant=1.0)
        self.LD(rb.ap[0:32, :], I["rel_bias"], R=[rb.b], W=[rb.b])
        ohv = A.alloc([33, 384], F32)
        self.LD(ohv.ap, C["oh_v"], W=[ohv.b])
        ohc = A.alloc([33, 2048], F32)
        self.LD(ohc.ap, C["oh_c"], W=[ohc.b])
        vrep = [A.alloc([128, 384], BF16) for _ in range(2)]
        for h in range(8):
            bk = h % 2
            vr = vrep[h % 2]
            self.MM(self.psf(bk)[:, 0:384], rb.ap[:, h:h + 1].to_broadcast([33, 128]), ohv.ap, True, True, [rb.b, ohv.b], [self.pb[bk]])
            self.V("tensor_copy", [self.pb[bk]], [vr.b], out=vr.ap, in_=self.psf(bk)[:, 0:384])
            self.ST(self.d_vs[h], vr.ap, R=[vr.b], W=[self.Db["vs"]])
        bcs = A.alloc([8, 2048], BF16)
        for j in range(4):
            bk = 2 + j % 2
            self.MM(self.psf(bk)[0:8, :], rb.ap, ohc.ap[:, j * 512:(j + 1) * 512], True, True, [rb.b, ohc.b], [self.pb[bk]])
            self.V("tensor_copy", [self.pb[bk]], [bcs.b], out=bcs.ap[:, j * 512:(j + 1) * 512], in_=self.psf(bk)[0:8, :])
        self.ST(self.d_bc, bcs.ap, R=[bcs.b], W=[self.Db["bc"]])
        self.phase_end()

    def ph_rope_tables(self):
        A, I, C = self.A, self.I, self.C
        posi = A.alloc([32, S], I32)
        self.LD(posi.ap, I["positions"].partition_broadcast(32), W=[posi.b])
        inv = A.alloc([32, 1], F32)
        self.LD(inv.ap, C["invfreq"], W=[inv.b])
        turns = A.alloc([32, S], F32)
        self.V("tensor_copy", [posi.b], [turns.b], out=turns.ap, in_=posi.ap)
        self.V("tensor_scalar", [turns.b, inv.b], [turns.b], out=turns.ap, in0=turns.ap, scalar1=inv.ap[:, 0:1], scalar2=None,
               op0=ALU.mult)
        r = A.alloc([32, S], F32)
        ti = A.alloc([32, S], I32)
        tf = A.alloc([32, S], F32)
        fl = A.alloc([32, S], F32)
        res = A.alloc([32, S], F32)
        qs = 192.0 ** -0.5
        for idx, shift in ((1, 0.0), (0, 0.25)):
            self.V("tensor_scalar", [turns.b], [r.b], out=r.ap, in0=turns.ap, scalar1=shift, scalar2=None, op0=ALU.add)
            self.V("tensor_copy", [r.b], [ti.b], out=ti.ap, in_=r.ap)
            self.V("tensor_copy", [ti.b], [tf.b], out=tf.ap, in_=ti.ap)
            self.V("tensor_tensor", [r.b, tf.b], [r.b], out=r.ap, in0=r.ap, in1=tf.ap, op=ALU.subtract)
            self.V("tensor_scalar", [r.b], [fl.b], out=fl.ap, in0=r.ap, scalar1=0.5, scalar2=None, op0=ALU.is_gt)
            self.V("tensor_tensor", [r.b, fl.b], [r.b], out=r.ap, in0=r.ap, in1=fl.ap, op=ALU.subtract)
            self.V("tensor_scalar", [r.b], [fl.b], out=fl.ap, in0=r.ap, scalar1=-0.5, scalar2=None, op0=ALU.is_lt)
            self.V("tensor_tensor", [r.b, fl.b], [r.b], out=r.ap, in0=r.ap, in1=fl.ap, op=ALU.add)
            self.ACT(res.ap, r.ap, AF.Sin, [r.b], [res.b], scale=2.0 * math.pi)
            self.ST(self.d_cs[idx], res.ap, R=[res.b], W=[self.Db["cs"]])
            self.V("tensor_scalar", [res.b], [tf.b], out=tf.ap, in0=res.ap, scalar1=qs, scalar2=None, op0=ALU.mult)
            self.ST(self.d_cs[2 + idx], tf.ap, R=[tf.b], W=[self.Db["cs"]])
        self.phase_end()
    def gelu_tanh(self, out_bf, ps_in, bias_col, n, tmps):
        x, x2, s = tmps
        self.ACT(x.ap[:, 0:n], ps_in, AF.Identity, [self._gb, self._gc], [x.b], bias=bias_col, scale=1.0)
        self.V("tensor_tensor", [x.b], [x2.b], out=x2.ap[:, 0:n], in0=x.ap[:, 0:n], in1=x.ap[:, 0:n], op=ALU.mult)
        self.V("tensor_scalar", [x2.b], [x2.b], out=x2.ap[:, 0:n], in0=x2.ap[:, 0:n], scalar1=0.044715, scalar2=1.0,
               op0=ALU.mult, op1=ALU.add)
        self.V("tensor_tensor", [x2.b, x.b], [x2.b], out=x2.ap[:, 0:n], in0=x2.ap[:, 0:n], in1=x.ap[:, 0:n], op=ALU.mult)
        self.ACT(s.ap[:, 0:n], x2.ap[:, 0:n], AF.Sigmoid, [x2.b], [s.b], scale=2.0 * math.sqrt(2.0 / math.pi))
        self.V("tensor_tensor", [x.b, s.b], [self._go], out=out_bf, in0=x.ap[:, 0:n], in1=s.ap[:, 0:n], op=ALU.mult)

    def ph_nsa(self, l):
        A, I, C = self.A, self.I, self.C
        kcmpT = A.alloc([64, 2, 256], BF16)
        vcmp = A.alloc([128, 2, 2, 64], BF16)
        self.G("memset", [], [kcmpT.b], ap=kcmpT.ap, constant=0.0)
        self.G("memset", [], [vcmp.b], ap=vcmp.ap, constant=0.0)
        onsa = A.alloc([128, NT, 512], F32)
        onsab = [Buf() for _ in range(NT)]
        m0 = A.mark()
        stg = A.alloc([64, 32 * 256], F32)
        w1s = A.alloc([64, 32, 256], BF16)
        w2f = A.alloc([128, 2, 64], F32)
        w2s = A.alloc([128, 2, 64], BF16)
        posn = A.alloc([32, 64], F32)
        posT = A.alloc([64, 32], BF16)
        b1 = A.alloc([128, 2], F32)
        c1 = A.alloc([128, 2], F32)
        srcT = [A.alloc([64, S], BF16) for _ in range(2)]
        hact = A.alloc([128, 2, 256], BF16)
        tmps = [A.alloc([128, 256], F32) for _ in range(3)]
        for kv_i, nm in enumerate(("k", "v")):
            sv = stg.ap.rearrange("p (l h) -> p l h", l=32)
            self.LD(sv, I[f"cmp_w1_{nm}"][l].rearrange("(l d) h -> d l h", d=64), R=[w1s.b], W=[stg.b])
            self.G("tensor_copy", [stg.b], [w1s.b], out=w1s.ap, in_=sv)
            self.LD(w2f.ap, I[f"cmp_w2_{nm}"][l].rearrange("(c p) n -> p c n", p=128), R=[w2s.b], W=[w2f.b])
            self.G("tensor_copy", [w2f.b], [w2s.b], out=w2s.ap, in_=w2f.ap)
            self.LD(posn.ap, I[f"cmp_pos_{nm}"][l], W=[posn.b])
            self.MM(self.psf(0)[0:64, 0:32], posn.ap, self.identf.ap[0:32, 0:32], True, True, [posn.b, self.identf.b], [self.pb[0]])
            self.V("tensor_copy", [self.pb[0]], [posT.b], out=posT.ap, in_=self.psf(0)[0:64, 0:32])
            for hc in range(2):
                self.P.dma("sp", b1.ap[:, hc:hc + 1], I[f"cmp_b1_{nm}"][l][hc * 128:(hc + 1) * 128].rearrange("(p o) -> p o", o=1),
                           R=[b1.b], W=[b1.b])
            for hc in range(2):
                for li in range(32):
                    self.MM(self.psf(1)[:, hc:hc + 1], w1s.ap[:, li, hc * 128:(hc + 1) * 128], posT.ap[:, li:li + 1],
                            li == 0, li == 31, [w1s.b, posT.b], [self.pb[1]])
            self.V("tensor_tensor", [self.pb[1], b1.b], [c1.b], out=c1.ap, in0=self.psf(1)[:, 0:2], in1=b1.ap, op=ALU.add)
            for g in range(2):
                st = srcT[g]
                self.LD(st.ap, self.d_kvT[kv_i, g], R=[self.Db["kvT"]], W=[st.b])
                s3 = st.ap.rearrange("p (c r) -> p c r", r=16)
                for hc in range(2):
                    bk = 2 + hc
                    for li in range(32):
                        a, r = li // 16, li % 16
                        self.MM(self.psf(bk)[:, 0:255], w1s.ap[:, li, hc * 128:(hc + 1) * 128], s3[:, a:a + 255, r],
                                li == 0, li == 31, [w1s.b, st.b], [self.pb[bk]])
                    self._gb, self._gc, self._go = self.pb[bk], c1.b, hact.b
                    self.gelu_tanh(hact.ap[:, hc, 0:255], self.psf(bk)[:, 0:255], c1.ap[:, hc:hc + 1], 255, tmps)
                if nm == "k":
                    for hc in range(2):
                        self.MM(self.psf(4)[0:64, 0:255], w2s.ap[:, hc, :], hact.ap[:, hc, 0:255], hc == 0, hc == 1,
                                [w2s.b, hact.b], [self.pb[4]])
                    self.V("tensor_copy", [self.pb[4]], [kcmpT.b], out=kcmpT.ap[:, g, 0:255], in_=self.psf(4)[0:64, 0:255])
                else:
                    for cb in range(2):
                        n = 128 if cb == 0 else 127
                        for hc in range(2):
                            self.MM(self.psf(5 + cb)[0:n, 0:64], hact.ap[:, hc, cb * 128:cb * 128 + n], w2s.ap[:, hc, :], hc == 0,
                                    hc == 1, [w2s.b, hact.b], [self.pb[5 + cb]])
                        self.V("tensor_copy", [self.pb[5 + cb]], [vcmp.b], out=vcmp.ap[0:n, g, cb, :], in_=self.psf(5 + cb)[0:n, 0:64])
        self.release(m0)
        gates = A.alloc([128, NT, 24], F32)
        self.LD(gates.ap, self.d_misc.rearrange("(t p) c -> p t c", p=128)[:, :, 0:24], R=[self.Db["misc"]], W=[gates.b])
        ebig = A.alloc([64, S], BF16)
        cover = A.alloc([128, 2, 64], BF16)
        emask = A.alloc([128, 128], BF16)
        cflag = A.alloc([128, 1], F32)
        self.LD(cflag.ap, C["cflag"][0:128, :], W=[cflag.b])
        m1 = A.mark()
        tmpf = A.alloc([64, S], F32)
        self.LD(tmpf.ap, C["ebig"], W=[tmpf.b])
        self.G("tensor_copy", [tmpf.b], [ebig.b], out=ebig.ap, in_=tmpf.ap)
        cvf = A.alloc([128, 2, 64], F32)
        self.LD(cvf.ap, C["cover"].rearrange("(c p) n -> p c n", p=128), W=[cvf.b])
        self.G("tensor_copy", [cvf.b], [cover.b], out=cover.ap, in_=cvf.ap)
        emf = A.alloc([128, 128], F32)
        self.LD(emf.ap, C["emask"], W=[emf.b])
        self.G("tensor_copy", [emf.b], [emask.b], out=emask.ap, in_=emf.ap)
        self.release(m1)
        qT = A.alloc([64, 4, S], BF16)
        kslc = A.alloc([64, S], BF16)
        kwin = A.alloc([64, S], BF16)
        vslc = A.alloc([128, NT, 65], BF16)
        vwin = A.alloc([128, NT, 65], BF16)
        strip = A.alloc([128, 4, 1024], BF16)
        bct = A.alloc([128, 4, 16], BF16)
        selbT = A.alloc([64, S], BF16)
        pTs = [A.alloc([128, 512], BF16) for _ in range(3)]
        pTfar = [A.alloc([128, 128], BF16) for _ in range(2)]
        pex = [A.alloc([128, 256], F32) for _ in range(2)]
        pcs = [A.alloc([128, 256], BF16) for _ in range(2)]
        pcT = [A.alloc([128, 2, 128], BF16) for _ in range(2)]
        sm = [A.alloc([128, 8], F32) for _ in range(4)]
        selA = [A.alloc([128, 64], F32) for _ in range(2)]
        selB = [A.alloc([128, 64], F32) for _ in range(2)]
        sc_t = [A.alloc([128, 64], F32) for _ in range(2)]
        s2_t = [A.alloc([128, 64], F32) for _ in range(2)]
        m8 = [A.alloc([128, 8], F32) for _ in range(2)]
        sbf = [A.alloc([128, 64], BF16) for _ in range(2)]
        self._smi = 0

        def small():
            t = sm[self._smi % 4]
            self._smi += 1
            return t
        vview = self.d_vtm.rearrange("(t p) a e -> p t a e", p=128)
        for g in range(2):
            for hh in range(4):
                self.LD(qT.ap[:, hh, :], self.d_qnT[g * 4 + hh], R=[self.Db["qnT"]], W=[qT.b])
            self.LD(kslc.ap, self.d_kvT[2, g], R=[self.Db["kvT"]], W=[kslc.b])
            self.LD(kwin.ap, self.d_kvT[3, g], R=[self.Db["kvT"]], W=[kwin.b])
            self.LD(vslc.ap, vview[:, :, g, :], R=[self.Db["vtm"]], W=[vslc.b])
            self.LD(vwin.ap, vview[:, :, 2 + g, :], R=[self.Db["vtm"]], W=[vwin.b])
            self.G("memset", [], [strip.b], ap=strip.ap[:, :, 0:384], constant=NEG)
            self.G("memset", [], [strip.b], ap=strip.ap[:, :, 640:1024], constant=0.0)
            for hh in range(4):
                h = g * 4 + hh
                for o in range(2):
                    src = AP(self.Dr["vs"], h * 128 * 384 + o * 128 + 127, [[383, 128], [1, 128]])
                    self.LD(strip.ap[:, hh, (3 + o) * 128:(4 + o) * 128], src, R=[self.Db["vs"]], W=[strip.b])
                self.LD(bct.ap[:, hh, :], self.d_bc[h].rearrange("(q c) -> q c", c=16), R=[self.Db["bc"]], W=[bct.b])
            for qt in range(NT):
                ncq = min(8 * qt + 8, 255)
                bs = 8 * qt - 8
                c_lo, c_hi = max(bs, 0), min(bs + 16, 255)
                nblk = (ncq + 127) // 128
                bimp = 6 + qt % 2
                sa, sb_ = selA[qt % 2], selB[qt % 2]
                self.LD(sa.ap, C["selA"][qt * 128:(qt + 1) * 128, :], W=[sa.b])
                self.LD(sb_.ap, C["selB"][qt * 128:(qt + 1) * 128, :], W=[sb_.b])
                for hh in range(4):
                    h = g * 4 + hh
                    i2 = (qt * 4 + hh) % 2
                    bsc, btr, bo = i2, 2 + i2, 4 + i2
                    sc = self.psf(bsc)
                    self.MM(sc[:, 0:ncq], qT.ap[:, hh, qt * 128:(qt + 1) * 128], kcmpT.ap[:, g, 0:ncq], True, False,
                            [qT.b, kcmpT.b], [self.pb[bsc]])
                    self.MM(sc[:, c_lo:c_hi], self.identb.ap, bct.ap[:, hh, c_lo - bs:c_hi - bs], False, True,
                            [self.identb.b, bct.b], [self.pb[bsc]])
                    mx = small()
                    self.V("reduce_max", [self.pb[bsc]], [mx.b], out=mx.ap[:, 0:1], in_=sc[:, 0:ncq], axis=AX.X)
                    self.V("tensor_scalar", [mx.b], [mx.b], out=mx.ap[:, 1:2], in0=mx.ap[:, 0:1], scalar1=-1.0, scalar2=None, op0=ALU.mult)
                    pe_, pc_, pt_ = pex[i2], pcs[i2], pcT[i2]
                    self.ACT(pe_.ap[:, 0:ncq], sc[:, 0:ncq], AF.Exp, [self.pb[bsc], mx.b], [pe_.b, mx.b], bias=mx.ap[:, 1:2], scale=1.0,
                             accum_out=mx.ap[:, 2:3])
                    self.V("reciprocal", [mx.b], [mx.b], out=mx.ap[:, 3:4], in_=mx.ap[:, 2:3])
                    if qt == 0:
                        self.V("tensor_tensor", [mx.b, cflag.b], [mx.b], out=mx.ap[:, 3:4], in0=mx.ap[:, 3:4], in1=cflag.ap, op=ALU.mult)
                    self.G("memset", [], [pc_.b], ap=pc_.ap, constant=0.0)
                    self.V("tensor_scalar", [pe_.b, mx.b], [pc_.b], out=pc_.ap[:, 0:ncq], in0=pe_.ap[:, 0:ncq], scalar1=mx.ap[:, 3:4],
                           scalar2=None, op0=ALU.mult)
                    ptb = self.psb(btr)[:, 0:256].rearrange("p (j n) -> p j n", j=2)
                    for j in range(nblk):
                        self.TR(ptb[:, j, :], pc_.ap[:, j * 128:(j + 1) * 128], [pc_.b], [self.pb[btr]])
                    self.ACT(pt_.ap[:, 0:nblk, :], ptb[:, 0:nblk, :], AF.Copy, [self.pb[btr]], [pt_.b])
                    for j in range(nblk):
                        self.MM(self.psf(bo)[:, 0:64], pt_.ap[:, j, :], vcmp.ap[:, g, j, :], j == 0, j == nblk - 1,
                                [pt_.b, vcmp.b], [self.pb[bo]])
                    for j in range(nblk):
                        self.MM(self.psf(bimp)[:, 0:64], pt_.ap[:, j, :], cover.ap[:, j, :], hh == 0 and j == 0,
                                hh == 3 and j == nblk - 1, [pt_.b, cover.b], [self.pb[bimp]])
                    self.ACT(onsa.ap[:, qt, h * 64:(h + 1) * 64], self.psf(bo)[:, 0:64], AF.Copy, [self.pb[bo], gates.b], [onsab[qt]],
                             scale=gates.ap[:, qt, h * 3:h * 3 + 1])
                sct, s2, m8a, sbb = sc_t[qt % 2], s2_t[qt % 2], m8[qt % 2], sbf[qt % 2]
                t = small()
                self.V("tensor_tensor", [self.pb[bimp], sa.b], [sct.b], out=sct.ap, in0=self.psf(bimp)[:, 0:64], in1=sa.ap, op=ALU.mult)
                self.V("tensor_tensor", [sct.b, sb_.b], [sct.b], out=sct.ap, in0=sct.ap, in1=sb_.ap, op=ALU.add)
                self.V("max", [sct.b], [m8a.b], out=m8a.ap, in_=sct.ap)
                self.V("tensor_reduce", [m8a.b], [t.b], out=t.ap[:, 0:1], in_=m8a.ap, axis=AX.X, op=ALU.min)
                self.V("tensor_scalar", [sct.b, t.b], [s2.b], out=s2.ap, in0=sct.ap, scalar1=t.ap[:, 0:1], scalar2=-1e9,
                       op0=ALU.is_ge, op1=ALU.mult)
                self.V("tensor_tensor", [s2.b, sct.b], [s2.b], out=s2.ap, in0=s2.ap, in1=sct.ap, op=ALU.add)
                self.V("max", [s2.b], [m8a.b], out=m8a.ap, in_=s2.ap)
                self.V("tensor_reduce", [m8a.b], [t.b], out=t.ap[:, 1:2], in_=m8a.ap, axis=AX.X, op=ALU.min)
                self.V("tensor_scalar", [t.b], [t.b], out=t.ap[:, 2:3], in0=t.ap[:, 1:2], scalar1=0.0, scalar2=None, op0=ALU.max)
                self.V("tensor_scalar", [sct.b, t.b], [s2.b], out=s2.ap, in0=sct.ap, scalar1=t.ap[:, 2:3], scalar2=None, op0=ALU.is_ge)
                self.V("tensor_scalar", [s2.b], [sbb.b], out=sbb.ap, in0=s2.ap, scalar1=-1.0, scalar2=-NEG, op0=ALU.add, op1=ALU.mult)
                btr = 2 + qt % 2
                self.TR(self.psb(btr)[0:64, 0:128], sbb.ap, [sbb.b], [self.pb[btr]])
                self.V("tensor_copy", [self.pb[btr]], [selbT.b], out=selbT.ap[:, qt * 128:(qt + 1) * 128], in_=self.psb(btr)[0:64, 0:128])
            for hh in range(4):
                h = g * 4 + hh
                for qc in range(NQC):
                    bo = 4 + qc % 2
                    O = self.psf(bo)[:, 0:260].rearrange("p (j e) -> p j e", j=4)
                    nk = 4 * qc + 4

                    def emit_sc(kt, hh=hh, qc=qc):
                        bsc = kt % 3
                        sc = self.psf(bsc)
                        near = kt >= 4 * qc - 1
                        self.MM(sc, kslc.ap[:, kt * 128:(kt + 1) * 128], qT.ap[:, hh, qc * 512:(qc + 1) * 512], True, False,
                                [kslc.b, qT.b], [self.pb[bsc]])
                        self.MM(sc, ebig.ap[:, kt * 128:(kt + 1) * 128], selbT.ap[:, qc * 512:(qc + 1) * 512], False, not near,
                                [ebig.b, selbT.b], [self.pb[bsc]])
                        if near:
                            o = kt - 4 * qc
                            self.MM(sc, self.identb.ap, strip.ap[:, hh, (3 - o) * 128:(3 - o) * 128 + 512], False, True,
                                    [self.identb.b, strip.b], [self.pb[bsc]])
                        pT = pTs[kt % 3]
                        self.ACT(pT.ap, sc, AF.Exp, [self.pb[bsc]], [pT.b])
                        return pT

                    def emit_pv(kt, pT, qc=qc, O=O, bo=bo):
                        for j in range(4):
                            qt = 4 * qc + j
                            if kt > qt:
                                continue
                            self.MM(O[:, j, :], pT.ap[:, j * 128:(j + 1) * 128], vslc.ap[:, kt, :], kt == 0 and j == 0, kt == qt,
                                    [pT.b, vslc.b], [self.pb[bo]])
                    pend = None
                    for kt in range(nk):
                        pT = emit_sc(kt)
                        if pend is not None:
                            emit_pv(*pend)
                        pend = (kt, pT)
                    emit_pv(*pend)
                    for j in range(4):
                        qt = 4 * qc + j
                        t = small()
                        self.V("reciprocal", [self.pb[bo]], [t.b], out=t.ap[:, 0:1], in_=O[:, j, 64:65])
                        self.V("tensor_tensor", [t.b, gates.b], [t.b], out=t.ap[:, 1:2], in0=t.ap[:, 0:1],
                               in1=gates.ap[:, qt, h * 3 + 1:h * 3 + 2], op=ALU.mult)
                        osl = onsa.ap[:, qt, h * 64:(h + 1) * 64]
                        self.V("scalar_tensor_tensor", [self.pb[bo], t.b, onsab[qt]], [onsab[qt]], out=osl, in0=O[:, j, 0:64],
                               scalar=t.ap[:, 1:2], in1=osl, op0=ALU.mult, op1=ALU.add)
                def win_a(qt, hh=hh):
                    main = [kt for kt in range(qt - 3, qt + 1) if kt >= 0]
                    far = qt - 4
                    pTf = None
                    if far >= 0:
                        bsc = 3
                        sc = self.psf(bsc)[:, 0:128]
                        self.MM(sc, kwin.ap[:, far * 128:(far + 1) * 128], qT.ap[:, hh, qt * 128:(qt + 1) * 128], True, False,
                                [kwin.b, qT.b], [self.pb[bsc]])
                        self.MM(sc, self.identb.ap, emask.ap, False, True, [self.identb.b, emask.b], [self.pb[bsc]])
                        pTf = pTfar[qt % 2]
                        self.ACT(pTf.ap, sc, AF.Exp, [self.pb[bsc]], [pTf.b])
                    bsc = qt % 3
                    sc = self.psf(bsc)
                    for i, kt in enumerate(main):
                        nb_ = kt >= qt - 1
                        self.MM(sc[:, i * 128:(i + 1) * 128], kwin.ap[:, kt * 128:(kt + 1) * 128], qT.ap[:, hh, qt * 128:(qt + 1) * 128],
                                True, not nb_, [kwin.b, qT.b], [self.pb[bsc]])
                        if nb_:
                            o = qt - kt
                            self.MM(sc[:, i * 128:(i + 1) * 128], self.identb.ap, strip.ap[:, hh, (3 + o) * 128:(4 + o) * 128], False, True,
                                    [self.identb.b, strip.b], [self.pb[bsc]])
                    pT = pTs[qt % 3]
                    nm_ = len(main) * 128
                    self.ACT(pT.ap[:, 0:nm_], sc[:, 0:nm_], AF.Exp, [self.pb[bsc]], [pT.b])
                    return (qt, main, far, pTf, pT)

                def win_b(qt, main, far, pTf, pT, hh=hh, h=h):
                    bo = 6 + qt % 2
                    O = self.psf(bo)[:, 0:65]
                    nmm = len(main) + (1 if far >= 0 else 0)
                    done = 0
                    if far >= 0:
                        self.MM(O, pTf.ap, vwin.ap[:, far, :], True, False, [pTf.b, vwin.b], [self.pb[bo]])
                        done = 1
                    for i, kt in enumerate(main):
                        self.MM(O, pT.ap[:, i * 128:(i + 1) * 128], vwin.ap[:, kt, :], done == 0, done == nmm - 1,
                                [pT.b, vwin.b], [self.pb[bo]])
                        done += 1
                    t = small()
                    self.V("reciprocal", [self.pb[bo]], [t.b], out=t.ap[:, 0:1], in_=O[:, 64:65])
                    self.V("tensor_tensor", [t.b, gates.b], [t.b], out=t.ap[:, 1:2], in0=t.ap[:, 0:1],
                           in1=gates.ap[:, qt, h * 3 + 2:h * 3 + 3], op=ALU.mult)
                    osl = onsa.ap[:, qt, h * 64:(h + 1) * 64]
                    self.V("scalar_tensor_tensor", [self.pb[bo], t.b, onsab[qt]], [onsab[qt]], out=osl, in0=O[:, 0:64],
                           scalar=t.ap[:, 1:2], in1=osl, op0=ALU.mult, op1=ALU.add)
                pend = None
                for qt in range(NT):
                    st_ = win_a(qt)
                    if pend is not None:
                        win_b(*pend)
                    pend = st_
                win_b(*pend)
        self.release(m0)
        oT = A.alloc([128, 4, S], BF16)
        ob = [A.alloc([128, 512], BF16) for _ in range(2)]
        for qt in range(NT):
            o_ = ob[qt % 2]
            self.V("tensor_copy", [onsab[qt]], [o_.b], out=o_.ap, in_=onsa.ap[:, qt, :])
            bk = qt % 2
            pt = self.psb(bk)[:, 0:512].rearrange("p (k n) -> p k n", k=4)
            for k in range(4):
                self.TR(pt[:, k, :], o_.ap[:, k * 128:(k + 1) * 128], [o_.b], [self.pb[bk]])
            self.ACT(oT.ap[:, :, qt * 128:(qt + 1) * 128], pt, AF.Copy, [self.pb[bk]], [oT.b])
        for k in range(4):
            self.ST(self.d_onsaT[k * 128:(k + 1) * 128, :], oT.ap[:, k, :], R=[oT.b], W=[self.Db["onsaT"]])
        if "onsa_dbg" in self.debug:
            self.ST(self.d_onsa_dbg.rearrange("(t p) c -> p t c", p=128), onsa.ap, R=onsab, W=[Buf()])
        self.phase_end()
    def ph_mla(self, l):
        A, I, C = self.A, self.I, self.C
        qs = 192.0 ** -0.5
        cqT = A.alloc([128, 3, S], BF16)
        ckvT = A.alloc([128, 2, S], BF16)
        self.cur_x_b = self.Db["misc"]
        self.norm_to_hT(self.d_misc[:, 24:408], I["mla_norm_q"][l], cqT, width=384)
        self.norm_to_hT(self.d_misc[:, 408:664], I["mla_norm_kv"][l], ckvT, width=256)
        tt = [A.alloc([32, 512], F32) for _ in range(4)]
        cs = A.alloc([32, 2, S], F32)
        rst = [A.alloc([32, 2, S], BF16) for _ in range(2)]

        def rope(x1, x2, R1, R2, co, si, o1, o2, ob, n):
            t1, t2, t3, t4 = tt
            self.V("tensor_tensor", R1 + [cs.b], [t1.b], out=t1.ap[:, 0:n], in0=x1, in1=co, op=ALU.mult)
            self.V("tensor_tensor", R2 + [cs.b], [t2.b], out=t2.ap[:, 0:n], in0=x2, in1=si, op=ALU.mult)
            self.V("tensor_tensor", [t1.b, t2.b], [ob], out=o1, in0=t1.ap[:, 0:n], in1=t2.ap[:, 0:n], op=ALU.subtract)
            self.V("tensor_tensor", R2 + [cs.b], [t3.b], out=t3.ap[:, 0:n], in0=x2, in1=co, op=ALU.mult)
            self.V("tensor_tensor", R1 + [cs.b], [t4.b], out=t4.ap[:, 0:n], in0=x1, in1=si, op=ALU.mult)
            self.V("tensor_tensor", [t3.b, t4.b], [ob], out=o2, in0=t3.ap[:, 0:n], in1=t4.ap[:, 0:n], op=ALU.add)

        m0 = A.mark()
        for i in range(2):
            self.LD(cs.ap[:, i, :], self.d_cs[i], R=[self.Db["cs"]], W=[cs.b])
        kr = A.alloc([32, 2, S], F32)
        for hf in range(2):
            self.LD(kr.ap[:, hf, :], self.d_krT[hf], R=[self.Db["krT"]], W=[kr.b])
        st = rst[0]
        for tc in range(NQC):
            sl = slice(tc * 512, (tc + 1) * 512)
            rope(kr.ap[:, 0, sl], kr.ap[:, 1, sl], [kr.b], [kr.b], cs.ap[:, 0, sl], cs.ap[:, 1, sl], st.ap[:, 0, sl], st.ap[:, 1, sl],
                 st.b, 512)
        self.ST(self.d_kpe[0], st.ap[:, 0, :], R=[st.b], W=[self.Db["kpe"]])
        self.ST(self.d_kpe[1], st.ap[:, 1, :], R=[st.b], W=[self.Db["kpe"]])
        self.release(m0)
        for i in range(2):
            self.LD(cs.ap[:, i, :], self.d_cs[2 + i], R=[self.Db["cs"]], W=[cs.b])
        stgw = A.alloc([128, 3 * 768], F32)
        wq = A.alloc([128, 3, 768], BF16)
        self.load_weight_bf16(wq, I["w_uq"][l], stgw, 3, 768)
        stgk = A.alloc([128, 2 * 1024], F32)
        wkv = A.alloc([128, 2, 1024], BF16)
        self.load_weight_bf16(wkv, I["w_ukv"][l], stgk, 2, 1024)
        stg = [A.alloc([128, S], BF16) for _ in range(2)]
        self._s = 0

        def nstg():
            s = stg[self._s % 2]
            self._s += 1
            return s
        self._pbk = 0

        def nb():
            b = self._pbk % 4
            self._pbk += 1
            return b

        for h in range(4):
            st = nstg()
            for tc in range(NQC):
                bk = nb()
                for k in range(3):
                    self.MM(self.psf(bk), wq.ap[:, k, h * 192:h * 192 + 128], cqT.ap[:, k, tc * 512:(tc + 1) * 512], k == 0, k == 2,
                            [wq.b, cqT.b], [self.pb[bk]])
                self.ACT(st.ap[:, tc * 512:(tc + 1) * 512], self.psf(bk), AF.Copy, [self.pb[bk]], [st.b], scale=qs)
            self.ST(self.d_qn[h], st.ap, R=[st.b], W=[self.Db["qn"]])
            st = rst[h % 2]
            for tc in range(NQC):
                b1, b2 = nb(), nb()
                sl = slice(tc * 512, (tc + 1) * 512)
                for hf, bk in ((0, b1), (1, b2)):
                    c0 = h * 192 + 128 + hf * 32
                    for k in range(3):
                        self.MM(self.psf(bk)[0:32, :], wq.ap[:, k, c0:c0 + 32], cqT.ap[:, k, sl], k == 0, k == 2,
                                [wq.b, cqT.b], [self.pb[bk]])
                rope(self.psf(b1)[0:32, :], self.psf(b2)[0:32, :], [self.pb[b1]], [self.pb[b2]], cs.ap[:, 0, sl], cs.ap[:, 1, sl],
                     st.ap[:, 0, sl], st.ap[:, 1, sl], st.b, 512)
            self.ST(self.d_qpe[h, 0], st.ap[:, 0, :], R=[st.b], W=[self.Db["qpe"]])
            self.ST(self.d_qpe[h, 1], st.ap[:, 1, :], R=[st.b], W=[self.Db["qpe"]])
            st = nstg()
            for tc in range(NQC):
                bk = nb()
                for k in range(2):
                    self.MM(self.psf(bk), wkv.ap[:, k, h * 256:h * 256 + 128], ckvT.ap[:, k, tc * 512:(tc + 1) * 512], k == 0, k == 1,
                            [wkv.b, ckvT.b], [self.pb[bk]])
                self.V("tensor_copy", [self.pb[bk]], [st.b], out=st.ap[:, tc * 512:(tc + 1) * 512], in_=self.psf(bk))
            self.ST(self.d_kn[h], st.ap, R=[st.b], W=[self.Db["kn"]])
        wv = A.alloc([128, 2, 512], BF16)
        for h in range(4):
            self.G("tensor_copy", [wkv.b], [wv.b], out=wv.ap[:, :, h * 128:(h + 1) * 128], in_=wkv.ap[:, :, h * 256 + 128:h * 256 + 256])
        vst = [A.alloc([128, 4, 129], BF16) for _ in range(2)]
        for v_ in vst:
            self.V("memset", [], [v_.b], ap=v_.ap, constant=1.0)
        for t in range(NT):
            bk = 4 + t % 2
            vs = vst[t % 2]
            for k in range(2):
                self.MM(self.psf(bk), ckvT.ap[:, k, t * 128:(t + 1) * 128], wv.ap[:, k, :], k == 0, k == 1, [ckvT.b, wv.b], [self.pb[bk]])
            self.V("tensor_copy", [self.pb[bk]], [vs.b], out=vs.ap[:, :, 0:128], in_=self.psf(bk).rearrange("p (h d) -> p h d", h=4))
            self.ST(self.d_vmla[t * 128:(t + 1) * 128], vs.ap, R=[vs.b], W=[self.Db["vmla"]])
        self.phase_end()
        cstrip = A.alloc([128, 896], BF16)
        cmf = A.alloc([128, 128], F32)
        self.LD(cmf.ap, C["cmask"], W=[cmf.b])
        self.G("memset", [], [cstrip.b], ap=cstrip.ap[:, 0:384], constant=NEG)
        self.G("memset", [], [cstrip.b], ap=cstrip.ap[:, 512:896], constant=0.0)
        self.G("tensor_copy", [cmf.b], [cstrip.b], out=cstrip.ap[:, 384:512], in_=cmf.ap)
        kpe = A.alloc([64, S], BF16)
        for hf in range(2):
            self.LD(kpe.ap[hf * 32:(hf + 1) * 32, :], self.d_kpe[hf], R=[self.Db["kpe"]], W=[kpe.b])
        qn = A.alloc([128, S], BF16)
        kn = A.alloc([128, S], BF16)
        qpe = A.alloc([64, S], BF16)
        vv = A.alloc([128, NT, 129], BF16)
        oT = A.alloc([128, S], BF16)
        pTs = [A.alloc([128, 512], BF16) for _ in range(3)]
        ob = [A.alloc([128, 128], BF16) for _ in range(2)]
        sm = [A.alloc([128, 4], F32) for _ in range(4)]
        vview = self.d_vmla.rearrange("(t p) h e -> p t h e", p=128)
        smi = 0
        for h in range(4):
            self.LD(qn.ap, self.d_qn[h], R=[self.Db["qn"]], W=[qn.b])
            self.LD(kn.ap, self.d_kn[h], R=[self.Db["kn"]], W=[kn.b])
            for hf in range(2):
                self.LD(qpe.ap[hf * 32:(hf + 1) * 32, :], self.d_qpe[h, hf], R=[self.Db["qpe"]], W=[qpe.b])
            self.LD(vv.ap, vview[:, :, h, :], R=[self.Db["vmla"]], W=[vv.b])
            for qc in range(NQC):
                ba = 4 + 2 * (qc % 2)
                Oj = [self.psf(ba + j // 2)[:, (j % 2) * 129:(j % 2) * 129 + 129] for j in range(4)]
                Ob = [self.pb[ba + j // 2] for j in range(4)]
                qsl = slice(qc * 512, (qc + 1) * 512)
                def emit_sc(kt, qc=qc, qsl=qsl):
                    bsc = kt % 3
                    sc = self.psf(bsc)
                    ksl = slice(kt * 128, (kt + 1) * 128)
                    diag = kt >= 4 * qc
                    self.MM(sc, kn.ap[:, ksl], qn.ap[:, qsl], True, False, [kn.b, qn.b], [self.pb[bsc]])
                    self.MM(sc, kpe.ap[:, ksl], qpe.ap[:, qsl], False, not diag, [kpe.b, qpe.b], [self.pb[bsc]])
                    if diag:
                        o = kt - 4 * qc
                        self.MM(sc, self.identb.ap, cstrip.ap[:, (3 - o) * 128:(3 - o) * 128 + 512], False, True,
                                [self.identb.b, cstrip.b], [self.pb[bsc]])
                    pT = pTs[kt % 3]
                    self.ACT(pT.ap, sc, AF.Exp, [self.pb[bsc]], [pT.b])
                    return pT

                def emit_pv(kt, pT, qc=qc, Oj=Oj, Ob=Ob):
                    for j in range(4):
                        qt = 4 * qc + j
                        if kt > qt:
                            continue
                        self.MM(Oj[j], pT.ap[:, j * 128:(j + 1) * 128], vv.ap[:, kt, :], kt == 0 and j % 2 == 0, kt == qt, [pT.b, vv.b], [Ob[j]])
                pend = None
                for kt in range(4 * qc + 4):
                    pT = emit_sc(kt)
                    if pend is not None:
                        emit_pv(*pend)
                    pend = (kt, pT)
                emit_pv(*pend)
                for j in range(4):
                    qt = 4 * qc + j
                    t = sm[smi % 4]
                    smi += 1
                    o_ = ob[qt % 2]
                    self.V("reciprocal", [Ob[j]], [t.b], out=t.ap[:, 0:1], in_=Oj[j][:, 128:129])
                    self.V("tensor_scalar", [Ob[j], t.b], [o_.b], out=o_.ap, in0=Oj[j][:, 0:128], scalar1=t.ap[:, 0:1], scalar2=None,
                           op0=ALU.mult)
                    bt = 3
                    self.TR(self.psb(bt)[:, 0:128], o_.ap, [o_.b], [self.pb[bt]])
                    self.ACT(oT.ap[:, qt * 128:(qt + 1) * 128], self.psb(bt)[:, 0:128], AF.Copy, [self.pb[bt]], [oT.b])
            self.ST(self.d_omlaT[h * 128:(h + 1) * 128, :], oT.ap, R=[oT.b], W=[self.Db["omlaT"]])
        self.phase_end()

    def ph_merge(self, l, xsrc, xsrc_b):
        A, I = self.A, self.I
        stg = A.alloc([128, 4096], F32)
        wb = {}
        for nm in ("w_branch_conv", "w_branch_nsa", "w_branch_mla"):
            wb[nm] = A.alloc([128, 4, 1024], BF16)
            self.load_weight_bf16(wb[nm], I[nm][l], stg, 4, 1024)
        wo = A.alloc([128, 8, 1024], BF16)
        for hf in range(2):
            sv = stg.ap.rearrange("p (k n) -> p k n", k=4)
            self.LD(sv, I["w_out"][l][hf * 512:(hf + 1) * 512, :].rearrange("(k p) n -> p k n", p=128), R=[wo.b], W=[stg.b])
            self.G("tensor_copy", [stg.b], [wo.b], out=wo.ap[:, hf * 4:(hf + 1) * 4, :], in_=sv)
        srcs = [("w_branch_conv", self.d_uactT, "uactT"), ("w_branch_nsa", self.d_onsaT, "onsaT"), ("w_branch_mla", self.d_omlaT, "omlaT")]
        acts = [[A.alloc([128, 4, 512], BF16) for _ in range(2)] for _ in range(3)]
        gms = [A.alloc([128, 24, 512], BF16) for _ in range(2)]
        mT = [A.alloc([128, 8, 512], BF16) for _ in range(2)]
        ta = [A.alloc([128, 512], F32) for _ in range(2)]
        tb = [A.alloc([128, 512], F32) for _ in range(2)]
        xts = [A.alloc([128, 1024], F32) for _ in range(2)]
        xos = [A.alloc([128, 1024], F32) for _ in range(2)]
        gv = self.d_gmT.rearrange("(b p) s -> p b s", p=128)
        n = 0
        for tc in range(NQC):
            sl = slice(tc * 512, (tc + 1) * 512)
            i2 = tc % 2
            for si, (wn, dsrc, dn) in enumerate(srcs):
                self.LD(acts[si][i2].ap, dsrc.rearrange("(k p) s -> p k s", p=128)[:, :, sl], R=[self.Db[dn]], W=[acts[si][i2].b])
            gm = gms[i2]
            self.LD(gm.ap, gv[:, :, sl], R=[self.Db["gmT"]], W=[gm.b])
            m_ = mT[i2]
            for fc in range(8):
                a_, b_ = ta[fc % 2], tb[fc % 2]
                for si, (wn, dsrc, dn) in enumerate(srcs):
                    bk = n % 4
                    n += 1
                    for k in range(4):
                        self.MM(self.psf(bk), wb[wn].ap[:, k, fc * 128:(fc + 1) * 128], acts[si][i2].ap[:, k, :], k == 0, k == 3,
                                [wb[wn].b, acts[si][i2].b], [self.pb[bk]])
                    dst = a_ if si == 0 else b_
                    self.V("tensor_tensor", [self.pb[bk], gm.b], [dst.b], out=dst.ap, in0=self.psf(bk), in1=gm.ap[:, si * 8 + fc, :], op=ALU.mult)
                    if si == 1:
                        self.V("tensor_tensor", [a_.b, b_.b], [a_.b], out=a_.ap, in0=a_.ap, in1=b_.ap, op=ALU.add)
                    if si == 2:
                        self.V("tensor_tensor", [a_.b, b_.b], [m_.b], out=m_.ap[:, fc, :], in0=a_.ap, in1=b_.ap, op=ALU.add)
            for tt_ in range(4):
                t = tc * 4 + tt_
                xt, xo = xts[t % 2], xos[t % 2]
                self.LD(xt.ap, xsrc[t * 128:(t + 1) * 128, :], R=[xsrc_b], W=[xt.b])
                for cg in range(2):
                    bk = 4 + (t * 2 + cg) % 4
                    for k in range(8):
                        self.MM(self.psf(bk), m_.ap[:, k, tt_ * 128:(tt_ + 1) * 128], wo.ap[:, k, cg * 512:(cg + 1) * 512], k == 0, k == 7,
                                [m_.b, wo.b], [self.pb[bk]])
                    self.V("tensor_tensor", [self.pb[bk], xt.b], [xo.b], out=xo.ap[:, cg * 512:(cg + 1) * 512], in0=self.psf(bk),
                           in1=xt.ap[:, cg * 512:(cg + 1) * 512], op=ALU.add)
                self.ST(self.d_xres[t * 128:(t + 1) * 128, :], xo.ap, R=[xo.b], W=[self.Db["xres"]])
        self.phase_end()

    def ph_xattn(self, l):
        A, I = self.A, self.I
        xs = 128.0 ** -0.5
        hT = A.alloc([128, 8, S], BF16)
        self.cur_x_b = self.Db["xres"]
        self.norm_to_hT(self.d_xres, I["norm_xattn"][l], hT)
        memT = A.alloc([128, 8, 256], BF16)
        self.cur_x_b = Buf()
        self.norm_to_hT(I["mem"], I["norm_mem"][l], memT, ntile=2, pbank=2)
        stg = A.alloc([128, 8 * 512], F32)
        wq = A.alloc([128, 8, 512], BF16)
        self.load_weight_bf16(wq, I["w_xq"][l], stg, 8, 512)
        wkv = A.alloc([128, 8, 1024], BF16)
        for hf in range(2):
            sv = stg.ap.rearrange("p (k n) -> p k n", k=8)
            self.LD(sv, I["w_xkv"][l][:, hf * 512:(hf + 1) * 512].rearrange("(k p) n -> p k n", p=128), R=[wkv.b], W=[stg.b])
            self.G("tensor_copy", [stg.b], [wkv.b], out=wkv.ap[:, :, hf * 512:(hf + 1) * 512], in_=sv)
        wo = A.alloc([128, 4, 1024], BF16)
        self.load_weight_bf16(wo, I["w_xo"][l], stg, 4, 1024)
        kT = A.alloc([128, 4, 256], BF16)
        vv = A.alloc([128, 2, 4, 129], BF16)
        self.V("memset", [], [vv.b], ap=vv.ap, constant=1.0)
        for h in range(4):
            for k in range(8):
                self.MM(self.psf(0)[:, 0:256], wkv.ap[:, k, h * 128:(h + 1) * 128], memT.ap[:, k, :], k == 0, k == 7, [wkv.b, memT.b], [self.pb[0]])
            self.V("tensor_copy", [self.pb[0]], [kT.b], out=kT.ap[:, h, :], in_=self.psf(0)[:, 0:256])
        for mt in range(2):
            for k in range(8):
                self.MM(self.psf(1), memT.ap[:, k, mt * 128:(mt + 1) * 128], wkv.ap[:, k, 512:1024], k == 0, k == 7, [wkv.b, memT.b], [self.pb[1]])
            self.V("tensor_copy", [self.pb[1]], [vv.b], out=vv.ap[:, mt, :, 0:128], in_=self.psf(1).rearrange("p (h d) -> p h d", h=4))
        qTs = [A.alloc([128, 512], BF16) for _ in range(2)]
        pTs = [A.alloc([128, 2, 512], BF16) for _ in range(2)]
        self._ox = [A.alloc([128, 512], BF16) for _ in range(4)]
        oxT = [A.alloc([128, 4, 128], BF16) for _ in range(2)]
        sm = [A.alloc([128, 4], F32) for _ in range(4)]
        xts = [A.alloc([128, 1024], F32) for _ in range(2)]
        xos = [A.alloc([128, 1024], F32) for _ in range(2)]
        smi = 0
        n = 0
        for tc in range(NQC):
            sl = slice(tc * 512, (tc + 1) * 512)
            for h in range(4):
                qT = qTs[h % 2]
                for k in range(8):
                    self.MM(self.psf(0), wq.ap[:, k, h * 128:(h + 1) * 128], hT.ap[:, k, sl], k == 0, k == 7, [wq.b, hT.b], [self.pb[0]])
                self.ACT(qT.ap, self.psf(0), AF.Copy, [self.pb[0]], [qT.b], scale=xs)
                pT = pTs[h % 2]
                for mt in range(2):
                    bsc = 1 + mt
                    self.MM(self.psf(bsc), kT.ap[:, h, mt * 128:(mt + 1) * 128], qT.ap, True, True, [kT.b, qT.b], [self.pb[bsc]])
                    self.ACT(pT.ap[:, mt, :], self.psf(bsc), AF.Exp, [self.pb[bsc]], [pT.b])
                for j in range(4):
                    bk = 4 + (h % 2) * 2 + j // 2
                    Oj = self.psf(bk)[:, (j % 2) * 129:(j % 2) * 129 + 129]
                    for mt in range(2):
                        self.MM(Oj, pT.ap[:, mt, j * 128:(j + 1) * 128], vv.ap[:, mt, h, :], mt == 0 and j % 2 == 0, mt == 1, [pT.b, vv.b], [self.pb[bk]])
                    t = sm[smi % 4]
                    smi += 1
                    self.V("reciprocal", [self.pb[bk]], [t.b], out=t.ap[:, 0:1], in_=Oj[:, 128:129])
                    ox = self._ox[j]
                    self.V("tensor_scalar", [self.pb[bk], t.b], [ox.b], out=ox.ap[:, h * 128:(h + 1) * 128], in0=Oj[:, 0:128],
                           scalar1=t.ap[:, 0:1], scalar2=None, op0=ALU.mult)
            for j in range(4):
                t = tc * 4 + j
                ox = self._ox[j]
                oT_ = oxT[t % 2]
                bt = 3
                pt = self.psb(bt)[:, 0:512].rearrange("p (k n) -> p k n", k=4)
                for k in range(4):
                    self.TR(pt[:, k, :], ox.ap[:, k * 128:(k + 1) * 128], [ox.b], [self.pb[bt]])
                self.ACT(oT_.ap, pt, AF.Copy, [self.pb[bt]], [oT_.b])
                xt, xo = xts[t % 2], xos[t % 2]
                self.LD(xt.ap, self.d_xres[t * 128:(t + 1) * 128, :], R=[self.Db["xres"]], W=[xt.b])
                for cg in range(2):
                    bk = 6 + cg
                    for k in range(4):
                        self.MM(self.psf(bk), oT_.ap[:, k, :], wo.ap[:, k, cg * 512:(cg + 1) * 512], k == 0, k == 3, [oT_.b, wo.b], [self.pb[bk]])
                    self.V("tensor_tensor", [self.pb[bk], xt.b], [xo.b], out=xo.ap[:, cg * 512:(cg + 1) * 512], in0=self.psf(bk),
                           in1=xt.ap[:, cg * 512:(cg + 1) * 512], op=ALU.add)
                self.ST(self.d_xres[t * 128:(t + 1) * 128, :], xo.ap, R=[xo.b], W=[self.Db["xres"]])
        self.phase_end()

    def ph_ffn(self, l):
        A, I = self.A, self.I
        hT = A.alloc([128, 8, S], BF16)
        self.cur_x_b = self.Db["xres"]
        self.norm_to_hT(self.d_xres, I["norm_ffn"][l], hT)
        wg = A.alloc([128, 8, 1408], BF16)
        wu = A.alloc([128, 8, 1408], BF16)
        wd = A.alloc([128, 11, 1024], BF16)
        stg = [A.alloc([128, 3072], F32) for _ in range(2)]
        actT = [A.alloc([128, 11, 512], BF16) for _ in range(1)]
        sg = [A.alloc([128, 512], F32) for _ in range(2)]
        xts = [A.alloc([128, 1024], F32) for _ in range(2)]
        xos = [A.alloc([128, 1024], F32) for _ in range(2)]
        w_gu = I["w_gate_up"][l]
        w_dn = I["w_down"][l]
        for ps_ in range(2):
            f0 = ps_ * 1408
            u = 0
            for dst, base in ((wg, 0), (wu, FFN)):
                for q4 in range(4):
                    s_ = stg[u % 2]
                    u += 1
                    sv = s_.ap[:, 0:2816].rearrange("p (k n) -> p k n", k=8)
                    c0 = base + f0 + q4 * 352
                    self.LD(sv, w_gu[:, c0:c0 + 352].rearrange("(k p) n -> p k n", p=128), R=[dst.b], W=[s_.b])
                    self.G("tensor_copy", [s_.b], [dst.b], out=dst.ap[:, :, q4 * 352:(q4 + 1) * 352], in_=sv)
            for q4 in range(4):
                s_ = stg[u % 2]
                u += 1
                nk = 3 if q4 < 3 else 2
                sv = s_.ap[:, 0:nk * 1024].rearrange("p (k n) -> p k n", k=nk)
                r0 = f0 + q4 * 384
                self.LD(sv, w_dn[r0:r0 + nk * 128, :].rearrange("(k p) n -> p k n", p=128), R=[wd.b], W=[s_.b])
                self.G("tensor_copy", [s_.b], [wd.b], out=wd.ap[:, q4 * 3:q4 * 3 + nk, :], in_=sv)
            n = 0
            for tc in range(NQC):
                sl = slice(tc * 512, (tc + 1) * 512)
                aT = actT[0]
                for f in range(11):
                    bg, bu = (n % 2) * 2, (n % 2) * 2 + 1
                    n += 1
                    for k in range(8):
                        self.MM(self.psf(bg), wg.ap[:, k, f * 128:(f + 1) * 128], hT.ap[:, k, sl], k == 0, k == 7, [wg.b, hT.b], [self.pb[bg]])
                    for k in range(8):
                        self.MM(self.psf(bu), wu.ap[:, k, f * 128:(f + 1) * 128], hT.ap[:, k, sl], k == 0, k == 7, [wu.b, hT.b], [self.pb[bu]])
                    s = sg[f % 2]
                    self.ACT(s.ap, self.psf(bg), AF.Silu, [self.pb[bg]], [s.b])
                    self.V("tensor_tensor", [self.pb[bu], s.b], [aT.b], out=aT.ap[:, f, :], in0=self.psf(bu), in1=s.ap, op=ALU.mult)
                for tt_ in range(4):
                    t = tc * 4 + tt_
                    xt, xo = xts[t % 2], xos[t % 2]
                    self.LD(xt.ap, self.d_xres[t * 128:(t + 1) * 128, :], R=[self.Db["xres"]], W=[xt.b])
                    for cg in range(2):
                        bk = 4 + (t * 2 + cg) % 4
                        for f in range(11):
                            self.MM(self.psf(bk), aT.ap[:, f, tt_ * 128:(tt_ + 1) * 128], wd.ap[:, f, cg * 512:(cg + 1) * 512], f == 0, f == 10,
                                    [aT.b, wd.b], [self.pb[bk]])
                        self.V("tensor_tensor", [self.pb[bk], xt.b], [xo.b], out=xo.ap[:, cg * 512:(cg + 1) * 512], in0=self.psf(bk),
                               in1=xt.ap[:, cg * 512:(cg + 1) * 512], op=ALU.add)
                    self.ST(self.d_xres[t * 128:(t + 1) * 128, :], xo.ap, R=[xo.b], W=[self.Db["xres"]])
            self.P.barrier()
        self.phase_end()

    def ph_final(self):
        A, I = self.A, self.I
        gb = A.alloc([128, D], F32)
        self.LD(gb.ap, I["norm_final"].partition_broadcast(128), W=[gb.b])
        xts = [A.alloc([128, D], F32) for _ in range(2)]
        junk = A.alloc([128, D], F32)
        ys = [A.alloc([128, D], F32) for _ in range(2)]
        sss = [A.alloc([128, 1], F32) for _ in range(2)]
        rss = [A.alloc([128, 1], F32) for _ in range(2)]
        yb = Buf()
        for t in range(NT):
            xt, y_, ss, rstd = xts[t % 2], ys[t % 2], sss[t % 2], rss[t % 2]
            self.LD(xt.ap, self.d_xres[t * 128:(t + 1) * 128, :], R=[self.Db["xres"]], W=[xt.b])
            self.rms_rstd(xt, junk, ss, rstd, D)
            self.V("scalar_tensor_tensor", [xt.b, rstd.b, gb.b], [y_.b], out=y_.ap, in0=xt.ap, scalar=rstd.ap[:, 0:1], in1=gb.ap,
                   op0=ALU.mult, op1=ALU.mult)
            self.ST(self.out[t * 128:(t + 1) * 128, :], y_.ap, R=[y_.b], W=[yb])
        self.phase_end()
    def build(self, n_layers=2, stop_after=None):
        d = self.dram
        self.d_xres = d("xres", [S, D], F32)
        self.d_uT = d("uT", [512, S], F32)
        self.d_qnT = d("qnT", [8, 64, S], BF16)
        self.d_kvT = d("kvT", [4, 2, 64, S], BF16)
        self.d_krT = d("krT", [2, 32, S], F32)
        self.d_gmT = d("gmT", [3072, S], BF16)
        self.d_vtm = d("vtm", [S, 4, 65], BF16)
        self.d_misc = d("misc", [S, 664], F32)
        self.d_uactT = d("uactT", [512, S], BF16)
        self.d_onsaT = d("onsaT", [512, S], BF16)
        self.d_omlaT = d("omlaT", [512, S], BF16)
        self.d_vs = d("vs", [8, 128, 384], BF16)
        self.d_bc = d("bc", [8, 2048], BF16)
        self.d_cs = d("cs", [4, 32, S], F32)
        self.d_qn = d("qn", [4, 128, S], BF16)
        self.d_kn = d("kn", [4, 128, S], BF16)
        self.d_qpe = d("qpe", [4, 2, 32, S], BF16)
        self.d_kpe = d("kpe", [2, 32, S], BF16)
        self.d_vmla = d("vmla", [S, 4, 129], BF16)
        if "onsa_dbg" in self.debug:
            self.d_onsa_dbg = d("onsa_dbg", [S, 512], F32)
        self.cur_x_b = Buf()
        self.setup()
        phases = []
        phases.append(("tables", lambda: (self.ph_bias_tables(), self.ph_rope_tables())))
        for l in range(n_layers):
            xsrc = self.I["x"] if l == 0 else self.d_xres
            xb = Buf() if l == 0 else self.Db["xres"]
            phases.append((f"inproj{l}", lambda l=l, xsrc=xsrc, xb=xb: (setattr(self, "cur_x_b", xb), self.ph_inproj(l, xsrc))))
            phases.append((f"conv{l}", lambda l=l: self.ph_conv(l)))
            phases.append((f"nsa{l}", lambda l=l: self.ph_nsa(l)))
            phases.append((f"mla{l}", lambda l=l: self.ph_mla(l)))
            phases.append((f"merge{l}", lambda l=l, xsrc=xsrc, xb=xb: self.ph_merge(l, xsrc, xb)))
            phases.append((f"xattn{l}", lambda l=l: self.ph_xattn(l)))
            phases.append((f"ffn{l}", lambda l=l: self.ph_ffn(l)))
        phases.append(("final", lambda: self.ph_final()))
        skip = set(self.skip)
        for name, fn in phases:
            if name not in skip:
                fn()
            if stop_after == name:
                break
        self.P.finish()


def build_nc(debug=None, n_layers=2, stop_after=None, skip=()):
    nc = bass.Bass("TRN2", target_bir_lowering=False)
    with contextlib.ExitStack() as es:
        k = K(nc, es, debug)
        k.skip = list(skip)
        k.build(n_layers, stop_after)
    return nc, k


def make_in_maps(inputs, consts):
    maps = []
    for b in range(8):
        m = {}
        for n in WEIGHT_SHAPES:
            a = np.asarray(inputs[n])
            if n in ("x", "mem", "positions"):
                a = a[b]
            m[n] = np.ascontiguousarray(a)
        for n, v in consts.items():
            m["c_" + n] = v
        maps.append(m)
    return maps


def kernel(**inputs):
    nc, k = build_nc()
    consts = host_consts()
    in_maps = make_in_maps(inputs, consts)
    res = run_bass_kernel_spmd(nc, in_maps, core_ids=list(range(8)))
    return np.stack([np.asarray(r["y"]) for r in res.results], axis=0).astype(np.float32)
```

```python
import contextlib
import math
import numpy as np
import concourse.bass as bass
import concourse.mybir as mybir
from concourse.ap import AP
from concourse.bass_utils import run_bass_kernel_spmd

F32 = mybir.dt.float32
BF16 = mybir.dt.bfloat16
I32 = mybir.dt.int32
U8 = mybir.dt.uint8
ALU = mybir.AluOpType
AF = mybir.ActivationFunctionType
AX = mybir.AxisListType

ENGS = ["pe", "act", "dve", "pool", "sp"]
N_DSEM = 8
DTSIZE = {F32: 4, BF16: 2, I32: 4, U8: 1}

S = 4096
D = 1024
NT = S // 128
NQC = S // 512
IN_COLS = 6104
FFN = 2816
NEG = -30000.0


class Buf:
    __slots__ = ("w", "r")

    def __init__(self):
        self.w = None
        self.r = {}


class Tl:
    __slots__ = ("ap", "b")

    def __init__(self, ap, b=None):
        self.ap = ap
        self.b = b if b is not None else Buf()


class Prog:
    def __init__(self, nc, es):
        self.nc = nc
        self.es = es
        self.q = {e: [] for e in ENGS}
        self.cnt = {}
        self.sems = {}
        self.known = {e: {} for e in ENGS}
        self.ep = {}
        self.ekey = {}
        for e in ENGS:
            self.ep[e] = 0
            self._new_epoch(e)
        self.dpool = {}
        self.dnext = {}
        for e in ("sp", "pool", "act"):
            self.dpool[e] = []
            for i in range(N_DSEM):
                k = f"d_{e}_{i}"
                self.sems[k] = es.enter_context(nc.semaphore(k))
                self.cnt[k] = 0
                self.dpool[e].append(k)
            self.dnext[e] = 0
        self.nins = 0

    SEM_LIMIT = 12000

    def _new_epoch(self, e):
        key = f"{e}#{self.ep[e]}"
        self.ep[e] += 1
        self.sems[key] = self.es.enter_context(self.nc.semaphore(f"sem_{e}_{self.ep[e]}"))
        self.cnt[key] = 0
        self.ekey[e] = key

    def _need(self, eng, R, W):
        need = {}

        def add(ev):
            if ev is None:
                return
            k, v = ev
            if need.get(k, 0) < v:
                need[k] = v
        for b in R:
            add(b.w)
        for b in W:
            add(b.w)
            for k, v in b.r.items():
                add((k, v))
        kn = self.known[eng]
        for k, v in need.items():
            if eng == "pe" and k.startswith("pe#"):
                continue
            if kn.get(k, 0) >= v:
                continue
            kn[k] = v
            self.q[eng].append(("wait", k, v))

    def _done(self, ev, R, W):
        k, v = ev
        for b in W:
            b.w = ev
            b.r = {}
        for b in R:
            if b.r.get(k, 0) < v:
                b.r[k] = v

    def op(self, eng, fn, R=(), W=()):
        self._need(eng, R, W)
        if self.cnt[self.ekey[eng]] >= self.SEM_LIMIT:
            self._new_epoch(eng)
        key = self.ekey[eng]
        self.cnt[key] += 1
        self.q[eng].append(("ins", fn, key, 1))
        self._done((key, self.cnt[key]), R, W)
        self.nins += 1

    def dma(self, eng, out, in_, R=(), W=(), **kw):
        self._need(eng, R, W)
        pool = self.dpool[eng]
        k = pool[self.dnext[eng] % len(pool)]
        self.dnext[eng] += 1
        if self.cnt[k] > 0 and self.known[eng].get(k, 0) < self.cnt[k]:
            self.known[eng][k] = self.cnt[k]
            self.q[eng].append(("wait", k, self.cnt[k]))
        self.cnt[k] += 16
        self.q[eng].append(("ins", lambda e: e.dma_start(out=out, in_=in_, **kw), k, 16))
        self._done((k, self.cnt[k]), R, W)
        self.nins += 1

    def barrier(self):
        for e in ENGS:
            kn = self.known[e]
            for k, c in self.cnt.items():
                if k.split("#")[0] == e or c == 0:
                    continue
                if kn.get(k, 0) < c:
                    kn[k] = c
                    self.q[e].append(("wait", k, c))

    def cut(self):
        self.barrier()
        for e in ENGS:
            self.q[e].append(("cut",))

    def finish(self):
        self.barrier()
        nc = self.nc
        sems = self.sems
        segs = {}
        nseg = 1
        for e in ENGS:
            cur = []
            segs[e] = [cur]
            for it in self.q[e]:
                if it[0] == "cut":
                    cur = []
                    segs[e].append(cur)
                else:
                    cur.append(it)
            nseg = max(nseg, len(segs[e]))

        def replay(items):
            def f(e):
                for it in items:
                    if it[0] == "wait":
                        e.wait_ge(sems[it[1]], it[2])
                    else:
                        it[1](e).then_inc(sems[it[2]], it[3])
            return f

        for s in range(nseg):
            if not any(len(segs[e][s]) for e in ENGS):
                continue
            with nc.Block() as block:
                block.tensor(replay(segs["pe"][s]))
                block.scalar(replay(segs["act"][s]))
                block.vector(replay(segs["dve"][s]))
                block.gpsimd(replay(segs["pool"][s]))
                block.sync(replay(segs["sp"][s]))


class Arena:
    def __init__(self, t, size):
        self.t = t
        self.size = size
        self.off = 0

    def alloc(self, shape, dtype, parts=None):
        p = shape[0] if parts is None else parts
        n = 1
        for s in shape[1:]:
            n *= s
        nb = n * DTSIZE[dtype]
        nb_al = (nb + 63) // 64 * 64
        assert self.off + nb_al <= self.size, f"SBUF arena overflow {self.off}+{nb_al}>{self.size}"
        v = self.t[0:p, self.off:self.off + nb].bitcast(dtype)
        self.off += nb_al
        if len(shape) == 3:
            v = v.rearrange("p (a b) -> p a b", a=shape[1])
        elif len(shape) == 4:
            v = v.rearrange("p (a b c) -> p a b c", a=shape[1], b=shape[2])
        return Tl(v)

    def mark(self):
        return self.off

    def reset(self, m):
        self.off = m


def t5_bucket_np(d):
    d = np.asarray(d)
    dd = np.maximum(d, 0)
    lr = np.log(np.maximum(dd, 1).astype(np.float32) / np.float32(16)) / np.float32(math.log(8.0))
    large = np.minimum(16 + (lr * 16).astype(np.int32), 31)
    return np.where(dd < 16, dd, large)


def host_consts():
    c = {}
    c["ident"] = np.eye(128, dtype=np.float32)
    oh = np.zeros((33, 384), np.float32)
    for i in range(383):
        d = i - 127
        if d >= 0:
            oh[int(t5_bucket_np(d)), i] += 1.0
            oh[31, i] -= 1.0
        else:
            oh[32, i] = NEG
    c["oh_v"] = oh
    ohc = np.zeros((33, 128 * 16), np.float32)
    for ql in range(128):
        for cp in range(16):
            d = ql - 16 * cp + 97
            j = ql * 16 + cp
            if d >= 0:
                ohc[int(t5_bucket_np(d)), j] += 1.0
                ohc[31, j] -= 1.0
            else:
                ohc[32, j] = NEG
    c["oh_c"] = ohc
    kl = np.arange(128)[:, None]
    ql = np.arange(128)[None, :]
    c["emask"] = np.where(ql >= kl, NEG, 0.0).astype(np.float32)
    c["cmask"] = np.where(ql < kl, NEG, 0.0).astype(np.float32)
    eb = np.zeros((64, S), np.float32)
    for j in range(64):
        eb[j, j * 64:(j + 1) * 64] = 1.0
    c["ebig"] = eb
    cs = 16 * np.arange(255)
    ss = 64 * np.arange(64)
    cov = ((cs[:, None] < ss[None, :] + 64) & (cs[:, None] + 32 > ss[None, :])).astype(np.float32)
    covp = np.zeros((256, 64), np.float32)
    covp[:255] = cov
    c["cover"] = covp
    t = np.arange(S)[:, None]
    j = np.arange(64)[None, :]
    cur = t // 64
    forced = (j == 0) | (j == cur) | (j == cur - 1)
    causal = (64 * j) <= t
    c["selA"] = np.where(forced, 0.0, np.where(causal, 1.0, 0.0)).astype(np.float32)
    c["selB"] = np.where(forced, 1e4 + j, np.where(causal, 0.0, -1.0)).astype(np.float32)
    c["cflag"] = (np.arange(S) >= 31).astype(np.float32).reshape(S, 1)
    half = 32
    inv = (np.float32(10000.0) ** (-np.arange(half, dtype=np.float32) / np.float32(half))).astype(np.float32)
    c["invfreq"] = (inv / np.float32(2 * math.pi)).astype(np.float32).reshape(32, 1)
    return c


CONST_SHAPES = {
    "ident": [128, 128], "oh_v": [33, 384], "oh_c": [33, 2048], "emask": [128, 128], "cmask": [128, 128],
    "ebig": [64, S], "cover": [256, 64], "selA": [S, 64], "selB": [S, 64], "cflag": [S, 1], "invfreq": [32, 1],
}

WEIGHT_SHAPES = {
    "x": [S, D], "mem": [256, D], "positions": [S], "rel_bias": [32, 8],
    "norm_mix": [2, D], "norm_xattn": [2, D], "norm_mem": [2, D], "norm_ffn": [2, D], "norm_final": [D],
    "w_in": [2, D, IN_COLS], "conv_w": [2, 31, 512], "conv_b": [2, 512], "conv_ln_g": [2, 512], "conv_ln_b": [2, 512],
    "w_branch_conv": [2, 512, D],
    "cmp_pos_k": [2, 32, 64], "cmp_w1_k": [2, 2048, 256], "cmp_b1_k": [2, 256], "cmp_w2_k": [2, 256, 64],
    "cmp_pos_v": [2, 32, 64], "cmp_w1_v": [2, 2048, 256], "cmp_b1_v": [2, 256], "cmp_w2_v": [2, 256, 64],
    "w_branch_nsa": [2, 512, D], "mla_norm_q": [2, 384], "mla_norm_kv": [2, 256],
    "w_uq": [2, 384, 768], "w_ukv": [2, 256, 1024], "w_branch_mla": [2, 512, D], "w_out": [2, D, D],
    "w_xq": [2, D, 512], "w_xkv": [2, D, D], "w_xo": [2, 512, D], "w_gate_up": [2, D, 2 * FFN], "w_down": [2, FFN, D],
}


class K:
    def __init__(self, nc, es, debug=None):
        self.nc = nc
        self.es = es
        self.debug = debug or []
        self.skip = []
        self.P = Prog(nc, es)
        self.I = {}
        for n, sh in WEIGHT_SHAPES.items():
            dt = I32 if n == "positions" else F32
            self.I[n] = nc.dram_tensor(n, sh, dt, kind="ExternalInput").ap()
        self.C = {}
        for n, sh in CONST_SHAPES.items():
            self.C[n] = nc.dram_tensor("c_" + n, sh, F32, kind="ExternalInput").ap()
        self.out = nc.dram_tensor("y", [S, D], F32, kind="ExternalOutput").ap()
        self.Db = {}
        self.Dr = {}
        arena_t = es.enter_context(nc.sbuf_tensor("arena", [128, 204800], U8))
        self.A = Arena(arena_t, 204800)
        ps = es.enter_context(nc.psum_tensor("ps", [128, 8, 512], F32))
        self.ps = ps
        self.pb = [Buf() for _ in range(8)]

    def dram(self, name, shape, dtype):
        kind = "ExternalOutput" if name in self.debug else "Internal"
        t = self.nc.dram_tensor(name, list(shape), dtype, kind=kind)
        self.Dr[name] = t
        self.Db[name] = Buf()
        return t.ap()

    def MM(self, out, lhsT, rhs, start, stop, R, W):
        self.P.op("pe", lambda e: e.matmul(out, lhsT=lhsT, rhs=rhs, start=start, stop=stop), R, W)

    def TR(self, out, in_, R, W):
        ident = self.identb.ap
        self.P.op("pe", lambda e: e.transpose(out=out, in_=in_, identity=ident), list(R) + [self.identb.b], W)

    def ACT(self, out, in_, func, R, W, **kw):
        self.P.op("act", lambda e: e.activation(out=out, in_=in_, func=func, **kw), R, W)

    def V(self, name, R, W, **kw):
        self.P.op("dve", lambda e: getattr(e, name)(**kw), R, W)

    def G(self, name, R, W, **kw):
        self.P.op("pool", lambda e: getattr(e, name)(**kw), R, W)

    def E(self, eng, name, R, W, **kw):
        self.P.op(eng, lambda e: getattr(e, name)(**kw), R, W)

    def LD(self, out, in_, R=(), W=(), **kw):
        self.P.dma("sp", out, in_, R, W, **kw)

    def ST(self, out, in_, R=(), W=(), **kw):
        self.P.dma("pool", out, in_, R, W, **kw)

    def psf(self, i):
        return self.ps[:, i, :]

    def psb(self, i):
        return self.ps[:, i, :].bitcast(BF16)

    def setup(self):
        A = self.A
        self.identf = A.alloc([128, 128], F32)
        self.identb = A.alloc([128, 128], BF16)
        self.eps = A.alloc([128, 1], F32)
        self.onesf = A.alloc([128, 128], F32)
        self.onesb = A.alloc([128, 128], BF16)
        self.LD(self.identf.ap, self.C["ident"], W=[self.identf.b])
        self.V("tensor_copy", [self.identf.b], [self.identb.b], out=self.identb.ap, in_=self.identf.ap)
        self.V("memset", [], [self.eps.b], ap=self.eps.ap, constant=1e-6)
        self.V("memset", [], [self.onesf.b], ap=self.onesf.ap, constant=1.0)
        self.V("memset", [], [self.onesb.b], ap=self.onesb.ap, constant=1.0)
        self.base_mark = A.mark()

    def phase_end(self):
        self.P.cut()
        self.A.reset(self.base_mark)

    def release(self, m):
        self.P.barrier()
        self.A.reset(m)

    def load_weight_bf16(self, dst, src, stage, kchunks, ncols, parts=128):
        sv = stage.ap[0:parts, 0:kchunks * ncols].rearrange("p (k n) -> p k n", k=kchunks)
        self.LD(sv, src.rearrange("(k p) n -> p k n", p=parts), W=[stage.b])
        self.G("tensor_copy", [stage.b], [dst.b], out=dst.ap, in_=sv)

    def rms_rstd(self, xt, junk, ss, rstd, n):
        self.ACT(junk.ap, xt.ap, AF.Square, [xt.b], [junk.b, ss.b], scale=float(n) ** -0.5, accum_out=ss.ap)
        self.ACT(rstd.ap, ss.ap, AF.Sqrt, [ss.b, self.eps.b], [rstd.b], bias=self.eps.ap, scale=1.0)
        self.V("reciprocal", [rstd.b], [rstd.b], out=rstd.ap, in_=rstd.ap)

    def norm_to_hT(self, src, gain, hT, ntile=NT, width=D, pbank=0):
        A = self.A
        m = A.mark()
        kc = width // 128
        gb = A.alloc([128, width], F32)
        self.LD(gb.ap, gain.partition_broadcast(128), W=[gb.b])
        xts = [A.alloc([128, width], F32) for _ in range(2)]
        junk = A.alloc([128, width], F32)
        hs = [A.alloc([128, width], BF16) for _ in range(2)]
        sss = [A.alloc([128, 1], F32) for _ in range(2)]
        rss = [A.alloc([128, 1], F32) for _ in range(2)]
        for t in range(ntile):
            xt, h, ss, rstd = xts[t % 2], hs[t % 2], sss[t % 2], rss[t % 2]
            self.LD(xt.ap, src[t * 128:(t + 1) * 128, :], R=[self.cur_x_b], W=[xt.b])
            self.rms_rstd(xt, junk, ss, rstd, width)
            self.V("scalar_tensor_tensor", [xt.b, rstd.b, gb.b], [h.b], out=h.ap, in0=xt.ap, scalar=rstd.ap[:, 0:1],
                   in1=gb.ap, op0=ALU.mult, op1=ALU.mult)
            bank = pbank + (t % 2)
            pt = self.psb(bank)[:, 0:kc * 128].rearrange("p (k n) -> p k n", k=kc)
            for k in range(kc):
                self.TR(pt[:, k, :], h.ap[:, k * 128:(k + 1) * 128], [h.b], [self.pb[bank]])
            self.ACT(hT.ap[:, :, t * 128:(t + 1) * 128], pt, AF.Copy, [self.pb[bank]], [hT.b])
        self.release(m)

    def ph_inproj(self, l, xsrc):
        A = self.A
        I = self.I
        w = I["w_in"][l]
        hT = A.alloc([128, 8, S], BF16)
        self.norm_to_hT(xsrc, I["norm_mix"][l], hT)
        wst = [A.alloc([128, 8 * 512], F32) for _ in range(2)]
        wbf = [A.alloc([128, 8, 512], BF16) for _ in range(2)]
        stg = [A.alloc([128, S], F32) for _ in range(2)]
        sig = [A.alloc([128, 512], F32) for _ in range(2)]
        self._u = 0
        self._s = 0
        self._pbk = 0

        def load_unit(ranges):
            i = self._u % 2
            self._u += 1
            off = 0
            ws, wb = wst[i], wbf[i]
            tot = sum(n for _, n in ranges)
            sv = ws.ap[:, 0:8 * tot].rearrange("p (k n) -> p k n", k=8)
            for (c0, n) in ranges:
                self.LD(sv[:, :, off:off + n], w[:, c0:c0 + n].rearrange("(k p) n -> p k n", p=128),
                        R=[wb.b], W=[ws.b])
                off += n
            self.G("tensor_copy", [ws.b], [wb.b], out=wb.ap[:, :, 0:tot], in_=sv)
            return wb

        def fm_mm(wb, off, m, tc, bank):
            for k in range(8):
                self.MM(self.psf(bank)[0:m, :], wb.ap[:, k, off:off + m], hT.ap[:, k, tc * 512:(tc + 1) * 512],
                        k == 0, k == 7, [wb.b, hT.b], [self.pb[bank]])

        def nb():
            b = self._pbk % 4
            self._pbk += 1
            return b

        def nstg():
            s = stg[self._s % 2]
            self._s += 1
            return s

        uT, qnT, kvT, krT, gmT = self.d_uT, self.d_qnT, self.d_kvT, self.d_krT, self.d_gmT
        for c in range(4):
            wb = load_unit([(c * 128, 128), (512 + c * 128, 128)])
            st = nstg()
            for tc in range(NQC):
                ba, bb = nb(), nb()
                fm_mm(wb, 0, 128, tc, ba)
                fm_mm(wb, 128, 128, tc, bb)
                sg = sig[tc % 2]
                self.ACT(sg.ap, self.psf(bb), AF.Sigmoid, [self.pb[bb]], [sg.b])
                self.V("tensor_tensor", [self.pb[ba], sg.b], [st.b], out=st.ap[:, tc * 512:(tc + 1) * 512],
                       in0=self.psf(ba), in1=sg.ap, op=ALU.mult)
            self.ST(uT[c * 128:(c + 1) * 128, :], st.ap, R=[st.b], W=[self.Db["uT"]])
        for hp in range(2):
            wb = load_unit([(1024 + hp * 256, 256)])
            for hh in range(4):
                h = hp * 4 + hh
                st = nstg()
                sb = st.ap[0:64, :].bitcast(BF16)
                for tc in range(NQC):
                    bk = nb()
                    fm_mm(wb, hh * 64, 64, tc, bk)
                    self.ACT(sb[:, tc * 512:(tc + 1) * 512], self.psf(bk)[0:64, :], AF.Copy, [self.pb[bk]], [st.b], scale=0.125)
                self.ST(qnT[h], sb[:, 0:S], R=[st.b], W=[self.Db["qnT"]])
        for si, slot in enumerate((0, 1, 2, 4)):
            wb = load_unit([(1536 + slot * 128, 128)])
            for g in range(2):
                st = nstg()
                sb = st.ap[0:64, :].bitcast(BF16)
                for tc in range(NQC):
                    bk = nb()
                    fm_mm(wb, g * 64, 64, tc, bk)
                    self.V("tensor_copy", [self.pb[bk]], [st.b], out=sb[:, tc * 512:(tc + 1) * 512], in_=self.psf(bk)[0:64, :])
                self.ST(kvT[si, g], sb[:, 0:S], R=[st.b], W=[self.Db["kvT"]])
        wb = load_unit([(2968, 64)])
        for hf in range(2):
            st = nstg()
            for tc in range(NQC):
                bk = nb()
                fm_mm(wb, hf * 32, 32, tc, bk)
                self.V("tensor_copy", [self.pb[bk]], [st.b], out=st.ap[0:32, tc * 512:(tc + 1) * 512], in_=self.psf(bk)[0:32, :])
            self.ST(krT[hf], st.ap[0:32, :], R=[st.b], W=[self.Db["krT"]])
        for cg in range(6):
            wb = load_unit([(3032 + cg * 512, 512)])
            for j in range(4):
                st = nstg()
                sb = st.ap.bitcast(BF16)
                for tc in range(NQC):
                    bk = nb()
                    fm_mm(wb, j * 128, 128, tc, bk)
                    self.ACT(sb[:, tc * 512:(tc + 1) * 512], self.psf(bk), AF.Sigmoid, [self.pb[bk]], [st.b])
                r0 = (cg * 4 + j) * 128
                self.ST(gmT[r0:r0 + 128, :], sb[:, 0:S], R=[st.b], W=[self.Db["gmT"]])
        wbv = load_unit([(1536 + 3 * 128, 128), (1536 + 5 * 128, 128)])
        wbm = load_unit([(2304, 512)])
        m = A.mark()
        ws3 = A.alloc([128, 8 * 152], F32)
        wb3 = A.alloc([128, 8, 152], BF16)
        sv3 = ws3.ap.rearrange("p (k n) -> p k n", k=8)
        self.LD(sv3, w[:, 2816:2968].rearrange("(k p) n -> p k n", p=128), W=[ws3.b])
        self.G("tensor_copy", [ws3.b], [wb3.b], out=wb3.ap, in_=sv3)
        vst = [A.alloc([128, 4, 65], BF16) for _ in range(2)]
        mst = [A.alloc([128, 664], F32) for _ in range(2)]
        for v_ in vst:
            self.V("memset", [], [v_.b], ap=v_.ap, constant=1.0)
        for t in range(NT):
            bv, bm, b3 = 4 + (t % 2) * 2, 5 + (t % 2) * 2, nb()
            vs, ms = vst[t % 2], mst[t % 2]
            for k in range(8):
                self.MM(self.psf(bv)[:, 0:256], hT.ap[:, k, t * 128:(t + 1) * 128], wbv.ap[:, k, 0:256], k == 0, k == 7,
                        [wbv.b, hT.b], [self.pb[bv]])
            for k in range(8):
                self.MM(self.psf(bm), hT.ap[:, k, t * 128:(t + 1) * 128], wbm.ap[:, k, 0:512], k == 0, k == 7,
                        [wbm.b, hT.b], [self.pb[bm]])
            for k in range(8):
                self.MM(self.psf(b3)[:, 0:152], hT.ap[:, k, t * 128:(t + 1) * 128], wb3.ap[:, k, :], k == 0, k == 7,
                        [wb3.b, hT.b], [self.pb[b3]])
            self.V("tensor_copy", [self.pb[bv]], [vs.b], out=vs.ap[:, :, 0:64],
                   in_=self.psf(bv)[:, 0:256].rearrange("p (a d) -> p a d", a=4))
            self.ACT(ms.ap[:, 0:24], self.psf(bm)[:, 0:24], AF.Sigmoid, [self.pb[bm]], [ms.b])
            self.V("tensor_copy", [self.pb[bm]], [ms.b], out=ms.ap[:, 24:512], in_=self.psf(bm)[:, 24:512])
            self.ACT(ms.ap[:, 512:664], self.psf(b3)[:, 0:152], AF.Copy, [self.pb[b3]], [ms.b])
            self.ST(self.d_vtm[t * 128:(t + 1) * 128], vs.ap, R=[vs.b], W=[self.Db["vtm"]])
            self.ST(self.d_misc[t * 128:(t + 1) * 128, :], ms.ap, R=[ms.b], W=[self.Db["misc"]])
        self.phase_end()

    def col_load(self, dst, src1d, c0, n=128):
        self.LD(dst, src1d[c0:c0 + n].rearrange("(p o) -> p o", o=1))

    def ph_conv(self, l):
        A, I = self.A, self.I
        cwn = A.alloc([31, 512], F32)
        self.LD(cwn.ap, I["conv_w"][l], W=[cwn.b])
        cwT = A.alloc([128, 4, 32], F32)
        for c in range(4):
            self.MM(self.psf(0)[:, c * 32:c * 32 + 31], cwn.ap[:, c * 128:(c + 1) * 128], self.identf.ap[0:31, 0:31],
                    True, True, [cwn.b, self.identf.b], [self.pb[0]])
        self.V("tensor_copy", [self.pb[0]], [cwT.b], out=cwT.ap[:, :, 0:31],
               in_=self.psf(0)[:, 0:128].rearrange("p (c k) -> p c k", c=4)[:, :, 0:31])
        vecs = A.alloc([128, 3, 4], F32)
        for vi, nm in enumerate(("conv_b", "conv_ln_g", "conv_ln_b")):
            for c in range(4):
                self.P.dma("sp", vecs.ap[:, vi, c:c + 1], I[nm][l][c * 128:(c + 1) * 128].rearrange("(p o) -> p o", o=1),
                           W=[vecs.b])
        accs = A.alloc([128, 4, S], F32)
        accb = [Buf() for _ in range(4)]
        ubuf = [A.alloc([128, 30 + S], F32) for _ in range(2)]
        for u in ubuf:
            self.G("memset", [], [u.b], ap=u.ap[:, 0:30], constant=0.0)
        for c in range(4):
            ub = ubuf[c % 2]
            self.LD(ub.ap[:, 30:30 + S], self.d_uT[c * 128:(c + 1) * 128, :], R=[self.Db["uT"]], W=[ub.b])
            acc = accs.ap[:, c, :]
            self.V("tensor_scalar", [ub.b, cwT.b, vecs.b], [accb[c]], out=acc, in0=ub.ap[:, 0:S], scalar1=cwT.ap[:, c, 0:1],
                   scalar2=vecs.ap[:, 0, c:c + 1], op0=ALU.mult, op1=ALU.add)
            for k in range(1, 31):
                self.V("scalar_tensor_tensor", [ub.b, cwT.b, accb[c]], [accb[c]], out=acc, in0=ub.ap[:, k:k + S],
                       scalar=cwT.ap[:, c, k:k + 1], in1=acc, op0=ALU.mult, op1=ALU.add)
        uact = A.alloc([128, 4, S], BF16)
        tmp = {n: [A.alloc([128, 512], F32) for _ in range(2)] for n in ("sq", "mean", "msq", "var", "t")}
        for tc in range(NQC):
            sl = slice(tc * 512, (tc + 1) * 512)
            i2 = tc % 2
            b1, b2 = 1 + 2 * i2, 2 + 2 * i2
            for c in range(4):
                self.MM(self.psf(b1), self.onesf.ap, accs.ap[:, c, sl], c == 0, c == 3, [self.onesf.b, accb[c]], [self.pb[b1]])
            for c in range(4):
                sq = tmp["sq"][c % 2]
                self.ACT(sq.ap, accs.ap[:, c, sl], AF.Square, [accb[c]], [sq.b])
                self.MM(self.psf(b2), self.onesf.ap, sq.ap, c == 0, c == 3, [self.onesf.b, sq.b], [self.pb[b2]])
            mean, msq, var = tmp["mean"][i2], tmp["msq"][i2], tmp["var"][i2]
            self.ACT(mean.ap, self.psf(b1), AF.Copy, [self.pb[b1]], [mean.b], scale=1.0 / 512)
            self.V("tensor_tensor", [mean.b], [msq.b], out=msq.ap, in0=mean.ap, in1=mean.ap, op=ALU.mult)
            self.V("scalar_tensor_tensor", [self.pb[b2], msq.b], [var.b], out=var.ap, in0=self.psf(b2), scalar=1.0 / 512,
                   in1=msq.ap, op0=ALU.mult, op1=ALU.subtract)
            self.ACT(var.ap, var.ap, AF.Sqrt, [var.b, self.eps.b], [var.b], bias=self.eps.ap, scale=1.0)
            self.V("reciprocal", [var.b], [var.b], out=var.ap, in_=var.ap)
            for c in range(4):
                t = tmp["t"][c % 2]
                self.V("tensor_tensor", [accb[c], mean.b], [t.b], out=t.ap, in0=accs.ap[:, c, sl], in1=mean.ap, op=ALU.subtract)
                self.V("tensor_tensor", [t.b, var.b], [t.b], out=t.ap, in0=t.ap, in1=var.ap, op=ALU.mult)
                self.ACT(uact.ap[:, c, sl], t.ap, AF.Silu, [t.b, vecs.b], [uact.b], scale=vecs.ap[:, 1, c:c + 1],
                         bias=vecs.ap[:, 2, c:c + 1])
        for c in range(4):
            self.ST(self.d_uactT[c * 128:(c + 1) * 128, :], uact.ap[:, c, :], R=[uact.b], W=[self.Db["uactT"]])
        self.phase_end()

    def ph_bias_tables(self):
        A, I, C = self.A, self.I, self.C
        rb = A.alloc([33, 8], F32)
        self.V("memset", [], [rb.b], ap=rb.ap, constant=1.0)
        self.LD(rb.ap[0:32, :], I["rel_bias"], R=[rb.b], W=[rb.b])
        ohv = A.alloc([33, 384], F32)
        self.LD(ohv.ap, C["oh_v"], W=[ohv.b])
        ohc = A.alloc([33, 2048], F32)
        self.LD(ohc.ap, C["oh_c"], W=[ohc.b])
        vrep = [A.alloc([128, 384], BF16) for _ in range(2)]
        for h in range(8):
            bk = h % 2
            vr = vrep[h % 2]
            self.MM(self.psf(bk)[:, 0:384], rb.ap[:, h:h + 1].to_broadcast([33, 128]), ohv.ap, True, True, [rb.b, ohv.b], [self.pb[bk]])
            self.V("tensor_copy", [self.pb[bk]], [vr.b], out=vr.ap, in_=self.psf(bk)[:, 0:384])
            self.ST(self.d_vs[h], vr.ap, R=[vr.b], W=[self.Db["vs"]])
        bcs = A.alloc([8, 2048], BF16)
        for j in range(4):
            bk = 2 + j % 2
            self.MM(self.psf(bk)[0:8, :], rb.ap, ohc.ap[:, j * 512:(j + 1) * 512], True, True, [rb.b, ohc.b], [self.pb[bk]])
            self.V("tensor_copy", [self.pb[bk]], [bcs.b], out=bcs.ap[:, j * 512:(j + 1) * 512], in_=self.psf(bk)[0:8, :])
        self.ST(self.d_bc, bcs.ap, R=[bcs.b], W=[self.Db["bc"]])
        self.phase_end()

    def ph_rope_tables(self):
        A, I, C = self.A, self.I, self.C
        posi = A.alloc([32, S], I32)
        self.LD(posi.ap, I["positions"].partition_broadcast(32), W=[posi.b])
        inv = A.alloc([32, 1], F32)
        self.LD(inv.ap, C["invfreq"], W=[inv.b])
        turns = A.alloc([32, S], F32)
        self.V("tensor_copy", [posi.b], [turns.b], out=turns.ap, in_=posi.ap)
        self.V("tensor_scalar", [turns.b, inv.b], [turns.b], out=turns.ap, in0=turns.ap, scalar1=inv.ap[:, 0:1], scalar2=None,
               op0=ALU.mult)
        r = A.alloc([32, S], F32)
        ti = A.alloc([32, S], I32)
        tf = A.alloc([32, S], F32)
        fl = A.alloc([32, S], F32)
        res = A.alloc([32, S], F32)
        qs = 192.0 ** -0.5
        for idx, shift in ((1, 0.0), (0, 0.25)):
            self.V("tensor_scalar", [turns.b], [r.b], out=r.ap, in0=turns.ap, scalar1=shift, scalar2=None, op0=ALU.add)
            self.V("tensor_copy", [r.b], [ti.b], out=ti.ap, in_=r.ap)
            self.V("tensor_copy", [ti.b], [tf.b], out=tf.ap, in_=ti.ap)
            self.V("tensor_tensor", [r.b, tf.b], [r.b], out=r.ap, in0=r.ap, in1=tf.ap, op=ALU.subtract)
            self.V("tensor_scalar", [r.b], [fl.b], out=fl.ap, in0=r.ap, scalar1=0.5, scalar2=None, op0=ALU.is_gt)
            self.V("tensor_tensor", [r.b, fl.b], [r.b], out=r.ap, in0=r.ap, in1=fl.ap, op=ALU.subtract)
            self.V("tensor_scalar", [r.b], [fl.b], out=fl.ap, in0=r.ap, scalar1=-0.5, scalar2=None, op0=ALU.is_lt)
            self.V("tensor_tensor", [r.b, fl.b], [r.b], out=r.ap, in0=r.ap, in1=fl.ap, op=ALU.add)
            self.ACT(res.ap, r.ap, AF.Sin, [r.b], [res.b], scale=2.0 * math.pi)
            self.ST(self.d_cs[idx], res.ap, R=[res.b], W=[self.Db["cs"]])
            self.V("tensor_scalar", [res.b], [tf.b], out=tf.ap, in0=res.ap, scalar1=qs, scalar2=None, op0=ALU.mult)
            self.ST(self.d_cs[2 + idx], tf.ap, R=[tf.b], W=[self.Db["cs"]])
        self.phase_end()
    def gelu_tanh(self, out_bf, ps_in, bias_col, n, tmps):
        x, x2, s = tmps
        self.ACT(x.ap[:, 0:n], ps_in, AF.Identity, [self._gb, self._gc], [x.b], bias=bias_col, scale=1.0)
        self.V("tensor_tensor", [x.b], [x2.b], out=x2.ap[:, 0:n], in0=x.ap[:, 0:n], in1=x.ap[:, 0:n], op=ALU.mult)
        self.V("tensor_scalar", [x2.b], [x2.b], out=x2.ap[:, 0:n], in0=x2.ap[:, 0:n], scalar1=0.044715, scalar2=1.0,
               op0=ALU.mult, op1=ALU.add)
        self.V("tensor_tensor", [x2.b, x.b], [x2.b], out=x2.ap[:, 0:n], in0=x2.ap[:, 0:n], in1=x.ap[:, 0:n], op=ALU.mult)
        self.ACT(s.ap[:, 0:n], x2.ap[:, 0:n], AF.Sigmoid, [x2.b], [s.b], scale=2.0 * math.sqrt(2.0 / math.pi))
        self.V("tensor_tensor", [x.b, s.b], [self._go], out=out_bf, in0=x.ap[:, 0:n], in1=s.ap[:, 0:n], op=ALU.mult)

    def ph_nsa(self, l):
        A, I, C = self.A, self.I, self.C
        kcmpT = A.alloc([64, 2, 256], BF16)
        vcmp = A.alloc([128, 2, 2, 64], BF16)
        self.G("memset", [], [kcmpT.b], ap=kcmpT.ap, constant=0.0)
        self.G("memset", [], [vcmp.b], ap=vcmp.ap, constant=0.0)
        onsa = A.alloc([128, NT, 512], F32)
        onsab = [Buf() for _ in range(NT)]
        m0 = A.mark()
        stg = A.alloc([64, 32 * 256], F32)
        w1s = A.alloc([64, 32, 256], BF16)
        w2f = A.alloc([128, 2, 64], F32)
        w2s = A.alloc([128, 2, 64], BF16)
        posn = A.alloc([32, 64], F32)
        posT = A.alloc([64, 32], BF16)
        b1 = A.alloc([128, 2], F32)
        c1 = A.alloc([128, 2], F32)
        srcT = [A.alloc([64, S], BF16) for _ in range(2)]
        hact = A.alloc([128, 2, 256], BF16)
        tmps = [A.alloc([128, 256], F32) for _ in range(3)]
        for kv_i, nm in enumerate(("k", "v")):
            sv = stg.ap.rearrange("p (l h) -> p l h", l=32)
            self.LD(sv, I[f"cmp_w1_{nm}"][l].rearrange("(l d) h -> d l h", d=64), R=[w1s.b], W=[stg.b])
            self.G("tensor_copy", [stg.b], [w1s.b], out=w1s.ap, in_=sv)
            self.LD(w2f.ap, I[f"cmp_w2_{nm}"][l].rearrange("(c p) n -> p c n", p=128), R=[w2s.b], W=[w2f.b])
            self.G("tensor_copy", [w2f.b], [w2s.b], out=w2s.ap, in_=w2f.ap)
            self.LD(posn.ap, I[f"cmp_pos_{nm}"][l], W=[posn.b])
            self.MM(self.psf(0)[0:64, 0:32], posn.ap, self.identf.ap[0:32, 0:32], True, True, [posn.b, self.identf.b], [self.pb[0]])
            self.V("tensor_copy", [self.pb[0]], [posT.b], out=posT.ap, in_=self.psf(0)[0:64, 0:32])
            for hc in range(2):
                self.P.dma("sp", b1.ap[:, hc:hc + 1], I[f"cmp_b1_{nm}"][l][hc * 128:(hc + 1) * 128].rearrange("(p o) -> p o", o=1),
                           R=[b1.b], W=[b1.b])
            for hc in range(2):
                for li in range(32):
                    self.MM(self.psf(1)[:, hc:hc + 1], w1s.ap[:, li, hc * 128:(hc + 1) * 128], posT.ap[:, li:li + 1],
                            li == 0, li == 31, [w1s.b, posT.b], [self.pb[1]])
            self.V("tensor_tensor", [self.pb[1], b1.b], [c1.b], out=c1.ap, in0=self.psf(1)[:, 0:2], in1=b1.ap, op=ALU.add)
            for g in range(2):
                st = srcT[g]
                self.LD(st.ap, self.d_kvT[kv_i, g], R=[self.Db["kvT"]], W=[st.b])
                s3 = st.ap.rearrange("p (c r) -> p c r", r=16)
                for hc in range(2):
                    bk = 2 + hc
                    for li in range(32):
                        a, r = li // 16, li % 16
                        self.MM(self.psf(bk)[:, 0:255], w1s.ap[:, li, hc * 128:(hc + 1) * 128], s3[:, a:a + 255, r],
                                li == 0, li == 31, [w1s.b, st.b], [self.pb[bk]])
                    self._gb, self._gc, self._go = self.pb[bk], c1.b, hact.b
                    self.gelu_tanh(hact.ap[:, hc, 0:255], self.psf(bk)[:, 0:255], c1.ap[:, hc:hc + 1], 255, tmps)
                if nm == "k":
                    for hc in range(2):
                        self.MM(self.psf(4)[0:64, 0:255], w2s.ap[:, hc, :], hact.ap[:, hc, 0:255], hc == 0, hc == 1,
                                [w2s.b, hact.b], [self.pb[4]])
                    self.V("tensor_copy", [self.pb[4]], [kcmpT.b], out=kcmpT.ap[:, g, 0:255], in_=self.psf(4)[0:64, 0:255])
                else:
                    for cb in range(2):
                        n = 128 if cb == 0 else 127
                        for hc in range(2):
                            self.MM(self.psf(5 + cb)[0:n, 0:64], hact.ap[:, hc, cb * 128:cb * 128 + n], w2s.ap[:, hc, :], hc == 0,
                                    hc == 1, [w2s.b, hact.b], [self.pb[5 + cb]])
                        self.V("tensor_copy", [self.pb[5 + cb]], [vcmp.b], out=vcmp.ap[0:n, g, cb, :], in_=self.psf(5 + cb)[0:n, 0:64])
        self.release(m0)
        gates = A.alloc([128, NT, 24], F32)
        self.LD(gates.ap, self.d_misc.rearrange("(t p) c -> p t c", p=128)[:, :, 0:24], R=[self.Db["misc"]], W=[gates.b])
        ebig = A.alloc([64, S], BF16)
        cover = A.alloc([128, 2, 64], BF16)
        emask = A.alloc([128, 128], BF16)
        cflag = A.alloc([128, 1], F32)
        self.LD(cflag.ap, C["cflag"][0:128, :], W=[cflag.b])
        m1 = A.mark()
        tmpf = A.alloc([64, S], F32)
        self.LD(tmpf.ap, C["ebig"], W=[tmpf.b])
        self.G("tensor_copy", [tmpf.b], [ebig.b], out=ebig.ap, in_=tmpf.ap)
        cvf = A.alloc([128, 2, 64], F32)
        self.LD(cvf.ap, C["cover"].rearrange("(c p) n -> p c n", p=128), W=[cvf.b])
        self.G("tensor_copy", [cvf.b], [cover.b], out=cover.ap, in_=cvf.ap)
        emf = A.alloc([128, 128], F32)
        self.LD(emf.ap, C["emask"], W=[emf.b])
        self.G("tensor_copy", [emf.b], [emask.b], out=emask.ap, in_=emf.ap)
        self.release(m1)
        qT = A.alloc([64, 4, S], BF16)
        kslc = A.alloc([64, S], BF16)
        kwin = A.alloc([64, S], BF16)
        vslc = A.alloc([128, NT, 65], BF16)
        vwin = A.alloc([128, NT, 65], BF16)
        strip = A.alloc([128, 4, 1024], BF16)
        bct = A.alloc([128, 4, 16], BF16)
        selbT = A.alloc([64, S], BF16)
        pTs = [A.alloc([128, 512], BF16) for _ in range(3)]
        pTfar = [A.alloc([128, 128], BF16) for _ in range(2)]
        pex = [A.alloc([128, 256], F32) for _ in range(2)]
        pcs = [A.alloc([128, 256], BF16) for _ in range(2)]
        pcT = [A.alloc([128, 2, 128], BF16) for _ in range(2)]
        sm = [A.alloc([128, 8], F32) for _ in range(4)]
        selA = [A.alloc([128, 64], F32) for _ in range(2)]
        selB = [A.alloc([128, 64], F32) for _ in range(2)]
        sc_t = [A.alloc([128, 64], F32) for _ in range(2)]
        s2_t = [A.alloc([128, 64], F32) for _ in range(2)]
        m8 = [A.alloc([128, 8], F32) for _ in range(2)]
        sbf = [A.alloc([128, 64], BF16) for _ in range(2)]
        self._smi = 0

        def small():
            t = sm[self._smi % 4]
            self._smi += 1
            return t
        vview = self.d_vtm.rearrange("(t p) a e -> p t a e", p=128)
        for g in range(2):
            for hh in range(4):
                self.LD(qT.ap[:, hh, :], self.d_qnT[g * 4 + hh], R=[self.Db["qnT"]], W=[qT.b])
            self.LD(kslc.ap, self.d_kvT[2, g], R=[self.Db["kvT"]], W=[kslc.b])
            self.LD(kwin.ap, self.d_kvT[3, g], R=[self.Db["kvT"]], W=[kwin.b])
            self.LD(vslc.ap, vview[:, :, g, :], R=[self.Db["vtm"]], W=[vslc.b])
            self.LD(vwin.ap, vview[:, :, 2 + g, :], R=[self.Db["vtm"]], W=[vwin.b])
            self.G("memset", [], [strip.b], ap=strip.ap[:, :, 0:384], constant=NEG)
            self.G("memset", [], [strip.b], ap=strip.ap[:, :, 640:1024], constant=0.0)
            for hh in range(4):
                h = g * 4 + hh
                for o in range(2):
                    src = AP(self.Dr["vs"], h * 128 * 384 + o * 128 + 127, [[383, 128], [1, 128]])
                    self.LD(strip.ap[:, hh, (3 + o) * 128:(4 + o) * 128], src, R=[self.Db["vs"]], W=[strip.b])
                self.LD(bct.ap[:, hh, :], self.d_bc[h].rearrange("(q c) -> q c", c=16), R=[self.Db["bc"]], W=[bct.b])
            for qt in range(NT):
                ncq = min(8 * qt + 8, 255)
                bs = 8 * qt - 8
                c_lo, c_hi = max(bs, 0), min(bs + 16, 255)
                nblk = (ncq + 127) // 128
                bimp = 6 + qt % 2
                sa, sb_ = selA[qt % 2], selB[qt % 2]
                self.LD(sa.ap, C["selA"][qt * 128:(qt + 1) * 128, :], W=[sa.b])
                self.LD(sb_.ap, C["selB"][qt * 128:(qt + 1) * 128, :], W=[sb_.b])
                for hh in range(4):
                    h = g * 4 + hh
                    i2 = (qt * 4 + hh) % 2
                    bsc, btr, bo = i2, 2 + i2, 4 + i2
                    sc = self.psf(bsc)
                    self.MM(sc[:, 0:ncq], qT.ap[:, hh, qt * 128:(qt + 1) * 128], kcmpT.ap[:, g, 0:ncq], True, False,
                            [qT.b, kcmpT.b], [self.pb[bsc]])
                    self.MM(sc[:, c_lo:c_hi], self.identb.ap, bct.ap[:, hh, c_lo - bs:c_hi - bs], False, True,
                            [self.identb.b, bct.b], [self.pb[bsc]])
                    mx = small()
                    self.V("reduce_max", [self.pb[bsc]], [mx.b], out=mx.ap[:, 0:1], in_=sc[:, 0:ncq], axis=AX.X)
                    self.V("tensor_scalar", [mx.b], [mx.b], out=mx.ap[:, 1:2], in0=mx.ap[:, 0:1], scalar1=-1.0, scalar2=None, op0=ALU.mult)
                    pe_, pc_, pt_ = pex[i2], pcs[i2], pcT[i2]
                    self.ACT(pe_.ap[:, 0:ncq], sc[:, 0:ncq], AF.Exp, [self.pb[bsc], mx.b], [pe_.b, mx.b], bias=mx.ap[:, 1:2], scale=1.0,
                             accum_out=mx.ap[:, 2:3])
                    self.V("reciprocal", [mx.b], [mx.b], out=mx.ap[:, 3:4], in_=mx.ap[:, 2:3])
                    if qt == 0:
                        self.V("tensor_tensor", [mx.b, cflag.b], [mx.b], out=mx.ap[:, 3:4], in0=mx.ap[:, 3:4], in1=cflag.ap, op=ALU.mult)
                    self.G("memset", [], [pc_.b], ap=pc_.ap, constant=0.0)
                    self.V("tensor_scalar", [pe_.b, mx.b], [pc_.b], out=pc_.ap[:, 0:ncq], in0=pe_.ap[:, 0:ncq], scalar1=mx.ap[:, 3:4],
                           scalar2=None, op0=ALU.mult)
                    ptb = self.psb(btr)[:, 0:256].rearrange("p (j n) -> p j n", j=2)
                    for j in range(nblk):
                        self.TR(ptb[:, j, :], pc_.ap[:, j * 128:(j + 1) * 128], [pc_.b], [self.pb[btr]])
                    self.ACT(pt_.ap[:, 0:nblk, :], ptb[:, 0:nblk, :], AF.Copy, [self.pb[btr]], [pt_.b])
                    for j in range(nblk):
                        self.MM(self.psf(bo)[:, 0:64], pt_.ap[:, j, :], vcmp.ap[:, g, j, :], j == 0, j == nblk - 1,
                                [pt_.b, vcmp.b], [self.pb[bo]])
                    for j in range(nblk):
                        self.MM(self.psf(bimp)[:, 0:64], pt_.ap[:, j, :], cover.ap[:, j, :], hh == 0 and j == 0,
                                hh == 3 and j == nblk - 1, [pt_.b, cover.b], [self.pb[bimp]])
                    self.ACT(onsa.ap[:, qt, h * 64:(h + 1) * 64], self.psf(bo)[:, 0:64], AF.Copy, [self.pb[bo], gates.b], [onsab[qt]],
                             scale=gates.ap[:, qt, h * 3:h * 3 + 1])
                sct, s2, m8a, sbb = sc_t[qt % 2], s2_t[qt % 2], m8[qt % 2], sbf[qt % 2]
                t = small()
                self.V("tensor_tensor", [self.pb[bimp], sa.b], [sct.b], out=sct.ap, in0=self.psf(bimp)[:, 0:64], in1=sa.ap, op=ALU.mult)
                self.V("tensor_tensor", [sct.b, sb_.b], [sct.b], out=sct.ap, in0=sct.ap, in1=sb_.ap, op=ALU.add)
                self.V("max", [sct.b], [m8a.b], out=m8a.ap, in_=sct.ap)
                self.V("tensor_reduce", [m8a.b], [t.b], out=t.ap[:, 0:1], in_=m8a.ap, axis=AX.X, op=ALU.min)
                self.V("tensor_scalar", [sct.b, t.b], [s2.b], out=s2.ap, in0=sct.ap, scalar1=t.ap[:, 0:1], scalar2=-1e9,
                       op0=ALU.is_ge, op1=ALU.mult)
                self.V("tensor_tensor", [s2.b, sct.b], [s2.b], out=s2.ap, in0=s2.ap, in1=sct.ap, op=ALU.add)
                self.V("max", [s2.b], [m8a.b], out=m8a.ap, in_=s2.ap)
                self.V("tensor_reduce", [m8a.b], [t.b], out=t.ap[:, 1:2], in_=m8a.ap, axis=AX.X, op=ALU.min)
                self.V("tensor_scalar", [t.b], [t.b], out=t.ap[:, 2:3], in0=t.ap[:, 1:2], scalar1=0.0, scalar2=None, op0=ALU.max)
                self.V("tensor_scalar", [sct.b, t.b], [s2.b], out=s2.ap, in0=sct.ap, scalar1=t.ap[:, 2:3], scalar2=None, op0=ALU.is_ge)
                self.V("tensor_scalar", [s2.b], [sbb.b], out=sbb.ap, in0=s2.ap, scalar1=-1.0, scalar2=-NEG, op0=ALU.add, op1=ALU.mult)
                btr = 2 + qt % 2
                self.TR(self.psb(btr)[0:64, 0:128], sbb.ap, [sbb.b], [self.pb[btr]])
                self.V("tensor_copy", [self.pb[btr]], [selbT.b], out=selbT.ap[:, qt * 128:(qt + 1) * 128], in_=self.psb(btr)[0:64, 0:128])
            for hh in range(4):
                h = g * 4 + hh
                for qc in range(NQC):
                    bo = 4 + qc % 2
                    O = self.psf(bo)[:, 0:260].rearrange("p (j e) -> p j e", j=4)
                    nk = 4 * qc + 4

                    def emit_sc(kt, hh=hh, qc=qc):
                        bsc = kt % 3
                        sc = self.psf(bsc)
                        near = kt >= 4 * qc - 1
                        self.MM(sc, kslc.ap[:, kt * 128:(kt + 1) * 128], qT.ap[:, hh, qc * 512:(qc + 1) * 512], True, False,
                                [kslc.b, qT.b], [self.pb[bsc]])
                        self.MM(sc, ebig.ap[:, kt * 128:(kt + 1) * 128], selbT.ap[:, qc * 512:(qc + 1) * 512], False, not near,
                                [ebig.b, selbT.b], [self.pb[bsc]])
                        if near:
                            o = kt - 4 * qc
                            self.MM(sc, self.identb.ap, strip.ap[:, hh, (3 - o) * 128:(3 - o) * 128 + 512], False, True,
                                    [self.identb.b, strip.b], [self.pb[bsc]])
                        pT = pTs[kt % 3]
                        self.ACT(pT.ap, sc, AF.Exp, [self.pb[bsc]], [pT.b])
                        return pT

                    def emit_pv(kt, pT, qc=qc, O=O, bo=bo):
                        for j in range(4):
                            qt = 4 * qc + j
                            if kt > qt:
                                continue
                            self.MM(O[:, j, :], pT.ap[:, j * 128:(j + 1) * 128], vslc.ap[:, kt, :], kt == 0 and j == 0, kt == qt,
                                    [pT.b, vslc.b], [self.pb[bo]])
                    pend = None
                    for kt in range(nk):
                        pT = emit_sc(kt)
                        if pend is not None:
                            emit_pv(*pend)
                        pend = (kt, pT)
                    emit_pv(*pend)
                    for j in range(4):
                        qt = 4 * qc + j
                        t = small()
                        self.V("reciprocal", [self.pb[bo]], [t.b], out=t.ap[:, 0:1], in_=O[:, j, 64:65])
                        self.V("tensor_tensor", [t.b, gates.b], [t.b], out=t.ap[:, 1:2], in0=t.ap[:, 0:1],
                               in1=gates.ap[:, qt, h * 3 + 1:h * 3 + 2], op=ALU.mult)
                        osl = onsa.ap[:, qt, h * 64:(h + 1) * 64]
                        self.V("scalar_tensor_tensor", [self.pb[bo], t.b, onsab[qt]], [onsab[qt]], out=osl, in0=O[:, j, 0:64],
                               scalar=t.ap[:, 1:2], in1=osl, op0=ALU.mult, op1=ALU.add)
                def win_a(qt, hh=hh):
                    main = [kt for kt in range(qt - 3, qt + 1) if kt >= 0]
                    far = qt - 4
                    pTf = None
                    if far >= 0:
                        bsc = 3
                        sc = self.psf(bsc)[:, 0:128]
                        self.MM(sc, kwin.ap[:, far * 128:(far + 1) * 128], qT.ap[:, hh, qt * 128:(qt + 1) * 128], True, False,
                                [kwin.b, qT.b], [self.pb[bsc]])
                        self.MM(sc, self.identb.ap, emask.ap, False, True, [self.identb.b, emask.b], [self.pb[bsc]])
                        pTf = pTfar[qt % 2]
                        self.ACT(pTf.ap, sc, AF.Exp, [self.pb[bsc]], [pTf.b])
                    bsc = qt % 3
                    sc = self.psf(bsc)
                    for i, kt in enumerate(main):
                        nb_ = kt >= qt - 1
                        self.MM(sc[:, i * 128:(i + 1) * 128], kwin.ap[:, kt * 128:(kt + 1) * 128], qT.ap[:, hh, qt * 128:(qt + 1) * 128],
                                True, not nb_, [kwin.b, qT.b], [self.pb[bsc]])
                        if nb_:
                            o = qt - kt
                            self.MM(sc[:, i * 128:(i + 1) * 128], self.identb.ap, strip.ap[:, hh, (3 + o) * 128:(4 + o) * 128], False, True,
                                    [self.identb.b, strip.b], [self.pb[bsc]])
                    pT = pTs[qt % 3]
                    nm_ = len(main) * 128
                    self.ACT(pT.ap[:, 0:nm_], sc[:, 0:nm_], AF.Exp, [self.pb[bsc]], [pT.b])
                    return (qt, main, far, pTf, pT)

                def win_b(qt, main, far, pTf, pT, hh=hh, h=h):
                    bo = 6 + qt % 2
                    O = self.psf(bo)[:, 0:65]
                    nmm = len(main) + (1 if far >= 0 else 0)
                    done = 0
                    if far >= 0:
                        self.MM(O, pTf.ap, vwin.ap[:, far, :], True, False, [pTf.b, vwin.b], [self.pb[bo]])
                        done = 1
                    for i, kt in enumerate(main):
                        self.MM(O, pT.ap[:, i * 128:(i + 1) * 128], vwin.ap[:, kt, :], done == 0, done == nmm - 1,
                                [pT.b, vwin.b], [self.pb[bo]])
                        done += 1
                    t = small()
                    self.V("reciprocal", [self.pb[bo]], [t.b], out=t.ap[:, 0:1], in_=O[:, 64:65])
                    self.V("tensor_tensor", [t.b, gates.b], [t.b], out=t.ap[:, 1:2], in0=t.ap[:, 0:1],
                           in1=gates.ap[:, qt, h * 3 + 2:h * 3 + 3], op=ALU.mult)
                    osl = onsa.ap[:, qt, h * 64:(h + 1) * 64]
                    self.V("scalar_tensor_tensor", [self.pb[bo], t.b, onsab[qt]], [onsab[qt]], out=osl, in0=O[:, 0:64],
                           scalar=t.ap[:, 1:2], in1=osl, op0=ALU.mult, op1=ALU.add)
                pend = None
                for qt in range(NT):
                    st_ = win_a(qt)
                    if pend is not None:
                        win_b(*pend)
                    pend = st_
                win_b(*pend)
        self.release(m0)
        oT = A.alloc([128, 4, S], BF16)
        ob = [A.alloc([128, 512], BF16) for _ in range(2)]
        for qt in range(NT):
            o_ = ob[qt % 2]
            self.V("tensor_copy", [onsab[qt]], [o_.b], out=o_.ap, in_=onsa.ap[:, qt, :])
            bk = qt % 2
            pt = self.psb(bk)[:, 0:512].rearrange("p (k n) -> p k n", k=4)
            for k in range(4):
                self.TR(pt[:, k, :], o_.ap[:, k * 128:(k + 1) * 128], [o_.b], [self.pb[bk]])
            self.ACT(oT.ap[:, :, qt * 128:(qt + 1) * 128], pt, AF.Copy, [self.pb[bk]], [oT.b])
        for k in range(4):
            self.ST(self.d_onsaT[k * 128:(k + 1) * 128, :], oT.ap[:, k, :], R=[oT.b], W=[self.Db["onsaT"]])
        if "onsa_dbg" in self.debug:
            self.ST(self.d_onsa_dbg.rearrange("(t p) c -> p t c", p=128), onsa.ap, R=onsab, W=[Buf()])
        self.phase_end()
    def ph_mla(self, l):
        A, I, C = self.A, self.I, self.C
        qs = 192.0 ** -0.5
        cqT = A.alloc([128, 3, S], BF16)
        ckvT = A.alloc([128, 2, S], BF16)
        self.cur_x_b = self.Db["misc"]
        self.norm_to_hT(self.d_misc[:, 24:408], I["mla_norm_q"][l], cqT, width=384)
        self.norm_to_hT(self.d_misc[:, 408:664], I["mla_norm_kv"][l], ckvT, width=256)
        tt = [A.alloc([32, 512], F32) for _ in range(4)]
        cs = A.alloc([32, 2, S], F32)
        rst = [A.alloc([32, 2, S], BF16) for _ in range(2)]

        def rope(x1, x2, R1, R2, co, si, o1, o2, ob, n):
            t1, t2, t3, t4 = tt
            self.V("tensor_tensor", R1 + [cs.b], [t1.b], out=t1.ap[:, 0:n], in0=x1, in1=co, op=ALU.mult)
            self.V("tensor_tensor", R2 + [cs.b], [t2.b], out=t2.ap[:, 0:n], in0=x2, in1=si, op=ALU.mult)
            self.V("tensor_tensor", [t1.b, t2.b], [ob], out=o1, in0=t1.ap[:, 0:n], in1=t2.ap[:, 0:n], op=ALU.subtract)
            self.V("tensor_tensor", R2 + [cs.b], [t3.b], out=t3.ap[:, 0:n], in0=x2, in1=co, op=ALU.mult)
            self.V("tensor_tensor", R1 + [cs.b], [t4.b], out=t4.ap[:, 0:n], in0=x1, in1=si, op=ALU.mult)
            self.V("tensor_tensor", [t3.b, t4.b], [ob], out=o2, in0=t3.ap[:, 0:n], in1=t4.ap[:, 0:n], op=ALU.add)

        m0 = A.mark()
        for i in range(2):
            self.LD(cs.ap[:, i, :], self.d_cs[i], R=[self.Db["cs"]], W=[cs.b])
        kr = A.alloc([32, 2, S], F32)
        for hf in range(2):
            self.LD(kr.ap[:, hf, :], self.d_krT[hf], R=[self.Db["krT"]], W=[kr.b])
        st = rst[0]
        for tc in range(NQC):
            sl = slice(tc * 512, (tc + 1) * 512)
            rope(kr.ap[:, 0, sl], kr.ap[:, 1, sl], [kr.b], [kr.b], cs.ap[:, 0, sl], cs.ap[:, 1, sl], st.ap[:, 0, sl], st.ap[:, 1, sl],
                 st.b, 512)
        self.ST(self.d_kpe[0], st.ap[:, 0, :], R=[st.b], W=[self.Db["kpe"]])
        self.ST(self.d_kpe[1], st.ap[:, 1, :], R=[st.b], W=[self.Db["kpe"]])
        self.release(m0)
        for i in range(2):
            self.LD(cs.ap[:, i, :], self.d_cs[2 + i], R=[self.Db["cs"]], W=[cs.b])
        stgw = A.alloc([128, 3 * 768], F32)
        wq = A.alloc([128, 3, 768], BF16)
        self.load_weight_bf16(wq, I["w_uq"][l], stgw, 3, 768)
        stgk = A.alloc([128, 2 * 1024], F32)
        wkv = A.alloc([128, 2, 1024], BF16)
        self.load_weight_bf16(wkv, I["w_ukv"][l], stgk, 2, 1024)
        stg = [A.alloc([128, S], BF16) for _ in range(2)]
        self._s = 0

        def nstg():
            s = stg[self._s % 2]
            self._s += 1
            return s
        self._pbk = 0

        def nb():
            b = self._pbk % 4
            self._pbk += 1
            return b

        for h in range(4):
            st = nstg()
            for tc in range(NQC):
                bk = nb()
                for k in range(3):
                    self.MM(self.psf(bk), wq.ap[:, k, h * 192:h * 192 + 128], cqT.ap[:, k, tc * 512:(tc + 1) * 512], k == 0, k == 2,
                            [wq.b, cqT.b], [self.pb[bk]])
                self.ACT(st.ap[:, tc * 512:(tc + 1) * 512], self.psf(bk), AF.Copy, [self.pb[bk]], [st.b], scale=qs)
            self.ST(self.d_qn[h], st.ap, R=[st.b], W=[self.Db["qn"]])
            st = rst[h % 2]
            for tc in range(NQC):
                b1, b2 = nb(), nb()
                sl = slice(tc * 512, (tc + 1) * 512)
                for hf, bk in ((0, b1), (1, b2)):
                    c0 = h * 192 + 128 + hf * 32
                    for k in range(3):
                        self.MM(self.psf(bk)[0:32, :], wq.ap[:, k, c0:c0 + 32], cqT.ap[:, k, sl], k == 0, k == 2,
                                [wq.b, cqT.b], [self.pb[bk]])
                rope(self.psf(b1)[0:32, :], self.psf(b2)[0:32, :], [self.pb[b1]], [self.pb[b2]], cs.ap[:, 0, sl], cs.ap[:, 1, sl],
                     st.ap[:, 0, sl], st.ap[:, 1, sl], st.b, 512)
            self.ST(self.d_qpe[h, 0], st.ap[:, 0, :], R=[st.b], W=[self.Db["qpe"]])
            self.ST(self.d_qpe[h, 1], st.ap[:, 1, :], R=[st.b], W=[self.Db["qpe"]])
            st = nstg()
            for tc in range(NQC):
                bk = nb()
                for k in range(2):
                    self.MM(self.psf(bk), wkv.ap[:, k, h * 256:h * 256 + 128], ckvT.ap[:, k, tc * 512:(tc + 1) * 512], k == 0, k == 1,
                            [wkv.b, ckvT.b], [self.pb[bk]])
                self.V("tensor_copy", [self.pb[bk]], [st.b], out=st.ap[:, tc * 512:(tc + 1) * 512], in_=self.psf(bk))
            self.ST(self.d_kn[h], st.ap, R=[st.b], W=[self.Db["kn"]])
        wv = A.alloc([128, 2, 512], BF16)
        for h in range(4):
            self.G("tensor_copy", [wkv.b], [wv.b], out=wv.ap[:, :, h * 128:(h + 1) * 128], in_=wkv.ap[:, :, h * 256 + 128:h * 256 + 256])
        vst = [A.alloc([128, 4, 129], BF16) for _ in range(2)]
        for v_ in vst:
            self.V("memset", [], [v_.b], ap=v_.ap, constant=1.0)
        for t in range(NT):
            bk = 4 + t % 2
            vs = vst[t % 2]
            for k in range(2):
                self.MM(self.psf(bk), ckvT.ap[:, k, t * 128:(t + 1) * 128], wv.ap[:, k, :], k == 0, k == 1, [ckvT.b, wv.b], [self.pb[bk]])
            self.V("tensor_copy", [self.pb[bk]], [vs.b], out=vs.ap[:, :, 0:128], in_=self.psf(bk).rearrange("p (h d) -> p h d", h=4))
            self.ST(self.d_vmla[t * 128:(t + 1) * 128], vs.ap, R=[vs.b], W=[self.Db["vmla"]])
        self.phase_end()
        cstrip = A.alloc([128, 896], BF16)
        cmf = A.alloc([128, 128], F32)
        self.LD(cmf.ap, C["cmask"], W=[cmf.b])
        self.G("memset", [], [cstrip.b], ap=cstrip.ap[:, 0:384], constant=NEG)
        self.G("memset", [], [cstrip.b], ap=cstrip.ap[:, 512:896], constant=0.0)
        self.G("tensor_copy", [cmf.b], [cstrip.b], out=cstrip.ap[:, 384:512], in_=cmf.ap)
        kpe = A.alloc([64, S], BF16)
        for hf in range(2):
            self.LD(kpe.ap[hf * 32:(hf + 1) * 32, :], self.d_kpe[hf], R=[self.Db["kpe"]], W=[kpe.b])
        qn = A.alloc([128, S], BF16)
        kn = A.alloc([128, S], BF16)
        qpe = A.alloc([64, S], BF16)
        vv = A.alloc([128, NT, 129], BF16)
        oT = A.alloc([128, S], BF16)
        pTs = [A.alloc([128, 512], BF16) for _ in range(3)]
        ob = [A.alloc([128, 128], BF16) for _ in range(2)]
        sm = [A.alloc([128, 4], F32) for _ in range(4)]
        vview = self.d_vmla.rearrange("(t p) h e -> p t h e", p=128)
        smi = 0
        for h in range(4):
            self.LD(qn.ap, self.d_qn[h], R=[self.Db["qn"]], W=[qn.b])
            self.LD(kn.ap, self.d_kn[h], R=[self.Db["kn"]], W=[kn.b])
            for hf in range(2):
                self.LD(qpe.ap[hf * 32:(hf + 1) * 32, :], self.d_qpe[h, hf], R=[self.Db["qpe"]], W=[qpe.b])
            self.LD(vv.ap, vview[:, :, h, :], R=[self.Db["vmla"]], W=[vv.b])
            for qc in range(NQC):
                ba = 4 + 2 * (qc % 2)
                Oj = [self.psf(ba + j // 2)[:, (j % 2) * 129:(j % 2) * 129 + 129] for j in range(4)]
                Ob = [self.pb[ba + j // 2] for j in range(4)]
                qsl = slice(qc * 512, (qc + 1) * 512)
                def emit_sc(kt, qc=qc, qsl=qsl):
                    bsc = kt % 3
                    sc = self.psf(bsc)
                    ksl = slice(kt * 128, (kt + 1) * 128)
                    diag = kt >= 4 * qc
                    self.MM(sc, kn.ap[:, ksl], qn.ap[:, qsl], True, False, [kn.b, qn.b], [self.pb[bsc]])
                    self.MM(sc, kpe.ap[:, ksl], qpe.ap[:, qsl], False, not diag, [kpe.b, qpe.b], [self.pb[bsc]])
                    if diag:
                        o = kt - 4 * qc
                        self.MM(sc, self.identb.ap, cstrip.ap[:, (3 - o) * 128:(3 - o) * 128 + 512], False, True,
                                [self.identb.b, cstrip.b], [self.pb[bsc]])
                    pT = pTs[kt % 3]
                    self.ACT(pT.ap, sc, AF.Exp, [self.pb[bsc]], [pT.b])
                    return pT

                def emit_pv(kt, pT, qc=qc, Oj=Oj, Ob=Ob):
                    for j in range(4):
                        qt = 4 * qc + j
                        if kt > qt:
                            continue
                        self.MM(Oj[j], pT.ap[:, j * 128:(j + 1) * 128], vv.ap[:, kt, :], kt == 0 and j % 2 == 0, kt == qt, [pT.b, vv.b], [Ob[j]])
                pend = None
                for kt in range(4 * qc + 4):
                    pT = emit_sc(kt)
                    if pend is not None:
                        emit_pv(*pend)
                    pend = (kt, pT)
                emit_pv(*pend)
                for j in range(4):
                    qt = 4 * qc + j
                    t = sm[smi % 4]
                    smi += 1
                    o_ = ob[qt % 2]
                    self.V("reciprocal", [Ob[j]], [t.b], out=t.ap[:, 0:1], in_=Oj[j][:, 128:129])
                    self.V("tensor_scalar", [Ob[j], t.b], [o_.b], out=o_.ap, in0=Oj[j][:, 0:128], scalar1=t.ap[:, 0:1], scalar2=None,
                           op0=ALU.mult)
                    bt = 3
                    self.TR(self.psb(bt)[:, 0:128], o_.ap, [o_.b], [self.pb[bt]])
                    self.ACT(oT.ap[:, qt * 128:(qt + 1) * 128], self.psb(bt)[:, 0:128], AF.Copy, [self.pb[bt]], [oT.b])
            self.ST(self.d_omlaT[h * 128:(h + 1) * 128, :], oT.ap, R=[oT.b], W=[self.Db["omlaT"]])
        self.phase_end()

    def ph_merge(self, l, xsrc, xsrc_b):
        A, I = self.A, self.I
        stg = A.alloc([128, 4096], F32)
        wb = {}
        for nm in ("w_branch_conv", "w_branch_nsa", "w_branch_mla"):
            wb[nm] = A.alloc([128, 4, 1024], BF16)
            self.load_weight_bf16(wb[nm], I[nm][l], stg, 4, 1024)
        wo = A.alloc([128, 8, 1024], BF16)
        for hf in range(2):
            sv = stg.ap.rearrange("p (k n) -> p k n", k=4)
            self.LD(sv, I["w_out"][l][hf * 512:(hf + 1) * 512, :].rearrange("(k p) n -> p k n", p=128), R=[wo.b], W=[stg.b])
            self.G("tensor_copy", [stg.b], [wo.b], out=wo.ap[:, hf * 4:(hf + 1) * 4, :], in_=sv)
        srcs = [("w_branch_conv", self.d_uactT, "uactT"), ("w_branch_nsa", self.d_onsaT, "onsaT"), ("w_branch_mla", self.d_omlaT, "omlaT")]
        acts = [[A.alloc([128, 4, 512], BF16) for _ in range(2)] for _ in range(3)]
        gms = [A.alloc([128, 24, 512], BF16) for _ in range(2)]
        mT = [A.alloc([128, 8, 512], BF16) for _ in range(2)]
        ta = [A.alloc([128, 512], F32) for _ in range(2)]
        tb = [A.alloc([128, 512], F32) for _ in range(2)]
        xts = [A.alloc([128, 1024], F32) for _ in range(2)]
        xos = [A.alloc([128, 1024], F32) for _ in range(2)]
        gv = self.d_gmT.rearrange("(b p) s -> p b s", p=128)
        n = 0
        for tc in range(NQC):
            sl = slice(tc * 512, (tc + 1) * 512)
            i2 = tc % 2
            for si, (wn, dsrc, dn) in enumerate(srcs):
                self.LD(acts[si][i2].ap, dsrc.rearrange("(k p) s -> p k s", p=128)[:, :, sl], R=[self.Db[dn]], W=[acts[si][i2].b])
            gm = gms[i2]
            self.LD(gm.ap, gv[:, :, sl], R=[self.Db["gmT"]], W=[gm.b])
            m_ = mT[i2]
            for fc in range(8):
                a_, b_ = ta[fc % 2], tb[fc % 2]
                for si, (wn, dsrc, dn) in enumerate(srcs):
                    bk = n % 4
                    n += 1
                    for k in range(4):
                        self.MM(self.psf(bk), wb[wn].ap[:, k, fc * 128:(fc + 1) * 128], acts[si][i2].ap[:, k, :], k == 0, k == 3,
                                [wb[wn].b, acts[si][i2].b], [self.pb[bk]])
                    dst = a_ if si == 0 else b_
                    self.V("tensor_tensor", [self.pb[bk], gm.b], [dst.b], out=dst.ap, in0=self.psf(bk), in1=gm.ap[:, si * 8 + fc, :], op=ALU.mult)
                    if si == 1:
                        self.V("tensor_tensor", [a_.b, b_.b], [a_.b], out=a_.ap, in0=a_.ap, in1=b_.ap, op=ALU.add)
                    if si == 2:
                        self.V("tensor_tensor", [a_.b, b_.b], [m_.b], out=m_.ap[:, fc, :], in0=a_.ap, in1=b_.ap, op=ALU.add)
            for tt_ in range(4):
                t = tc * 4 + tt_
                xt, xo = xts[t % 2], xos[t % 2]
                self.LD(xt.ap, xsrc[t * 128:(t + 1) * 128, :], R=[xsrc_b], W=[xt.b])
                for cg in range(2):
                    bk = 4 + (t * 2 + cg) % 4
                    for k in range(8):
                        self.MM(self.psf(bk), m_.ap[:, k, tt_ * 128:(tt_ + 1) * 128], wo.ap[:, k, cg * 512:(cg + 1) * 512], k == 0, k == 7,
                                [m_.b, wo.b], [self.pb[bk]])
                    self.V("tensor_tensor", [self.pb[bk], xt.b], [xo.b], out=xo.ap[:, cg * 512:(cg + 1) * 512], in0=self.psf(bk),
                           in1=xt.ap[:, cg * 512:(cg + 1) * 512], op=ALU.add)
                self.ST(self.d_xres[t * 128:(t + 1) * 128, :], xo.ap, R=[xo.b], W=[self.Db["xres"]])
        self.phase_end()

    def ph_xattn(self, l):
        A, I = self.A, self.I
        xs = 128.0 ** -0.5
        hT = A.alloc([128, 8, S], BF16)
        self.cur_x_b = self.Db["xres"]
        self.norm_to_hT(self.d_xres, I["norm_xattn"][l], hT)
        memT = A.alloc([128, 8, 256], BF16)
        self.cur_x_b = Buf()
        self.norm_to_hT(I["mem"], I["norm_mem"][l], memT, ntile=2, pbank=2)
        stg = A.alloc([128, 8 * 512], F32)
        wq = A.alloc([128, 8, 512], BF16)
        self.load_weight_bf16(wq, I["w_xq"][l], stg, 8, 512)
        wkv = A.alloc([128, 8, 1024], BF16)
        for hf in range(2):
            sv = stg.ap.rearrange("p (k n) -> p k n", k=8)
            self.LD(sv, I["w_xkv"][l][:, hf * 512:(hf + 1) * 512].rearrange("(k p) n -> p k n", p=128), R=[wkv.b], W=[stg.b])
            self.G("tensor_copy", [stg.b], [wkv.b], out=wkv.ap[:, :, hf * 512:(hf + 1) * 512], in_=sv)
        wo = A.alloc([128, 4, 1024], BF16)
        self.load_weight_bf16(wo, I["w_xo"][l], stg, 4, 1024)
        kT = A.alloc([128, 4, 256], BF16)
        vv = A.alloc([128, 2, 4, 129], BF16)
        self.V("memset", [], [vv.b], ap=vv.ap, constant=1.0)
        for h in range(4):
            for k in range(8):
                self.MM(self.psf(0)[:, 0:256], wkv.ap[:, k, h * 128:(h + 1) * 128], memT.ap[:, k, :], k == 0, k == 7, [wkv.b, memT.b], [self.pb[0]])
            self.V("tensor_copy", [self.pb[0]], [kT.b], out=kT.ap[:, h, :], in_=self.psf(0)[:, 0:256])
        for mt in range(2):
            for k in range(8):
                self.MM(self.psf(1), memT.ap[:, k, mt * 128:(mt + 1) * 128], wkv.ap[:, k, 512:1024], k == 0, k == 7, [wkv.b, memT.b], [self.pb[1]])
            self.V("tensor_copy", [self.pb[1]], [vv.b], out=vv.ap[:, mt, :, 0:128], in_=self.psf(1).rearrange("p (h d) -> p h d", h=4))
        qTs = [A.alloc([128, 512], BF16) for _ in range(2)]
        pTs = [A.alloc([128, 2, 512], BF16) for _ in range(2)]
        self._ox = [A.alloc([128, 512], BF16) for _ in range(4)]
        oxT = [A.alloc([128, 4, 128], BF16) for _ in range(2)]
        sm = [A.alloc([128, 4], F32) for _ in range(4)]
        xts = [A.alloc([128, 1024], F32) for _ in range(2)]
        xos = [A.alloc([128, 1024], F32) for _ in range(2)]
        smi = 0
        n = 0
        for tc in range(NQC):
            sl = slice(tc * 512, (tc + 1) * 512)
            for h in range(4):
                qT = qTs[h % 2]
                for k in range(8):
                    self.MM(self.psf(0), wq.ap[:, k, h * 128:(h + 1) * 128], hT.ap[:, k, sl], k == 0, k == 7, [wq.b, hT.b], [self.pb[0]])
                self.ACT(qT.ap, self.psf(0), AF.Copy, [self.pb[0]], [qT.b], scale=xs)
                pT = pTs[h % 2]
                for mt in range(2):
                    bsc = 1 + mt
                    self.MM(self.psf(bsc), kT.ap[:, h, mt * 128:(mt + 1) * 128], qT.ap, True, True, [kT.b, qT.b], [self.pb[bsc]])
                    self.ACT(pT.ap[:, mt, :], self.psf(bsc), AF.Exp, [self.pb[bsc]], [pT.b])
                for j in range(4):
                    bk = 4 + (h % 2) * 2 + j // 2
                    Oj = self.psf(bk)[:, (j % 2) * 129:(j % 2) * 129 + 129]
                    for mt in range(2):
                        self.MM(Oj, pT.ap[:, mt, j * 128:(j + 1) * 128], vv.ap[:, mt, h, :], mt == 0 and j % 2 == 0, mt == 1, [pT.b, vv.b], [self.pb[bk]])
                    t = sm[smi % 4]
                    smi += 1
                    self.V("reciprocal", [self.pb[bk]], [t.b], out=t.ap[:, 0:1], in_=Oj[:, 128:129])
                    ox = self._ox[j]
                    self.V("tensor_scalar", [self.pb[bk], t.b], [ox.b], out=ox.ap[:, h * 128:(h + 1) * 128], in0=Oj[:, 0:128],
                           scalar1=t.ap[:, 0:1], scalar2=None, op0=ALU.mult)
            for j in range(4):
                t = tc * 4 + j
                ox = self._ox[j]
                oT_ = oxT[t % 2]
                bt = 3
                pt = self.psb(bt)[:, 0:512].rearrange("p (k n) -> p k n", k=4)
                for k in range(4):
                    self.TR(pt[:, k, :], ox.ap[:, k * 128:(k + 1) * 128], [ox.b], [self.pb[bt]])
                self.ACT(oT_.ap, pt, AF.Copy, [self.pb[bt]], [oT_.b])
                xt, xo = xts[t % 2], xos[t % 2]
                self.LD(xt.ap, self.d_xres[t * 128:(t + 1) * 128, :], R=[self.Db["xres"]], W=[xt.b])
                for cg in range(2):
                    bk = 6 + cg
                    for k in range(4):
                        self.MM(self.psf(bk), oT_.ap[:, k, :], wo.ap[:, k, cg * 512:(cg + 1) * 512], k == 0, k == 3, [oT_.b, wo.b], [self.pb[bk]])
                    self.V("tensor_tensor", [self.pb[bk], xt.b], [xo.b], out=xo.ap[:, cg * 512:(cg + 1) * 512], in0=self.psf(bk),
                           in1=xt.ap[:, cg * 512:(cg + 1) * 512], op=ALU.add)
                self.ST(self.d_xres[t * 128:(t + 1) * 128, :], xo.ap, R=[xo.b], W=[self.Db["xres"]])
        self.phase_end()

    def ph_ffn(self, l):
        A, I = self.A, self.I
        hT = A.alloc([128, 8, S], BF16)
        wg = A.alloc([128, 8, 1408], BF16)
        wu = A.alloc([128, 8, 1408], BF16)
        wd = A.alloc([128, 11, 1024], BF16)
        stg = [A.alloc([128, 3072], F32) for _ in range(2)]
        w_gu = I["w_gate_up"][l]
        w_dn = I["w_down"][l]

        def load_w(ps_):
            f0 = ps_ * 1408
            u = 0
            for dst, base in ((wg, 0), (wu, FFN)):
                for q4 in range(4):
                    s_ = stg[u % 2]
                    u += 1
                    sv = s_.ap[:, 0:2816].rearrange("p (k n) -> p k n", k=8)
                    c0 = base + f0 + q4 * 352
                    self.P.dma("pool", sv, w_gu[:, c0:c0 + 352].rearrange("(k p) n -> p k n", p=128), R=[dst.b], W=[s_.b])
                    self.G("tensor_copy", [s_.b], [dst.b], out=dst.ap[:, :, q4 * 352:(q4 + 1) * 352], in_=sv)
            for q4 in range(4):
                s_ = stg[u % 2]
                u += 1
                nk = 3 if q4 < 3 else 2
                sv = s_.ap[:, 0:nk * 1024].rearrange("p (k n) -> p k n", k=nk)
                r0 = f0 + q4 * 384
                self.P.dma("pool", sv, w_dn[r0:r0 + nk * 128, :].rearrange("(k p) n -> p k n", p=128), R=[wd.b], W=[s_.b])
                self.G("tensor_copy", [s_.b], [wd.b], out=wd.ap[:, q4 * 3:q4 * 3 + nk, :], in_=sv)
        load_w(0)
        self.cur_x_b = self.Db["xres"]
        self.norm_to_hT(self.d_xres, I["norm_ffn"][l], hT)
        actT = [A.alloc([128, 11, 512], BF16) for _ in range(1)]
        sg = [A.alloc([128, 512], F32) for _ in range(2)]
        xts = [A.alloc([128, 1024], F32) for _ in range(2)]
        xos = [A.alloc([128, 1024], F32) for _ in range(2)]
        for ps_ in range(2):
            if ps_ == 1:
                load_w(1)
            n = 0
            for tc in range(NQC):
                sl = slice(tc * 512, (tc + 1) * 512)
                aT = actT[0]
                for f in range(11):
                    bg, bu = (n % 2) * 2, (n % 2) * 2 + 1
                    n += 1
                    for k in range(8):
                        self.MM(self.psf(bg), wg.ap[:, k, f * 128:(f + 1) * 128], hT.ap[:, k, sl], k == 0, k == 7, [wg.b, hT.b], [self.pb[bg]])
                    for k in range(8):
                        self.MM(self.psf(bu), wu.ap[:, k, f * 128:(f + 1) * 128], hT.ap[:, k, sl], k == 0, k == 7, [wu.b, hT.b], [self.pb[bu]])
                    s = sg[f % 2]
                    self.ACT(s.ap, self.psf(bg), AF.Silu, [self.pb[bg]], [s.b])
                    self.V("tensor_tensor", [self.pb[bu], s.b], [aT.b], out=aT.ap[:, f, :], in0=self.psf(bu), in1=s.ap, op=ALU.mult)
                for tt_ in range(4):
                    t = tc * 4 + tt_
                    xt, xo = xts[t % 2], xos[t % 2]
                    self.LD(xt.ap, self.d_xres[t * 128:(t + 1) * 128, :], R=[self.Db["xres"]], W=[xt.b])
                    for cg in range(2):
                        bk = 4 + (t * 2 + cg) % 4
                        for f in range(11):
                            self.MM(self.psf(bk), aT.ap[:, f, tt_ * 128:(tt_ + 1) * 128], wd.ap[:, f, cg * 512:(cg + 1) * 512], f == 0, f == 10,
                                    [aT.b, wd.b], [self.pb[bk]])
                        self.V("tensor_tensor", [self.pb[bk], xt.b], [xo.b], out=xo.ap[:, cg * 512:(cg + 1) * 512], in0=self.psf(bk),
                               in1=xt.ap[:, cg * 512:(cg + 1) * 512], op=ALU.add)
                    self.ST(self.d_xres[t * 128:(t + 1) * 128, :], xo.ap, R=[xo.b], W=[self.Db["xres"]])
            self.P.barrier()
        self.phase_end()

    def ph_final(self):
        A, I = self.A, self.I
        gb = A.alloc([128, D], F32)
        self.LD(gb.ap, I["norm_final"].partition_broadcast(128), W=[gb.b])
        xts = [A.alloc([128, D], F32) for _ in range(2)]
        junk = A.alloc([128, D], F32)
        ys = [A.alloc([128, D], F32) for _ in range(2)]
        sss = [A.alloc([128, 1], F32) for _ in range(2)]
        rss = [A.alloc([128, 1], F32) for _ in range(2)]
        yb = Buf()
        for t in range(NT):
            xt, y_, ss, rstd = xts[t % 2], ys[t % 2], sss[t % 2], rss[t % 2]
            self.LD(xt.ap, self.d_xres[t * 128:(t + 1) * 128, :], R=[self.Db["xres"]], W=[xt.b])
            self.rms_rstd(xt, junk, ss, rstd, D)
            self.V("scalar_tensor_tensor", [xt.b, rstd.b, gb.b], [y_.b], out=y_.ap, in0=xt.ap, scalar=rstd.ap[:, 0:1], in1=gb.ap,
                   op0=ALU.mult, op1=ALU.mult)
            self.ST(self.out[t * 128:(t + 1) * 128, :], y_.ap, R=[y_.b], W=[yb])
        self.phase_end()
    def build(self, n_layers=2, stop_after=None):
        d = self.dram
        self.d_xres = d("xres", [S, D], F32)
        self.d_uT = d("uT", [512, S], F32)
        self.d_qnT = d("qnT", [8, 64, S], BF16)
        self.d_kvT = d("kvT", [4, 2, 64, S], BF16)
        self.d_krT = d("krT", [2, 32, S], F32)
        self.d_gmT = d("gmT", [3072, S], BF16)
        self.d_vtm = d("vtm", [S, 4, 65], BF16)
        self.d_misc = d("misc", [S, 664], F32)
        self.d_uactT = d("uactT", [512, S], BF16)
        self.d_onsaT = d("onsaT", [512, S], BF16)
        self.d_omlaT = d("omlaT", [512, S], BF16)
        self.d_vs = d("vs", [8, 128, 384], BF16)
        self.d_bc = d("bc", [8, 2048], BF16)
        self.d_cs = d("cs", [4, 32, S], F32)
        self.d_qn = d("qn", [4, 128, S], BF16)
        self.d_kn = d("kn", [4, 128, S], BF16)
        self.d_qpe = d("qpe", [4, 2, 32, S], BF16)
        self.d_kpe = d("kpe", [2, 32, S], BF16)
        self.d_vmla = d("vmla", [S, 4, 129], BF16)
        if "onsa_dbg" in self.debug:
            self.d_onsa_dbg = d("onsa_dbg", [S, 512], F32)
        self.cur_x_b = Buf()
        self.setup()
        phases = []
        phases.append(("tables", lambda: (self.ph_bias_tables(), self.ph_rope_tables())))
        for l in range(n_layers):
            xsrc = self.I["x"] if l == 0 else self.d_xres
            xb = Buf() if l == 0 else self.Db["xres"]
            phases.append((f"inproj{l}", lambda l=l, xsrc=xsrc, xb=xb: (setattr(self, "cur_x_b", xb), self.ph_inproj(l, xsrc))))
            phases.append((f"conv{l}", lambda l=l: self.ph_conv(l)))
            phases.append((f"nsa{l}", lambda l=l: self.ph_nsa(l)))
            phases.append((f"mla{l}", lambda l=l: self.ph_mla(l)))
            phases.append((f"merge{l}", lambda l=l, xsrc=xsrc, xb=xb: self.ph_merge(l, xsrc, xb)))
            phases.append((f"xattn{l}", lambda l=l: self.ph_xattn(l)))
            phases.append((f"ffn{l}", lambda l=l: self.ph_ffn(l)))
        phases.append(("final", lambda: self.ph_final()))
        skip = set(self.skip)
        for name, fn in phases:
            if name not in skip:
                fn()
            if stop_after == name:
                break
        self.P.finish()


def build_nc(debug=None, n_layers=2, stop_after=None, skip=()):
    nc = bass.Bass("TRN2", target_bir_lowering=False)
    with contextlib.ExitStack() as es:
        k = K(nc, es, debug)
        k.skip = list(skip)
        k.build(n_layers, stop_after)
    return nc, k


def make_in_maps(inputs, consts):
    maps = []
    for b in range(8):
        m = {}
        for n in WEIGHT_SHAPES:
            a = np.asarray(inputs[n])
            if n in ("x", "mem", "positions"):
                a = a[b]
            m[n] = np.ascontiguousarray(a)
        for n, v in consts.items():
            m["c_" + n] = v
        maps.append(m)
    return maps


def kernel(**inputs):
    nc, k = build_nc()
    consts = host_consts()
    in_maps = make_in_maps(inputs, consts)
    res = run_bass_kernel_spmd(nc, in_maps, core_ids=list(range(8)))
    return np.stack([np.asarray(r["y"]) for r in res.results], axis=0).astype(np.float32)
```
